# Optimizing a Trainium2 kernel written in Bass

```python
import math
import jax, jax.numpy as jnp
from jax import lax
import numpy as np

D_MODEL = 2048
BATCH = 4
SEQ = 2048
DEPTH = 2
DEC_BATCH = 128
DEC_SEQ = 1
PAST_LEN = 16384
PAGE_SIZE = 128

D_A = 512
S5_GROUP = 16
G_A = D_A // S5_GROUP
P_A = 64
D_B = 512
H_B = 8
GM_CHUNK = 128
D_C = 1024
HD_C = 64
H_C = D_C // HD_C
N_C = 128
G_C = 2
K_C = 4
SSD_CHUNK = 128
D_XBC = D_C + 2 * G_C * N_C
D_FF = 5632
K_F = 3
N_IN = D_A + 2 * D_B + D_C + D_XBC + H_C + 3 * D_MODEL
EPS = 1e-6

kernel_name = "hybrid_s5_gmlp_ssd_convffn_step"


def rmsnorm(x, g):
    xf = x.astype(jnp.float32)
    y = xf * lax.rsqrt(jnp.mean(xf * xf, axis=-1, keepdims=True) + EPS)
    return (y * g.astype(jnp.float32)).astype(x.dtype)


def causal_dwconv(x, buf, w, b):
    K = w.shape[0]
    L = x.shape[1]
    xp = jnp.concatenate([buf.astype(x.dtype), x], axis=1)
    y = b
    for k in range(K):
        y = y + w[k] * xp[:, k:k + L]
    return y, xp[:, L:]


def s5_combine(e1, e2):
    a1r, a1i, b1r, b1i = e1
    a2r, a2i, b2r, b2i = e2
    return (a2r * a1r - a2i * a1i, a2r * a1i + a2i * a1r,
            a2r * b1r - a2i * b1i + b2r, a2r * b1i + a2i * b1r + b2i)


def s5_branch(u, h0_re, h0_im, lam_re, lam_im, log_dt, b_re, b_im, c_re, c_im, d_skip, w_glu):
    Bsz, L, _ = u.shape
    uf = u.astype(jnp.float32)
    ug = uf.reshape(Bsz, L, G_A, S5_GROUP)
    dt = jnp.exp(log_dt.astype(jnp.float32))[:, None]
    lr = lam_re.astype(jnp.float32)
    li = lam_im.astype(jnp.float32)
    mag = jnp.exp(lr * dt)
    ar = mag * jnp.cos(li * dt)
    ai = mag * jnp.sin(li * dt)
    den = lr * lr + li * li
    nr = ar - 1.0
    kr = (nr * lr + ai * li) / den
    ki = (ai * lr - nr * li) / den
    bre = b_re.astype(jnp.float32)
    bim = b_im.astype(jnp.float32)
    bbr = kr[..., None] * bre - ki[..., None] * bim
    bbi = kr[..., None] * bim + ki[..., None] * bre
    bu_r = jnp.einsum('gph,blgh->blgp', bbr, ug)
    bu_i = jnp.einsum('gph,blgh->blgp', bbi, ug)
    h0r = h0_re.astype(jnp.float32)
    h0i = h0_im.astype(jnp.float32)
    bu_r = bu_r.at[:, 0].add(ar * h0r - ai * h0i)
    bu_i = bu_i.at[:, 0].add(ar * h0i + ai * h0r)
    a_r = jnp.broadcast_to(ar, bu_r.shape)
    a_i = jnp.broadcast_to(ai, bu_i.shape)
    _, _, hr, hi = lax.associative_scan(s5_combine, (a_r, a_i, bu_r, bu_i), axis=1)
    y = (jnp.einsum('ghp,blgp->blgh', c_re.astype(jnp.float32), hr)
         - jnp.einsum('ghp,blgp->blgh', c_im.astype(jnp.float32), hi))
    y = y.reshape(Bsz, L, D_A) + d_skip.astype(jnp.float32) * uf
    y = jax.nn.gelu(y)
    y = y * jax.nn.sigmoid(y @ w_glu.astype(jnp.float32))
    return y.astype(u.dtype), hr[:, -1], hi[:, -1]


def gmlp_branch(u, v, g_v, w_s, b_s):
    Bsz, L, _ = u.shape
    vn = rmsnorm(v, g_v)
    nc = -(-L // GM_CHUNK)
    pad = nc * GM_CHUNK - L
    vp = jnp.pad(vn, ((0, 0), (0, pad), (0, 0))).reshape(Bsz, nc, GM_CHUNK, H_B, D_B // H_B)
    mask = jnp.tril(jnp.ones((GM_CHUNK, GM_CHUNK), dtype=bool))
    ws = jnp.where(mask, w_s, 0.0)
    s = jnp.einsum('hij,bcjhd->bcihd', ws, vp) + b_s.T[:, :, None]
    s = s.reshape(Bsz, nc * GM_CHUNK, D_B)[:, :L]
    return u * s, vn


def ssd_scan(x, dt, a, bm, cm, h0):
    Bsz, L = x.shape[:2]
    Q = SSD_CHUNK if L % SSD_CHUNK == 0 else L
    nc = L // Q

    def chunk(t):
        return t.reshape((Bsz, nc, Q) + t.shape[2:])

    xc, dtc, bc, cc = chunk(x), chunk(dt), chunk(bm), chunk(cm)
    da_cs = jnp.cumsum(dtc * a, axis=2)
    xdt = xc * dtc[..., None]
    seg = da_cs[:, :, :, None, :] - da_cs[:, :, None, :, :]
    causal = jnp.tril(jnp.ones((Q, Q), dtype=bool))[:, :, None]
    decay = jnp.exp(jnp.where(causal, seg, -jnp.inf))
    scores = jnp.einsum('bcihn,bcjhn->bcijh', cc, bc) * decay
    y_diag = jnp.einsum('bcijh,bcjhp->bcihp', scores, xdt)
    dec_end = jnp.exp(da_cs[:, :, -1:, :] - da_cs)
    states = jnp.einsum('bcjhn,bcjh,bcjhp->bchpn', bc, dec_end, xdt)
    chunk_decay = jnp.exp(da_cs[:, :, -1, :])

    def step(h, inp):
        s_c, d_c = inp
        return h * d_c[..., None, None] + s_c, h

    h_last, h_prev = lax.scan(step, h0, (jnp.moveaxis(states, 1, 0), jnp.moveaxis(chunk_decay, 1, 0)))
    h_prev = jnp.moveaxis(h_prev, 0, 1)
    y_off = jnp.einsum('bcihn,bchpn,bcih->bcihp', cc, h_prev, jnp.exp(da_cs))
    y = (y_diag + y_off).reshape(Bsz, L, x.shape[2], x.shape[3])
    return y, h_last


def ssd_branch(z, xbc, dt_raw, conv_buf, h0, conv_w, conv_b, dt_bias, a_log, d_skip, g_norm):
    xbc, new_buf = causal_dwconv(xbc, conv_buf, conv_w, conv_b)
    xbc = jax.nn.silu(xbc)
    Bsz, L, _ = xbc.shape
    xs = xbc[..., :D_C].reshape(Bsz, L, H_C, HD_C).astype(jnp.float32)
    bm = xbc[..., D_C:D_C + G_C * N_C].reshape(Bsz, L, G_C, N_C)
    cm = xbc[..., D_C + G_C * N_C:].reshape(Bsz, L, G_C, N_C)
    bm = jnp.repeat(bm, H_C // G_C, axis=2).astype(jnp.float32)
    cm = jnp.repeat(cm, H_C // G_C, axis=2).astype(jnp.float32)
    dt = jax.nn.softplus(dt_raw.astype(jnp.float32) + dt_bias.astype(jnp.float32))
    a = -jnp.exp(a_log.astype(jnp.float32))
    y, h_last = ssd_scan(xs, dt, a, bm, cm, h0.astype(jnp.float32))
    y = y + d_skip.astype(jnp.float32)[:, None] * xs
    y = y.reshape(Bsz, L, D_C)
    y = rmsnorm(y * jax.nn.silu(z.astype(jnp.float32)), g_norm)
    return y.astype(z.dtype), h_last, new_buf


def conv_ffn(h, buf, w_up, conv_w, conv_b, w_down):
    up = h @ w_up
    up, new_buf = causal_dwconv(up, buf, conv_w, conv_b)
    a, b = jnp.split(up, 2, axis=-1)
    return (jax.nn.silu(a) * b) @ w_down, new_buf


def layer(x, c, s5_re, s5_im, ssm, conv_c, conv_f, p):
    mod = jax.nn.silu(c) @ p['w_mod'] + p['b_mod']
    sh_m, sc_m, gt_m, sh_f, sc_f, gt_f = [m[:, None, :] for m in jnp.split(mod, 6, axis=-1)]
    h = rmsnorm(x, p['g_mix']) * (1.0 + sc_m) + sh_m
    proj = h @ p['w_in']
    i1 = D_A
    i2 = i1 + D_B
    i3 = i2 + D_B
    i4 = i3 + D_C
    i5 = i4 + D_XBC
    i6 = i5 + H_C
    u_a, u_b, v_b, z_c, xbc_c, dt_c, gates = jnp.split(proj, [i1, i2, i3, i4, i5, i6], axis=-1)
    o_a, s5r_new, s5i_new = s5_branch(u_a, s5_re, s5_im, p['lam_re'], p['lam_im'], p['log_dt'],
                                      p['b_re'], p['b_im'], p['c_re'], p['c_im'], p['s5_d'], p['w_glu'])
    o_b, v_rows = gmlp_branch(jax.nn.gelu(u_b), jax.nn.gelu(v_b), p['g_v'], p['w_s'], p['b_s'])
    o_c, ssm_new, convc_new = ssd_branch(z_c, xbc_c, dt_c, conv_c, ssm, p['ssd_conv_w'], p['ssd_conv_b'],
                                         p['dt_bias'], p['a_log'], p['ssd_d'], p['ssd_g'])
    g_a, g_b, g_c = jnp.split(jax.nn.sigmoid(gates), 3, axis=-1)
    merged = g_a * (o_a @ p['w_pa']) + g_b * (o_b @ p['w_pb']) + g_c * (o_c @ p['w_pc'])
    x = x + gt_m * (merged @ p['w_out'])
    h2 = rmsnorm(x, p['g_ffn']) * (1.0 + sc_f) + sh_f
    f, convf_new = conv_ffn(h2, conv_f, p['w_up'], p['ffn_conv_w'], p['ffn_conv_b'], p['w_down'])
    x = x + gt_f * f
    return x, s5r_new, s5i_new, ssm_new, convc_new, convf_new, v_rows


def setup_inputs(seed: int = 0) -> dict:
    key = jax.random.key(seed)
    ks = iter(jax.random.split(key, 64))

    def nrm(shape, scale):
        return scale * jax.random.normal(next(ks), shape, jnp.float32)

    def unif(shape, lo, hi):
        return jax.random.uniform(next(ks), shape, jnp.float32, lo, hi)

    L = DEPTH
    dt0 = jnp.exp(unif((L, H_C), math.log(1e-3), math.log(1e-1)))
    lam_im0 = jnp.broadcast_to(math.pi * jnp.arange(P_A, dtype=jnp.float32), (L, G_A, P_A))
    return {
        'x_prompt': nrm((BATCH, SEQ, D_MODEL), 1.0),
        'x_sample': nrm((DEC_BATCH, DEC_SEQ, D_MODEL), 1.0),
        'c_prompt': nrm((BATCH, D_MODEL), 1.0),
        'c_sample': nrm((DEC_BATCH, D_MODEL), 1.0),
        'state_s5_re': nrm((L, DEC_BATCH, G_A, P_A), 0.3),
        'state_s5_im': nrm((L, DEC_BATCH, G_A, P_A), 0.3),
        'state_ssm': nrm((L, DEC_BATCH, H_C, HD_C, N_C), 0.1),
        'state_ssd_conv': nrm((L, DEC_BATCH, K_C - 1, D_XBC), 1.0),
        'state_ffn_conv': nrm((L, DEC_BATCH, K_F - 1, 2 * D_FF), 1.0),
        'w_mod': nrm((L, D_MODEL, 6 * D_MODEL), 0.3 * D_MODEL ** -0.5),
        'b_mod': nrm((L, 6 * D_MODEL), 0.02),
        'g_mix': 1.0 + nrm((L, D_MODEL), 0.02),
        'w_in': nrm((L, D_MODEL, N_IN), D_MODEL ** -0.5),
        's5_lam_re': -0.5 + nrm((L, G_A, P_A), 0.01),
        's5_lam_im': lam_im0 + nrm((L, G_A, P_A), 0.01),
        's5_log_dt': unif((L, G_A), math.log(1e-3), math.log(1e-1)),
        's5_b_re': nrm((L, G_A, P_A, S5_GROUP), (2 * S5_GROUP) ** -0.5),
        's5_b_im': nrm((L, G_A, P_A, S5_GROUP), (2 * S5_GROUP) ** -0.5),
        's5_c_re': nrm((L, G_A, S5_GROUP, P_A), P_A ** -0.5),
        's5_c_im': nrm((L, G_A, S5_GROUP, P_A), P_A ** -0.5),
        's5_d': nrm((L, D_A), 1.0),
        's5_w_glu': nrm((L, D_A, D_A), D_A ** -0.5),
        'gm_g_v': 1.0 + nrm((L, D_B), 0.02),
        'gm_w_s': nrm((L, H_B, GM_CHUNK, GM_CHUNK), 0.5 * GM_CHUNK ** -0.5),
        'gm_b_s': 1.0 + nrm((L, H_B, GM_CHUNK), 0.02),
        'ssd_conv_w': nrm((L, K_C, D_XBC), K_C ** -0.5),
        'ssd_conv_b': nrm((L, D_XBC), 0.02),
        'ssd_dt_bias': dt0 + jnp.log(-jnp.expm1(-dt0)),
        'ssd_a_log': jnp.log(unif((L, H_C), 1.0, 16.0)),
        'ssd_d': 1.0 + nrm((L, H_C), 0.1),
        'ssd_g_norm': 1.0 + nrm((L, D_C), 0.02),
        'w_pa': nrm((L, D_A, D_MODEL), D_A ** -0.5),
        'w_pb': nrm((L, D_B, D_MODEL), D_B ** -0.5),
        'w_pc': nrm((L, D_C, D_MODEL), D_C ** -0.5),
        'w_out': nrm((L, D_MODEL, D_MODEL), D_MODEL ** -0.5),
        'g_ffn': 1.0 + nrm((L, D_MODEL), 0.02),
        'ffn_w_up': nrm((L, D_MODEL, 2 * D_FF), D_MODEL ** -0.5),
        'ffn_conv_w': nrm((L, K_F, 2 * D_FF), K_F ** -0.5),
        'ffn_conv_b': nrm((L, 2 * D_FF), 0.02),
        'ffn_w_down': nrm((L, D_FF, D_MODEL), D_FF ** -0.5),
        'g_final': 1.0 + nrm((D_MODEL,), 0.02),
    }


def reference(x_prompt, x_sample, c_prompt, c_sample, state_s5_re, state_s5_im, state_ssm,
              state_ssd_conv, state_ffn_conv, w_mod, b_mod, g_mix, w_in, s5_lam_re, s5_lam_im,
              s5_log_dt, s5_b_re, s5_b_im, s5_c_re, s5_c_im, s5_d, s5_w_glu, gm_g_v, gm_w_s, gm_b_s,
              ssd_conv_w, ssd_conv_b, ssd_dt_bias, ssd_a_log, ssd_d, ssd_g_norm, w_pa, w_pb, w_pc,
              w_out, g_ffn, ffn_w_up, ffn_conv_w, ffn_conv_b, ffn_w_down, g_final):
    bp = x_prompt.shape[0]
    dtp = x_prompt.dtype
    xp, xs = x_prompt, x_sample
    s5r_p, s5i_p, ssm_p, cc_p, cf_p = [], [], [], [], []
    s5r_s, s5i_s, ssm_s, cc_s, cf_s, gv_s = [], [], [], [], [], []
    for l in range(DEPTH):
        p = {
            'w_mod': w_mod[l], 'b_mod': b_mod[l], 'g_mix': g_mix[l], 'w_in': w_in[l],
            'lam_re': s5_lam_re[l], 'lam_im': s5_lam_im[l], 'log_dt': s5_log_dt[l],
            'b_re': s5_b_re[l], 'b_im': s5_b_im[l], 'c_re': s5_c_re[l], 'c_im': s5_c_im[l],
            's5_d': s5_d[l], 'w_glu': s5_w_glu[l],
            'g_v': gm_g_v[l], 'w_s': gm_w_s[l], 'b_s': gm_b_s[l],
            'ssd_conv_w': ssd_conv_w[l], 'ssd_conv_b': ssd_conv_b[l], 'dt_bias': ssd_dt_bias[l],
            'a_log': ssd_a_log[l], 'ssd_d': ssd_d[l], 'ssd_g': ssd_g_norm[l],
            'w_pa': w_pa[l], 'w_pb': w_pb[l], 'w_pc': w_pc[l], 'w_out': w_out[l],
            'g_ffn': g_ffn[l], 'w_up': ffn_w_up[l], 'ffn_conv_w': ffn_conv_w[l],
            'ffn_conv_b': ffn_conv_b[l], 'w_down': ffn_w_down[l],
        }
        xp, a1, a2, a3, a4, a5, _ = layer(
            xp, c_prompt,
            jnp.zeros((bp, G_A, P_A), dtp), jnp.zeros((bp, G_A, P_A), dtp),
            jnp.zeros((bp, H_C, HD_C, N_C), dtp), jnp.zeros((bp, K_C - 1, D_XBC), dtp),
            jnp.zeros((bp, K_F - 1, 2 * D_FF), dtp), p)
        s5r_p.append(a1); s5i_p.append(a2); ssm_p.append(a3); cc_p.append(a4); cf_p.append(a5)
        xs, b1, b2, b3, b4, b5, b6 = layer(
            xs, c_sample, state_s5_re[l], state_s5_im[l], state_ssm[l],
            state_ssd_conv[l], state_ffn_conv[l], p)
        s5r_s.append(b1); s5i_s.append(b2); ssm_s.append(b3); cc_s.append(b4); cf_s.append(b5); gv_s.append(b6)
    y_prompt = rmsnorm(xp, g_final)
    y_sample = rmsnorm(xs, g_final)
    return (y_prompt, y_sample,
            jnp.stack(s5r_p), jnp.stack(s5i_p), jnp.stack(ssm_p), jnp.stack(cc_p), jnp.stack(cf_p),
            jnp.stack(s5r_s), jnp.stack(s5i_s), jnp.stack(ssm_s), jnp.stack(cc_s), jnp.stack(cf_s),
            jnp.stack(gv_s))
```

```python
import math
from contextlib import ExitStack
import numpy as np
import concourse.bass as bass
import concourse.mybir as mybir
from concourse.bass_utils import run_bass_kernel_spmd

F32 = mybir.dt.float32
FR = mybir.dt.float32r
AF = mybir.ActivationFunctionType
ALU = mybir.AluOpType

D = 2048; NP = 1024; NSMP = 16; DEPTH = 2
EPS = 1e-6
NBIN = 33
BIN_UA, BIN_UB, BIN_VB, BIN_Z, BIN_XBC, BIN_DT = 0, 4, 8, 12, 20, 32
SEQC = 2048 + NSMP
TW = 1048
PI = math.pi


class Tok:
    __slots__ = ("w", "r")

    def __init__(self):
        self.w = None
        self.r = []


class Buf:
    def __init__(self, t, P, name):
        self.t = t
        self.k = Tok()
        self._d = None
        self.P = P
        self.name = name

    @property
    def d(self):
        if self._d is None:
            self._d = self.P.dsem(self.name)
        return self._d


class Prog:
    def __init__(self, nc):
        self.nc = nc
        self.eng = {"pe": nc.tensor, "act": nc.scalar, "dve": nc.vector, "pool": nc.gpsimd, "sp": nc.sync}
        self.streams = {e: [] for e in self.eng}
        self.sem = {}
        self.cnt = {}
        self.seen = {e: {} for e in self.eng}
        self.dsems = {}
        self.pend = {e: ([], []) for e in self.eng}

    def open(self, stack):
        self.stack = stack
        for e in ("pe", "act", "dve", "pool"):
            self.sem[e] = stack.enter_context(self.nc.semaphore("s_" + e))
            self.cnt[e] = 0

    def dsem(self, name):
        if name not in self.dsems:
            s = self.stack.enter_context(self.nc.semaphore("d_" + name))
            self.dsems[name] = [s, 0, None, name]
        return self.dsems[name]

    def _waits(self, eng, R, W, extra=()):
        evs = list(extra)
        for t in R:
            if t.w is not None:
                evs.append(t.w)
        for t in W:
            if t.w is not None:
                evs.append(t.w)
            evs.extend(t.r)
        need = {}
        for (key, s, v, src) in evs:
            if src == "pe" and eng == "pe":
                continue
            if need.get(key, (None, 0))[1] < v:
                need[key] = (s, v)
        out = []
        seen = self.seen[eng]
        for key, (s, v) in need.items():
            if seen.get(key, 0) >= v:
                continue
            seen[key] = v
            out.append((s, v))
        return out

    def op(self, eng, fn, R=(), W=(), inc=True):
        waits = self._waits(eng, R, W)
        if inc:
            self.cnt[eng] += 1
            ev = (eng, self.sem[eng], self.cnt[eng], eng)
            pr, pw = self.pend[eng]
            self.pend[eng] = ([], [])
            for t in list(R) + pr:
                t.r.append(ev)
            for t in list(W) + pw:
                t.w = ev
                t.r = []
        else:
            self.pend[eng][0].extend(R)
            self.pend[eng][1].extend(W)
        self.streams[eng].append((waits, fn, (self.sem[eng], 1) if inc else None))

    def dma(self, q, ds, out, in_, R=(), W=()):
        extra = [ds[2]] if ds[2] is not None else []
        waits = self._waits(q, R, W, extra)
        ds[1] += 16
        ev = (ds[3], ds[0], ds[1], "dma")
        ds[2] = ev
        for t in R:
            t.r.append(ev)
        for t in W:
            t.w = ev
            t.r = []
        self.streams[q].append((waits, lambda e, o=out, i=in_: e.dma_start(out=o, in_=i), (ds[0], 16)))

    def barrier(self):
        evs = [(e, self.sem[e], self.cnt[e]) for e in self.sem if self.cnt[e] > 0]
        evs += [(d[3], d[0], d[1]) for d in self.dsems.values() if d[1] > 0]
        for e in self.eng:
            waits = []
            for key, s, v in evs:
                if self.seen[e].get(key, 0) < v:
                    self.seen[e][key] = v
                    waits.append((s, v))
            if waits:
                self.streams[e].append((waits, None, None))

    def emit(self, block):
        def mk(e):
            def body(engine):
                for waits, fn, inc in self.streams[e]:
                    for s, v in waits:
                        engine.wait_ge(s, v)
                    if fn is not None:
                        ins = fn(engine)
                        if inc is not None:
                            ins.then_inc(inc[0], inc[1])
            return body
        block.tensor(mk("pe"))
        block.scalar(mk("act"))
        block.vector(mk("dve"))
        block.gpsimd(mk("pool"))
        block.sync(mk("sp"))


def build_program(stop_after=None):
    nc = bass.Bass("TRN2", target_bir_lowering=False)
    nc.dge_precook = False
    P = Prog(nc)

    def din(name, shape, dt=F32):
        return nc.dram_tensor(name, list(shape), dt, kind="ExternalInput").ap()

    def dout(name, shape):
        return nc.dram_tensor(name, list(shape), F32, kind="ExternalOutput").ap()

    def dscr(name, shape, dt=F32):
        return nc.dram_tensor(name, list(shape), dt, kind="Internal").ap()

    xT = [din("xT0", [128, 16, NP + NSMP]), din("xT1", [128, 16, NP])]
    cT = din("cT", [128, 16, 18])
    g_final = din("g_final", [128, 16])
    c_ident = din("ident", [128, 128])
    c_ones = din("ones", [128, 128], FR)
    c_negmask = din("negmaskT", [128, 128])
    c_tril = din("trilT", [128, 128])
    c_sel16 = din("sel16", [16, 16, 128], FR)
    c_selexp = din("selexp", [16, 8, 128], FR)
    c_iota = din("iota", [128, 128])
    s5r0 = din("s5r0", [DEPTH, NSMP, 128, 16]); s5i0 = din("s5i0", [DEPTH, NSMP, 128, 16])
    ssm0 = din("ssm0", [DEPTH, NSMP, 128, 1024], FR)
    cbuf0 = din("cbuf0", [DEPTH, 128, 12, NSMP, 3])
    fbuf0 = din("fbuf0", [DEPTH, 128, 88, NSMP, 2])
    Lw = []
    for l in range(DEPTH):
        w = {}
        w["mod"] = din(f"w_mod{l}", [96, 128, 2048], FR)
        w["b_mod"] = din(f"b_mod{l}", [128, 96])
        w["g_mix"] = din(f"g_mix{l}", [128, 16]); w["g_ffn"] = din(f"g_ffn{l}", [128, 16])
        w["inb"] = din(f"w_inb{l}", [32, 128, 2048], FR)
        w["indt"] = din(f"w_indt{l}", [128, 256], FR)
        w["ing"] = din(f"w_ing{l}", [48, 128, 2048], FR)
        w["pa"] = din(f"w_pa{l}", [16, 128, 512], FR)
        w["pb"] = din(f"w_pb{l}", [16, 128, 512], FR)
        w["pc"] = din(f"w_pc{l}", [16, 128, 1024], FR)
        w["out"] = din(f"w_out{l}", [16, 128, 2048], FR)
        w["up"] = din(f"w_up{l}", [88, 128, 2048], FR)
        w["down"] = din(f"w_down{l}", [44, 128, 2048], FR)
        w["glu"] = din(f"w_glu{l}", [128, 2048], FR)
        w["lamr"] = din(f"lamr{l}", [128, 16]); w["lami"] = din(f"lami{l}", [128, 16])
        w["logdt"] = din(f"logdt{l}", [128, 16])
        w["Bre"] = din(f"Bre{l}", [128, 16, 128]); w["Bim"] = din(f"Bim{l}", [128, 16, 128])
        w["Cre"] = din(f"Cre{l}", [128, 16, 128], FR); w["Cim"] = din(f"Cim{l}", [128, 16, 128], FR)
        w["s5d"] = din(f"s5d{l}", [128, 4])
        w["gv"] = din(f"gv{l}", [128, 4])
        w["wsT"] = din(f"wsT{l}", [128, 8, 128]); w["bsx"] = din(f"bsx{l}", [128, 4, 128])
        w["cw"] = din(f"cw{l}", [128, 12, 4]); w["cb"] = din(f"cb{l}", [128, 12])
        w["dtb"] = din(f"dtb{l}", [16, 1]); w["alog"] = din(f"alog{l}", [16, 1])
        w["ssdD"] = din(f"ssdD{l}", [128, 8]); w["ssdg"] = din(f"ssdg{l}", [128, 8])
        w["fcw"] = din(f"fcw{l}", [128, 88, 3]); w["fcb"] = din(f"fcb{l}", [128, 88])
        Lw.append(w)
    o_y = [dout("o_y0", [128, 16, NP + NSMP]), dout("o_y1", [128, 16, NP])]
    o_s5p = dout("o_s5p", [DEPTH, 2, 128, 16])
    o_s5s = dout("o_s5s", [DEPTH, NSMP, 2, 128, 16])
    o_ssmp = dout("o_ssmp", [DEPTH, 128, 1024])
    o_ssms = dout("o_ssms", [DEPTH, NSMP, 128, 1024])
    o_ccp = dout("o_ccp", [DEPTH, 128, 12, 3])
    o_ccs = dout("o_ccs", [DEPTH, 128, 12, NSMP, 3])
    o_cfp = dout("o_cfp", [DEPTH, 128, 88, 2])
    o_cfs = dout("o_cfs", [DEPTH, 128, 88, NSMP, 2])
    o_gv = dout("o_gv", [DEPTH, 128, 4, NSMP])
    xres = [dscr("xres0", [128, 16, NP + NSMP]), dscr("xres1", [128, 16, NP])]
    binb = dscr("bin", [NBIN, 128, SEQC], FR)
    binF = binb.bitcast(F32)
    mrg = dscr("mrg", [16, 128, NP + NSMP], FR)
    k_xres = [[Tok() for _ in range(16)] for _ in range(2)]
    k_bin = [Tok() for _ in range(NBIN)]
    k_mrg = [Tok() for _ in range(16)]
    k_out = Tok()
    k_in = Tok()

    with ExitStack() as st:
        P.open(st)

        uid = [0]

        def sbuf(stack, name, shape, dt=F32):
            uid[0] += 1
            return Buf(stack.enter_context(nc.sbuf_tensor(f"{name}_{uid[0]}", list(shape), dt)), P, name)

        def ACT(out, in_, func, R, W, **kw):
            P.op("act", lambda e: e.activation(out=out, in_=in_, func=func, **kw), R, W)

        def TT(out, a, b, op, R, W, eng="dve"):
            P.op(eng, lambda e: e.tensor_tensor(out, a, b, op), R, W)

        def TS(out, a, s1, s2, op0, op1, R, W):
            if op1 is None:
                P.op("dve", lambda e: e.tensor_scalar(out, a, s1, None, op0), R, W)
            else:
                P.op("dve", lambda e: e.tensor_scalar(out, a, s1, s2, op0, op1), R, W)

        def STT(out, a, s, b, op0, op1, R, W):
            P.op("dve", lambda e: e.scalar_tensor_tensor(out, a, s, b, op0, op1), R, W)

        def CP(out, in_, R, W, eng="dve"):
            P.op(eng, lambda e: e.tensor_copy(out, in_), R, W)

        def MS(ap, val, W, eng="dve"):
            P.op(eng, lambda e: e.memset(ap, val), (), W)

        def MM(out, lhsT, rhs, start, stop, R, W, inc):
            P.op("pe", lambda e: e.matmul(out, lhsT, rhs, start=start, stop=stop), R, W, inc)

        def TR(out, in_, idn, R, W):
            P.op("pe", lambda e: e.transpose(out, in_, idn), R, W)

        def DMA(ds, out, in_, R, W, q="pool"):
            P.dma(q, ds, out, in_, R, W)

        hbuf = sbuf(st, "hbuf", [128, 16, NP + NSMP], FR); k_h = [Tok() for _ in range(16)]
        wr = sbuf(st, "wring", [128, 2, 2048], FR); k_wr = [Tok(), Tok()]
        cur_l = [0]

        def d_wr_(s_):
            return P.dsem(f"wr{s_}_{cur_l[0]}")
        modT = sbuf(st, "modT", [128, 96, 18])
        G1 = sbuf(st, "G1", [128, 16, 18]); G2 = sbuf(st, "G2", [128, 16, 18])
        ident = sbuf(st, "identS", [128, 128]); ones = sbuf(st, "onesS", [128, 128], FR)
        silc = sbuf(st, "silc", [128, 16, 18], FR)
        bmod = sbuf(st, "bmod", [128, 96]); gmix = sbuf(st, "gmixS", [128, 16]); gffn = sbuf(st, "gffnS", [128, 16])
        gfin = sbuf(st, "gfin", [128, 16])
        T = [sbuf(st, f"T{i}", [128, TW]) for i in range(5)]
        RB = [sbuf(st, f"R{i}", [128, TW], FR) for i in range(2)]
        Snp = sbuf(st, "Snp", [128, 1024], FR)
        hlp = sbuf(st, "hlp", [128, 2, 16])
        fhalo = sbuf(st, "fhalo", [128, 88, 2])
        psum = [st.enter_context(nc.psum_tensor(f"ps{i}", [128, 512], F32)) for i in range(8)]
        k_ps = [Tok() for _ in range(8)]
        onesF = ones.t[:].bitcast(F32)
        zer = sbuf(st, "zer", [128, 512])
        MS(zer.t[:], 0.0, [zer.k])

        DMA(ident.d, ident.t[:], c_ident, [], [ident.k])
        DMA(ones.d, ones.t[:], c_ones, [], [ones.k])
        DMA(gfin.d, gfin.t[:], g_final, [], [gfin.k])

        def nchunks(ncol):
            r = [(0, 512), (512, 512)]
            if ncol > 1024:
                r.append((1024, ncol - 1024))
            return r

        class WS:
            nxt = 0

            def __init__(self, blocks):
                self.blocks = blocks
                self.loaded = 0
                self.slot0 = WS.nxt

            def get(self, i):
                while self.loaded < min(len(self.blocks), i + 2):
                    j = self.loaded
                    s = (self.slot0 + j) % 2
                    ap, n = self.blocks[j]
                    P.dma("sp", d_wr_(s), wr.t[:, s, 0:n], ap, [], [k_wr[s]])
                    self.loaded += 1
                s = (self.slot0 + i) % 2
                if i == len(self.blocks) - 1:
                    WS.nxt = (s + 1) % 2
                return s

        pset = [0]

        def proj_group(lhs_fn, rhs_fn, KC, ncol, M=128):
            base = 3 * pset[0]; pset[0] ^= 1
            chs = nchunks(ncol)
            for k in range(KC):
                lt, lR = lhs_fn(k)
                for i, (c0, cn) in enumerate(chs):
                    rt, rR = rhs_fn(k, c0, cn)
                    MM(psum[base + i][0:M, 0:cn], lt, rt, k == 0, k == KC - 1, list(lR) + list(rR), [k_ps[base + i]], k == KC - 1)
            return [(psum[base + i], c0, cn, k_ps[base + i]) for i, (c0, cn) in enumerate(chs)]

        def gelu_from(src, srcR, out, outW, tA, tB):
            n = src.shape[-1] if len(src.shape) == 2 else None
            a = tA.t[0:src.shape[0], 0:src.shape[1]]; b = tB.t[0:src.shape[0], 0:src.shape[1]]
            ACT(a, src, AF.Square, srcR, [tA.k])
            TS(a, a, 0.044715, 1.0, ALU.mult, ALU.add, [tA.k], [tA.k])
            TT(a, a, src, ALU.mult, [tA.k] + srcR, [tA.k])
            ACT(b, a, AF.Sigmoid, [tA.k], [tB.k], scale=1.5957691216)
            TT(out, b, src, ALU.mult, [tB.k] + srcR, outW)

        def phase_mod(l):
            w = Lw[l]
            DMA(modT.d, modT.t[:, 0:16, :], cT, [], [modT.k])
            ACT(silc.t[:], modT.t[:, 0:16, :], AF.Silu, [modT.k], [silc.k])
            DMA(bmod.d, bmod.t[:], w["b_mod"], [], [bmod.k])
            ws = WS([(w["mod"][b], 2048) for b in range(96)])
            for fo in range(96):
                s = ws.get(fo)
                wv = wr.t[:, s, :].rearrange("p (k n) -> p k n", k=16)
                base = 6 + (fo % 2)
                for k in range(16):
                    MM(psum[base][:, 0:18], wv[:, k, :], silc.t[:, k, :], k == 0, k == 15, [k_wr[s], silc.k], [k_ps[base]], k == 15)
                ACT(modT.t[:, fo, :], psum[base][:, 0:18], AF.Identity, [k_ps[base], bmod.k], [modT.k], bias=bmod.t[:, fo:fo + 1], scale=1.0)
            DMA(gmix.d, gmix.t[:], w["g_mix"], [], [gmix.k])
            DMA(gffn.d, gffn.t[:], w["g_ffn"], [], [gffn.k])
            for (Gt, off, g) in ((G1, 16, gmix), (G2, 64, gffn)):
                TS(Gt.t[:], modT.t[:, off:off + 16, :], 1.0, None, ALU.add, None, [modT.k], [Gt.k])
                TT(Gt.t[:], Gt.t[:], g.t[:].unsqueeze(2).to_broadcast([128, 16, 18]), ALU.mult, [Gt.k, g.k], [Gt.k])

        def phase_norm(src, src_k, ncol, Gt=None, SHoff=0, dst=None):
            chs = nchunks(ncol)
            rstd = T[4]; sq = RB[0]; tA = T[2]
            for k in range(16):
                s_ = T[k % 2]
                DMA(s_.d, s_.t[:, 0:ncol], src[:, k, 0:ncol], [src_k[k]], [s_.k])
                ACT(sq.t[:, 0:ncol], s_.t[:, 0:ncol], AF.Square, [s_.k], [sq.k])
                for j, (c0, cn) in enumerate(chs):
                    MM(psum[j][:, 0:cn], ones.t[:], sq.t[:, c0:c0 + cn], k == 0, k == 15, [sq.k, ones.k], [k_ps[j]], True)
            for j, (c0, cn) in enumerate(chs):
                TS(rstd.t[:, c0:c0 + cn], psum[j][:, 0:cn], 1.0 / D, EPS, ALU.mult, ALU.add, [k_ps[j]], [rstd.k])
            ACT(rstd.t[:, 0:ncol], rstd.t[:, 0:ncol], AF.Sqrt, [rstd.k], [rstd.k])
            P.op("dve", lambda e: e.reciprocal(rstd.t[:, 0:ncol], rstd.t[:, 0:ncol]), [rstd.k], [rstd.k])
            for k in range(16):
                s_ = T[k % 2]
                DMA(s_.d, s_.t[:, 0:ncol], src[:, k, 0:ncol], [src_k[k]], [s_.k])
                TT(tA.t[:, 0:ncol], s_.t[:, 0:ncol], rstd.t[:, 0:ncol], ALU.mult, [s_.k, rstd.k], [tA.k])
                if dst is None:
                    ACT(hbuf.t[:, k, 0:NP], tA.t[:, 0:NP], AF.Identity, [tA.k, Gt.k, modT.k], [k_h[k]],
                        scale=Gt.t[:, k, 0:1], bias=modT.t[:, SHoff + k, 0:1])
                    if ncol > NP:
                        TT(tA.t[:, NP:ncol], tA.t[:, NP:ncol], Gt.t[:, k, 1:17], ALU.mult, [tA.k, Gt.k], [tA.k])
                        TT(hbuf.t[:, k, NP:ncol], tA.t[:, NP:ncol], modT.t[:, SHoff + k, 1:17], ALU.add, [tA.k, modT.k], [k_h[k]])
                else:
                    o_ = T[3]
                    ACT(o_.t[:, 0:ncol], tA.t[:, 0:ncol], AF.Identity, [tA.k, gfin.k], [o_.k], scale=gfin.t[:, k:k + 1])
                    DMA(o_.d, dst[:, k, 0:ncol], o_.t[:, 0:ncol], [o_.k], [k_out])

        def phase_inproj(l, t, ncol):
            w = Lw[l]
            gcol = NP * t
            ws = WS([(w["inb"][b], 2048) for b in range(32)] + [(w["indt"], 256)])
            sti = [0]

            def evac(fo, res, M=128):
                for (ps_, c0, cn, tk) in res:
                    s_ = T[sti[0]]; sti[0] ^= 1
                    src = ps_[0:M, 0:cn]; o = s_.t[0:M, 0:cn]
                    if BIN_UB <= fo < BIN_Z:
                        gelu_from(src, [tk], o, [s_.k], T[2], T[3])
                    elif BIN_Z <= fo < BIN_XBC:
                        ACT(o, src, AF.Silu, [tk], [s_.k])
                    else:
                        ACT(o, src, AF.Copy, [tk], [s_.k])
                    dcol = gcol + c0 if c0 < NP else 2048
                    DMA(s_.d, binb[fo, 0:M, dcol:dcol + cn], o.bitcast(FR), [s_.k], [k_bin[fo]])

            for b in range(33):
                s = ws.get(b)
                if b < 32:
                    wv = wr.t[:, s, :].rearrange("p (k n) -> p k n", k=16)
                    res = proj_group(lambda k: (wv[:, k, :], [k_wr[s]]),
                                     lambda k, c0, cn: (hbuf.t[:, k, c0:c0 + cn], [k_h[k]]), 16, ncol)
                    evac(b, res)
                else:
                    wv = wr.t[:, s, 0:256].rearrange("p (k n) -> p k n", k=16)
                    res = proj_group(lambda k: (wv[:, k, :], [k_wr[s]]),
                                     lambda k, c0, cn: (hbuf.t[:, k, c0:c0 + cn], [k_h[k]]), 16, ncol, M=16)
                    evac(BIN_DT, res, M=16)

        def phase_s5(l, t, ncol, oa):
            w = Lw[l]
            with ExitStack() as s:
                cosT = sbuf(s, "cosT", [128, 16, 128]); sinT = sbuf(s, "sinT", [128, 16, 128])
                rtab = sbuf(s, "rtab", [128, 16, 128])
                Bbr = sbuf(s, "Bbr", [128, 16, 128], FR); Bbi = sbuf(s, "Bbi", [128, 16, 128], FR)
                Cre = sbuf(s, "CreS", [128, 16, 128], FR); Cim = sbuf(s, "CimS", [128, 16, 128], FR)
                prm = sbuf(s, "s5prm", [128, 16, 16])
                dg = sbuf(s, "s5dg", [128, 2, 128], FR)
                class _W:
                    def __init__(self, b_, ap):
                        self.t = ap; self.k = b_.k; self.d = b_.d
                bl = _W(T[2], T[2].t[:, 0:256].rearrange("p (g i) -> p g i", g=2))
                bt = _W(T[3], T[3].t[:, 0:256].rearrange("p (g i) -> p g i", g=2))
                u = sbuf(s, "s5u", [128, 4, 128], FR)
                hl = sbuf(s, "s5hl", [128, 2, 16]); car = sbuf(s, "s5car", [128, 2, 16]); ct = sbuf(s, "s5ct", [128, 4, 16])
                s5d = sbuf(s, "s5dS", [128, 4]); iot = sbuf(s, "iotS", [128, 128])
                pk = prm.k
                LR, LI, LD, DT_, LRD, TH, RR, AR, AI, DEN, NR, KR, KI, X1, X2 = [prm.t[:, i, :] for i in range(15)]
                DMA(prm.d, LR, w["lamr"], [], [pk]); DMA(prm.d, LI, w["lami"], [], [pk]); DMA(prm.d, LD, w["logdt"], [], [pk])
                DMA(Cre.d, Cre.t[:], w["Cre"], [], [Cre.k]); DMA(Cim.d, Cim.t[:], w["Cim"], [], [Cim.k])
                DMA(s5d.d, s5d.t[:], w["s5d"], [], [s5d.k])
                DMA(iot.d, iot.t[:], c_iota, [], [iot.k])
                ACT(DT_, LD, AF.Exp, [pk], [pk])
                TT(LRD, LR, DT_, ALU.mult, [pk], [pk]); TT(TH, LI, DT_, ALU.mult, [pk], [pk])
                ACT(RR, LRD, AF.Exp, [pk], [pk])
                ki32 = sbuf(s, "s5ki", [128, 128], mybir.dt.int32)
                hpi = sbuf(s, "s5hpi", [128, 1])
                MS(hpi.t[:], 0.5 * PI, [hpi.k])
                for m in range(16):
                    a_ = T[0].t[:, 0:128]; b_ = T[1].t[:, 0:128]; s_ = T[0].t[:, 128:256]; c_ = T[1].t[:, 128:256]
                    TS(a_, iot.t[:], TH[:, m:m + 1], None, ALU.mult, None, [iot.k, pk], [T[0].k])
                    TS(b_, a_, 1.0 / (2 * PI), None, ALU.mult, None, [T[0].k], [T[1].k])
                    CP(ki32.t[:], b_, [T[1].k], [ki32.k])
                    CP(b_, ki32.t[:], [ki32.k], [T[1].k])
                    STT(b_, b_, -2.0 * PI, a_, ALU.mult, ALU.add, [T[1].k, T[0].k], [T[1].k])
                    ACT(s_, b_, AF.Sin, [T[1].k], [T[0].k], scale=0.5)
                    ACT(c_, b_, AF.Sin, [T[1].k, hpi.k], [T[1].k], scale=-0.5, bias=hpi.t[:, 0:1])
                    STT(sinT.t[:, m, :], s_, 2.0, c_, ALU.mult, ALU.mult, [T[0].k, T[1].k], [sinT.k])
                    TT(c_, s_, s_, ALU.mult, [T[0].k], [T[1].k])
                    TS(cosT.t[:, m, :], c_, -2.0, 1.0, ALU.mult, ALU.add, [T[1].k], [cosT.k])
                    TS(rtab.t[:, m, :], onesF, RR[:, m:m + 1], None, ALU.mult, None, [ones.k, pk], [rtab.k])
                MS(rtab.t[:, :, 0:1], 0.0, [rtab.k])
                TT(AR, RR, cosT.t[:, :, 1], ALU.mult, [pk, cosT.k], [pk]); TT(AI, RR, sinT.t[:, :, 1], ALU.mult, [pk, sinT.k], [pk])
                TT(DEN, LR, LR, ALU.mult, [pk], [pk]); TT(X1, LI, LI, ALU.mult, [pk], [pk]); TT(DEN, DEN, X1, ALU.add, [pk], [pk])
                P.op("dve", lambda e: e.reciprocal(DEN, DEN), [pk], [pk])
                TS(NR, AR, -1.0, None, ALU.add, None, [pk], [pk])
                TT(X1, NR, LR, ALU.mult, [pk], [pk]); TT(X2, AI, LI, ALU.mult, [pk], [pk]); TT(X1, X1, X2, ALU.add, [pk], [pk]); TT(KR, X1, DEN, ALU.mult, [pk], [pk])
                TT(X1, AI, LR, ALU.mult, [pk], [pk]); TT(X2, NR, LI, ALU.mult, [pk], [pk]); TT(X1, X1, X2, ALU.subtract, [pk], [pk]); TT(KI, X1, DEN, ALU.mult, [pk], [pk])
                for m in range(16):
                    TS(dg.t[:, 0, :], ident.t[:], KR[:, m:m + 1], None, ALU.mult, None, [ident.k, pk], [dg.k])
                    TS(dg.t[:, 1, :], ident.t[:], KI[:, m:m + 1], None, ALU.mult, None, [ident.k, pk], [dg.k])
                    MM(psum[7][:, 0:128], ones.t[:], dg.t[:, 0, :], True, True, [ones.k, dg.k], [k_ps[7]], False)
                    MM(psum[7][:, 128:256], ones.t[:], dg.t[:, 1, :], True, True, [ones.k, dg.k], [k_ps[7]], True)
                    DMA(bl.d, bl.t[:, 0, :], w["Bre"][:, m, :], [], [bl.k]); DMA(bl.d, bl.t[:, 1, :], w["Bim"][:, m, :], [], [bl.k])
                    kr_b = psum[7][:, 0:128]; ki_b = psum[7][:, 128:256]
                    TT(bt.t[:, 0, :], kr_b, bl.t[:, 0, :], ALU.mult, [k_ps[7], bl.k], [bt.k])
                    TT(bt.t[:, 1, :], ki_b, bl.t[:, 1, :], ALU.mult, [k_ps[7], bl.k], [bt.k])
                    TT(Bbr.t[:, m, :], bt.t[:, 0, :], bt.t[:, 1, :], ALU.subtract, [bt.k], [Bbr.k])
                    TT(bt.t[:, 0, :], kr_b, bl.t[:, 1, :], ALU.mult, [k_ps[7], bl.k], [bt.k])
                    TT(bt.t[:, 1, :], ki_b, bl.t[:, 0, :], ALU.mult, [k_ps[7], bl.k], [bt.k])
                    TT(Bbi.t[:, m, :], bt.t[:, 0, :], bt.t[:, 1, :], ALU.add, [bt.k], [Bbi.k])

                def carry_from(hsrc, hk):
                    c0, c1, c2, c3 = [ct.t[:, i, :] for i in range(4)]
                    TT(c0, AR, hsrc[:, 0, :], ALU.mult, [pk, hk], [ct.k]); TT(c1, AI, hsrc[:, 1, :], ALU.mult, [pk, hk], [ct.k])
                    TT(car.t[:, 0, :], c0, c1, ALU.subtract, [ct.k], [car.k])
                    TT(c2, AR, hsrc[:, 1, :], ALU.mult, [pk, hk], [ct.k]); TT(c3, AI, hsrc[:, 0, :], ALU.mult, [pk, hk], [ct.k])
                    TT(car.t[:, 1, :], c2, c3, ALU.add, [ct.k], [car.k])

                def unit(gc, b):
                    smp = b is not None
                    qi = 0 if smp else 127
                    hst = hl if smp else hlp
                    if smp:
                        CP(u.t[:].rearrange("p k i -> p (k i)"), zer.t[:], [zer.k], [u.k])
                        DMA(u.d, u.t[:, :, 0:1], binb[BIN_UA:BIN_UA + 4, :, 2048 + b:2049 + b].rearrange("k p c -> p k c"), [k_bin[i] for i in range(4)], [u.k])
                        DMA(hl.d, hl.t[:, 0, :], s5r0[l, b], [], [hl.k]); DMA(hl.d, hl.t[:, 1, :], s5i0[l, b], [], [hl.k])
                    else:
                        DMA(u.d, u.t[:], binb[BIN_UA:BIN_UA + 4, :, gc * 128:gc * 128 + 128].rearrange("k p c -> p k c"), [k_bin[i] for i in range(4)], [u.k])
                        if gc == 0:
                            MS(hlp.t[:], 0.0, [hlp.k])
                    carry_from(hst.t, hst.k)
                    for hf in range(2):
                        cosv = cosT.t[:, 8 * hf:8 * hf + 8, :]; sinv = sinT.t[:, 8 * hf:8 * hf + 8, :]
                        v3 = lambda B_: B_.t[:, 0:1024].rearrange("p (m i) -> p m i", m=8)
                        for mm in range(8):
                            m = 8 * hf + mm
                            bank = mm // 4; c0 = 128 * (mm % 4)
                            MM(psum[bank][:, c0:c0 + 128], Bbr.t[:, m, :], u.t[:, m // 4, :], True, True, [Bbr.k, u.k], [k_ps[bank]], mm % 4 == 3)
                            MM(psum[2 + bank][:, c0:c0 + 128], Bbi.t[:, m, :], u.t[:, m // 4, :], True, True, [Bbi.k, u.k], [k_ps[2 + bank]], mm % 4 == 3)
                        for bank in range(2):
                            cs_ = cosT.t[:, 8 * hf + 4 * bank:8 * hf + 4 * bank + 4, :]; sn_ = sinT.t[:, 8 * hf + 4 * bank:8 * hf + 4 * bank + 4, :]
                            sl = slice(512 * bank, 512 * bank + 512)
                            pr = psum[bank][:, :].rearrange("p (m i) -> p m i", m=4); pi_ = psum[2 + bank][:, :].rearrange("p (m i) -> p m i", m=4)
                            v4 = lambda B_: B_.t[:, sl].rearrange("p (m i) -> p m i", m=4)
                            TT(v4(T[0]), pr, cs_, ALU.mult, [k_ps[bank], cosT.k], [T[0].k])
                            TT(v4(T[1]), pi_, sn_, ALU.mult, [k_ps[2 + bank], sinT.k], [T[1].k])
                            TT(v4(T[2]), v4(T[0]), v4(T[1]), ALU.add, [T[0].k, T[1].k], [T[2].k], eng="pool")
                            TT(v4(T[0]), pi_, cs_, ALU.mult, [k_ps[2 + bank], cosT.k], [T[0].k])
                            TT(v4(T[1]), pr, sn_, ALU.mult, [k_ps[bank], sinT.k], [T[1].k])
                            TT(v4(T[3]), v4(T[0]), v4(T[1]), ALU.subtract, [T[0].k, T[1].k], [T[3].k], eng="pool")
                        TT(v3(T[2])[:, :, 0], v3(T[2])[:, :, 0], car.t[:, 0, 8 * hf:8 * hf + 8], ALU.add, [T[2].k, car.k], [T[2].k])
                        TT(v3(T[3])[:, :, 0], v3(T[3])[:, :, 0], car.t[:, 1, 8 * hf:8 * hf + 8], ALU.add, [T[3].k, car.k], [T[3].k])
                        rt = rtab.t[:, 8 * hf:8 * hf + 8, :].rearrange("p m i -> p (m i)")
                        P.op("dve", lambda e, rt=rt: e.tensor_tensor_scan(T[2].t[:, 0:1024], rt, T[2].t[:, 0:1024], 0.0, ALU.mult, ALU.add), [rtab.k, T[2].k], [T[2].k])
                        P.op("dve", lambda e, rt=rt: e.tensor_tensor_scan(T[3].t[:, 0:1024], rt, T[3].t[:, 0:1024], 0.0, ALU.mult, ALU.add), [rtab.k, T[3].k], [T[3].k])
                        gr = v3(T[2]); gi = v3(T[3])
                        hr = RB[0]; hin = RB[1]
                        v3f = lambda B_: B_.t[:, 0:1024].bitcast(F32).rearrange("p (m i) -> p m i", m=8)
                        TT(v3(T[0]), gr, cosv, ALU.mult, [T[2].k, cosT.k], [T[0].k])
                        TT(v3(T[1]), gi, sinv, ALU.mult, [T[3].k, sinT.k], [T[1].k])
                        TT(v3(hr), v3(T[0]), v3(T[1]), ALU.subtract, [T[0].k, T[1].k], [hr.k], eng="pool")
                        TT(v3(T[0]), gr, sinv, ALU.mult, [T[2].k, sinT.k, hr.k], [T[0].k])
                        TT(v3(T[1]), gi, cosv, ALU.mult, [T[3].k, cosT.k, hr.k], [T[1].k])
                        STT(v3(hin), v3(T[0]), -1.0, v3(T[1]), ALU.mult, ALU.subtract, [T[0].k, T[1].k], [hin.k])
                        CP(hst.t[:, 0, 8 * hf:8 * hf + 8], v3f(hr)[:, :, qi], [hr.k], [hst.k])
                        TS(hst.t[:, 1, 8 * hf:8 * hf + 8], v3f(hin)[:, :, qi], -1.0, None, ALU.mult, None, [hin.k], [hst.k])
                        for jj in range(2):
                            j = 2 * hf + jj
                            for q in range(4):
                                mm = 4 * jj + q; m = 8 * hf + mm
                                MM(psum[4][:, 128 * jj:128 * jj + 128], Cre.t[:, m, :], hr.t[:, 128 * mm:128 * mm + 128], q == 0, False, [Cre.k, hr.k], [k_ps[4]], False)
                                MM(psum[4][:, 128 * jj:128 * jj + 128], Cim.t[:, m, :], hin.t[:, 128 * mm:128 * mm + 128], False, q == 3, [Cim.k, hin.k], [k_ps[4]], q == 3)
                            yv = T[0].t[:, 0:128]
                            STT(yv, u.t[:, j, :].bitcast(F32), s5d.t[:, j:j + 1], psum[4][:, 128 * jj:128 * jj + 128], ALU.mult, ALU.add, [u.k, s5d.k, k_ps[4]], [T[0].k])
                            if smp:
                                _g1(T[0].t[:, 0:1], T[0].k, oa.t[:, j, NP + b:NP + b + 1], oa.k)
                            else:
                                lc = (gc % 8) * 128
                                _g1(yv, T[0].k, oa.t[:, j, lc:lc + 128], oa.k)
                    if smp:
                        DMA(hl.d, o_s5s[l, b].rearrange("r p m -> p r m"), hl.t[:], [hl.k], [k_out])
                    elif gc == 15:
                        DMA(hlp.d, o_s5p[l].rearrange("r p m -> p r m"), hlp.t[:], [hlp.k], [k_out])

                gtmp = _W(T[4], T[4].t[:, 0:256].rearrange("p (g i) -> p g i", g=2))

                def _g1(src, srck, out, outk):
                    n = src.shape[1]
                    a = gtmp.t[:, 0, 0:n]; b_ = gtmp.t[:, 1, 0:n]
                    ACT(a, src, AF.Square, [srck], [gtmp.k])
                    TS(a, a, 0.044715, 1.0, ALU.mult, ALU.add, [gtmp.k], [gtmp.k])
                    TT(a, a, src, ALU.mult, [gtmp.k, srck], [gtmp.k])
                    ACT(b_, a, AF.Sigmoid, [gtmp.k], [gtmp.k], scale=1.5957691216)
                    TT(out, b_, src, ALU.mult, [gtmp.k, srck], [outk])

                for c in range(8):
                    unit(8 * t + c, None)
                if t == 0:
                    for b in range(NSMP):
                        unit(None, b)
                ws = WS([(w["glu"], 2048)])
                s_ = ws.get(0)
                wv = wr.t[:, s_, :].rearrange("p (k n) -> p k n", k=4)
                for fo in range(4):
                    res = proj_group(lambda k: (wv[:, k, 128 * fo:128 * fo + 128], [k_wr[s_]]),
                                     lambda k, c0, cn: (oa.t[:, k, c0:c0 + cn], [oa.k]), 4, ncol)
                    for (ps_, c0, cn, tk) in res:
                        ACT(T[fo].t[:, c0:c0 + cn], ps_[:, 0:cn], AF.Sigmoid, [tk], [T[fo].k])
                for fo in range(4):
                    TT(oa.t[:, fo, 0:ncol], oa.t[:, fo, 0:ncol].bitcast(F32), T[fo].t[:, 0:ncol], ALU.mult, [oa.k, T[fo].k], [oa.k])

        def phase_gmlp(l, t, ncol, ob):
            w = Lw[l]
            with ExitStack() as s:
                wsm = sbuf(s, "wsm", [128, 8, 128], FR); bsx = sbuf(s, "bsxS", [128, 4, 128]); tri = sbuf(s, "triS", [128, 128])
                gv = sbuf(s, "gvS", [128, 4])
                ug = sbuf(s, "ug", [128, 4, 128]); vg = sbuf(s, "vg", [128, 4, 128]); vn = sbuf(s, "vn", [128, 4, 128])
                vtk = sbuf(s, "vtk", [128, 512], FR); sq = sbuf(s, "gsq", [128, 128], FR); rs = sbuf(s, "grs", [128, 128]); sv = sbuf(s, "gsv", [128, 128])
                DMA(bsx.d, bsx.t[:], w["bsx"], [], [bsx.k]); DMA(tri.d, tri.t[:], c_tril, [], [tri.k]); DMA(gv.d, gv.t[:], w["gv"], [], [gv.k])
                wtmp = T[0].t[:, 0:1024].rearrange("p (h i) -> p h i", h=8)
                DMA(T[0].d, wtmp, w["wsT"], [], [T[0].k])
                TT(wsm.t[:], wtmp, tri.t[:].unsqueeze(1).to_broadcast([128, 8, 128]), ALU.mult, [T[0].k, tri.k], [wsm.k])

                def unit(gc, b):
                    smp = b is not None
                    if smp:
                        MS(ug.t[:], 0.0, [ug.k]); MS(vg.t[:], 0.0, [vg.k])
                        DMA(ug.d, ug.t[:, :, 0:1], binF[BIN_UB:BIN_UB + 4, :, 2048 + b:2049 + b].rearrange("k p c -> p k c"), [k_bin[BIN_UB + i] for i in range(4)], [ug.k])
                        DMA(vg.d, vg.t[:, :, 0:1], binF[BIN_VB:BIN_VB + 4, :, 2048 + b:2049 + b].rearrange("k p c -> p k c"), [k_bin[BIN_VB + i] for i in range(4)], [vg.k])
                    else:
                        c0 = gc * 128
                        DMA(ug.d, ug.t[:], binF[BIN_UB:BIN_UB + 4, :, c0:c0 + 128].rearrange("k p c -> p k c"), [k_bin[BIN_UB + i] for i in range(4)], [ug.k])
                        DMA(vg.d, vg.t[:], binF[BIN_VB:BIN_VB + 4, :, c0:c0 + 128].rearrange("k p c -> p k c"), [k_bin[BIN_VB + i] for i in range(4)], [vg.k])
                    for k in range(4):
                        ACT(sq.t[:], vg.t[:, k, :], AF.Square, [vg.k], [sq.k])
                        MM(psum[6][:, 0:128], ones.t[:], sq.t[:], k == 0, k == 3, [ones.k, sq.k], [k_ps[6]], True)
                    TS(rs.t[:], psum[6][:, 0:128], 1.0 / 512, EPS, ALU.mult, ALU.add, [k_ps[6]], [rs.k])
                    ACT(rs.t[:], rs.t[:], AF.Sqrt, [rs.k], [rs.k])
                    P.op("dve", lambda e: e.reciprocal(rs.t[:], rs.t[:]), [rs.k], [rs.k])
                    for k in range(4):
                        STT(vn.t[:, k, :], vg.t[:, k, :], gv.t[:, k:k + 1], rs.t[:], ALU.mult, ALU.mult, [vg.k, gv.k, rs.k], [vn.k])
                    if smp:
                        DMA(vn.d, o_gv[l][:, :, b:b + 1], vn.t[:, :, 0:1], [vn.k], [k_out])
                    for k in range(4):
                        TR(psum[7][:, 128 * k:128 * k + 128], vn.t[:, k, :], ident.t[:], [vn.k, ident.k], [k_ps[7]])
                    ACT(vtk.t[:], psum[7][:, :], AF.Copy, [k_ps[7]], [vtk.k])
                    for k in range(4):
                        bank = 4 + (k % 2)
                        MM(psum[bank][:, 0:128], vtk.t[:, 128 * k:128 * k + 128], wsm.t[:, 2 * k, :], True, True, [vtk.k, wsm.k], [k_ps[bank]], False)
                        MM(psum[bank][:, 128:256], vtk.t[:, 128 * k:128 * k + 128], wsm.t[:, 2 * k + 1, :], True, True, [vtk.k, wsm.k], [k_ps[bank]], True)
                        TT(sv.t[0:64, :], psum[bank][0:64, 0:128], bsx.t[0:64, k, :], ALU.add, [k_ps[bank], bsx.k], [sv.k])
                        TT(sv.t[64:128, :], psum[bank][64:128, 128:256], bsx.t[64:128, k, :], ALU.add, [k_ps[bank], bsx.k], [sv.k])
                        if smp:
                            TT(ob.t[:, k, NP + b:NP + b + 1], sv.t[:, 0:1], ug.t[:, k, 0:1], ALU.mult, [sv.k, ug.k], [ob.k])
                        else:
                            lc = (gc % 8) * 128
                            TT(ob.t[:, k, lc:lc + 128], sv.t[:], ug.t[:, k, :], ALU.mult, [sv.k, ug.k], [ob.k])

                for c in range(8):
                    unit(8 * t + c, None)
                if t == 0:
                    for b in range(NSMP):
                        unit(None, b)

        def phase_ssd(l, t, ncol, oc):
            w = Lw[l]
            with ExitStack() as s:
                xe = sbuf(s, "xe", [128, 12, 131]); bc = sbuf(s, "bcS", [128, 4, 128], FR)
                class _V:
                    def __init__(self, t, d=None):
                        self.t = t; self.k = Tok(); self.d = d
                MT = _V(wr.t[:, 0, :].rearrange("p (h i) -> p h i", h=16))
                sex = _V(wr.t[0:16, 1, 1024:2048].rearrange("p (m i) -> p m i", m=8), d_wr_(1))
                xdd = _V(wr.t[:, 1, 0:1024])
                sm = _V(T[3].t[0:16, 0:640].rearrange("p (m i) -> p m i", m=5), T[3].d)
                GT = _V(T[3].t[:, 640:896].rearrange("p (g i) -> p g i", g=2))
                LT = _V(T[4].t[:, 0:256].rearrange("p (g i) -> p g i", g=2))
                ecs = _V(T[4].t[:, 256:384]); yt = _V(T[4].t[:, 384:512])
                cva = _V(T[4].t[:, 512:768].rearrange("p (g i) -> p g i", g=2)); rs2 = _V(T[4].t[:, 768:896])
                csm = sbuf(s, "csm", [16, 2, 128], FR)
                negm = sbuf(s, "negm", [128, 128])
                cw = sbuf(s, "cwS", [128, 12, 4]); cb = sbuf(s, "cbS", [128, 12])
                csr = sbuf(s, "csr", [16, 128], FR)
                p16 = sbuf(s, "ssdp16", [16, 4])
                Dx = sbuf(s, "DxS", [128, 8]); gn = sbuf(s, "gnS", [128, 8])
                tk_ = sbuf(s, "ssdtok", [128, 6, 16])
                Btok = sbuf(s, "Btok", [128, 2, 128], FR)
                sqm = sbuf(s, "sqm", [128, 128], FR)
                Sns = _V(RB[1].t[:, 0:1024], RB[1].d)
                b32 = sbuf(s, "b32", [128, 2, 128])
                DMA(sex.d, sex.t, c_selexp, [], [sex.k]); DMA(negm.d, negm.t[:], c_negmask, [], [negm.k])
                DMA(cw.d, cw.t[:], w["cw"], [], [cw.k]); DMA(cb.d, cb.t[:], w["cb"], [], [cb.k])
                DMA(p16.d, p16.t[:, 0:1], w["dtb"], [], [p16.k]); DMA(p16.d, p16.t[:, 1:2], w["alog"], [], [p16.k])
                DMA(Dx.d, Dx.t[:], w["ssdD"], [], [Dx.k]); DMA(gn.d, gn.t[:], w["ssdg"], [], [gn.k])
                ACT(p16.t[:, 2:3], p16.t[:, 1:2], AF.Exp, [p16.k], [p16.k])
                TS(p16.t[:, 2:3], p16.t[:, 2:3], -1.0, None, ALU.mult, None, [p16.k], [p16.k])
                xs = T[0]; zs = T[1]; yz = T[2]; xdt = RB[0]
                xs3 = xs.t[:, 0:1024].rearrange("p (m i) -> p m i", m=8)
                zs3 = zs.t[:, 0:1024].rearrange("p (m i) -> p m i", m=8)
                yz3 = yz.t[:, 0:1024].rearrange("p (m i) -> p m i", m=8)
                dtr, ee, dt32, dA, cs32 = [sm.t[:, i, :] for i in range(5)]
                dt_tok, ncs_tok, csl, de_tok, cd, tk5 = [tk_.t[:, i, :] for i in range(6)]

                def unit(gc, b):
                    smp = b is not None
                    Sn = Sns if smp else Snp
                    if smp:
                        MS(xe.t[:], 0.0, [xe.k]); MS(zs.t[:, 0:1024], 0.0, [zs.k]); MS(dtr, 0.0, [sm.k])
                        DMA(xe.d, xe.t[:, :, 0:3], cbuf0[l][:, :, b, :], [], [xe.k])
                        DMA(xe.d, xe.t[:, :, 3:4], binF[BIN_XBC:BIN_XBC + 12, :, 2048 + b:2049 + b].rearrange("k p c -> p k c"), [k_bin[BIN_XBC + i] for i in range(12)], [xe.k])
                        DMA(zs.d, zs3[:, :, 0:1], binF[BIN_Z:BIN_Z + 8, :, 2048 + b:2049 + b].rearrange("k p c -> p k c"), [k_bin[BIN_Z + i] for i in range(8)], [zs.k])
                        DMA(sm.d, dtr[:, 0:1], binF[BIN_DT, 0:16, 2048 + b:2049 + b], [k_bin[BIN_DT]], [sm.k])
                        DMA(Sn.d, Sn.t[:, 0:1024], ssm0[l, b], [], [Sn.k])
                    else:
                        c0 = gc * 128
                        if gc == 0:
                            MS(xe.t[:, :, 0:3], 0.0, [xe.k])
                            DMA(xe.d, xe.t[:, :, 3:131], binF[BIN_XBC:BIN_XBC + 12, :, 0:128].rearrange("k p c -> p k c"), [k_bin[BIN_XBC + i] for i in range(12)], [xe.k])
                            CP(Sn.t[:, 0:512], zer.t[:], [zer.k], [Sn.k]); CP(Sn.t[:, 512:1024], zer.t[:], [zer.k], [Sn.k])
                        else:
                            DMA(xe.d, xe.t[:], binF[BIN_XBC:BIN_XBC + 12, :, c0 - 3:c0 + 128].rearrange("k p c -> p k c"), [k_bin[BIN_XBC + i] for i in range(12)], [xe.k])
                        DMA(zs.d, zs3, binF[BIN_Z:BIN_Z + 8, :, c0:c0 + 128].rearrange("k p c -> p k c"), [k_bin[BIN_Z + i] for i in range(8)], [zs.k])
                        DMA(sm.d, dtr, binF[BIN_DT, 0:16, c0:c0 + 128], [k_bin[BIN_DT]], [sm.k])
                    if smp:
                        DMA(xe.d, o_ccs[l][:, :, b, :], xe.t[:, :, 1:4], [xe.k], [k_out])
                    elif gc == 15:
                        DMA(xe.d, o_ccp[l], xe.t[:, :, 128:131], [xe.k], [k_out])
                    for k in range(12):
                        cv = cva.t[:, k % 2, :]
                        TS(cv, xe.t[:, k, 3:131], cw.t[:, k, 3:4], cb.t[:, k:k + 1], ALU.mult, ALU.add, [xe.k, cw.k, cb.k], [cva.k])
                        for j_ in (2, 1, 0):
                            STT(cv, xe.t[:, k, j_:j_ + 128], cw.t[:, k, j_:j_ + 1], cv, ALU.mult, ALU.add, [xe.k, cw.k, cva.k], [cva.k])
                        if k < 8:
                            ACT(xs3[:, k, :], cv, AF.Silu, [cva.k], [xs.k])
                        else:
                            ACT(bc.t[:, k - 8, :], cv, AF.Silu, [cva.k], [bc.k])
                            if k < 10:
                                ACT(b32.t[:, k - 8, :], cv, AF.Silu, [cva.k], [b32.k])
                    ACT(ee, dtr, AF.Exp, [sm.k, p16.k], [sm.k], bias=p16.t[:, 0:1], scale=1.0)
                    TS(ee, ee, 1.0, None, ALU.add, None, [sm.k], [sm.k])
                    ACT(dt32, ee, AF.Ln, [sm.k], [sm.k])
                    if smp:
                        MS(dt32[:, 1:128], 0.0, [sm.k])
                    TS(dA, dt32, p16.t[:, 2:3], None, ALU.mult, None, [sm.k, p16.k], [sm.k])
                    P.op("dve", lambda e: e.tensor_tensor_scan(cs32, onesF[0:16, :], dA, 0.0, ALU.mult, ALU.add), [sm.k, ones.k], [sm.k])
                    ACT(csr.t[:], cs32, AF.Copy, [sm.k], [csr.k])
                    TR(psum[0][:, 0:16], dt32, ident.t[0:16, 0:16], [sm.k, ident.k], [k_ps[0]])
                    TR(psum[0][:, 16:32], cs32, ident.t[0:16, 0:16], [sm.k, ident.k], [k_ps[0]])
                    ACT(dt_tok, psum[0][:, 0:16], AF.Copy, [k_ps[0]], [tk_.k])
                    TS(ncs_tok, psum[0][:, 16:32], -1.0, None, ALU.mult, None, [k_ps[0]], [tk_.k])
                    for g in range(2):
                        MM(psum[0][:, 128 + 128 * g:256 + 128 * g], bc.t[:, g, :], bc.t[:, 2 + g, :], True, True, [bc.k], [k_ps[0]], g == 1)
                        TR(psum[1][:, 128 * g:128 * g + 128], b32.t[:, g, :], ident.t[:], [b32.k, ident.k], [k_ps[1]])
                    ACT(GT.t, psum[0][:, 128:384].rearrange("p (g i) -> p g i", g=2), AF.Copy, [k_ps[0]], [GT.k])
                    ACT(Btok.t[:], psum[1][:, 0:256].rearrange("p (g i) -> p g i", g=2), AF.Copy, [k_ps[1]], [Btok.k])
                    for q in range(4):
                        for hh in range(4):
                            h = 4 * q + hh
                            TS(csm.t[:, hh % 2, :], cs32, ident.t[0:16, h:h + 1], None, ALU.mult, None, [sm.k, ident.k], [csm.k])
                            MM(psum[4][:, 128 * hh:128 * hh + 128], ones.t[0:16, :], csm.t[:, hh % 2, :], True, True, [ones.k, csm.k], [k_ps[4]], True)
                        CP(csl[:, 4 * q:4 * q + 4], psum[4][:, :].rearrange("p (h i) -> p h i", h=4)[:, :, 127], [k_ps[4]], [tk_.k])
                        for hh in range(4):
                            h = 4 * q + hh
                            lt = LT.t[:, hh % 2, :]
                            TT(lt, psum[4][:, 128 * hh:128 * hh + 128], negm.t[:], ALU.add, [k_ps[4], negm.k], [LT.k])
                            ACT(lt, lt, AF.Exp, [LT.k, tk_.k], [LT.k], bias=ncs_tok[:, h:h + 1], scale=1.0)
                            TT(MT.t[:, h, :], GT.t[:, h // 8, :], lt, ALU.mult, [GT.k, LT.k], [MT.k])
                    TT(tk5, csl, ncs_tok, ALU.add, [tk_.k], [tk_.k])
                    ACT(de_tok, tk5, AF.Exp, [tk_.k], [tk_.k])
                    ACT(cd, csl, AF.Exp, [tk_.k], [tk_.k])
                    for m in range(8):
                        bank = 2 + m // 4
                        TR(psum[bank][:, 128 * (m % 4):128 * (m % 4) + 128], xs3[:, m, :], ident.t[:], [xs.k, ident.k], [k_ps[bank]])
                    for hb in range(2):
                        TT(xdt.t[:, 512 * hb:512 * hb + 512].rearrange("p (h d) -> p h d", h=8),
                           psum[2 + hb][:, :].rearrange("p (h d) -> p h d", h=8),
                           dt_tok[:, 8 * hb:8 * hb + 8].unsqueeze(2).to_broadcast([128, 8, 64]), ALU.mult, [k_ps[2 + hb], tk_.k], [xdt.k])
                    TT(xdd.t.rearrange("p (h d) -> p h d", h=16), xdt.t[:, 0:1024].bitcast(F32).rearrange("p (h d) -> p h d", h=16),
                       de_tok.unsqueeze(2).to_broadcast([128, 16, 64]), ALU.mult, [xdt.k, tk_.k], [xdd.k])
                    for m in range(8):
                        g = m // 4
                        bank = 5 + (m % 2)
                        pY = psum[bank]
                        MM(pY[:, 0:128], Sn.t[:, 128 * m:128 * m + 128], bc.t[:, 2 + g, :], True, True, [Sn.k, bc.k], [k_ps[bank]], False)
                        MM(pY[:, 128:256], xdt.t[:, 128 * m:128 * m + 128], MT.t[:, 2 * m, :], True, True, [xdt.k, MT.k], [k_ps[bank]], False)
                        MM(pY[:, 256:384], xdt.t[:, 128 * m:128 * m + 128], MT.t[:, 2 * m + 1, :], True, True, [xdt.k, MT.k], [k_ps[bank]], False)
                        MM(pY[:, 384:512], sex.t[:, m, :], csr.t[:], True, True, [sex.k, csr.k], [k_ps[bank]], True)
                        ACT(ecs.t, pY[:, 384:512], AF.Exp, [k_ps[bank]], [ecs.k])
                        TT(yt.t, pY[:, 0:128], ecs.t, ALU.mult, [k_ps[bank], ecs.k], [yt.k])
                        TT(yt.t[0:64, :], yt.t[0:64, :], pY[0:64, 128:256], ALU.add, [yt.k, k_ps[bank]], [yt.k])
                        TT(yt.t[64:128, :], yt.t[64:128, :], pY[64:128, 256:384], ALU.add, [yt.k, k_ps[bank]], [yt.k])
                        STT(yt.t, xs3[:, m, :], Dx.t[:, m:m + 1], yt.t, ALU.mult, ALU.add, [xs.k, Dx.k, yt.k], [yt.k])
                        TT(yz3[:, m, :], yt.t, zs3[:, m, :], ALU.mult, [yt.k, zs.k], [yz.k])
                        ACT(sqm.t[:], yz3[:, m, :], AF.Square, [yz.k], [sqm.k])
                        MM(psum[7][:, 0:128], ones.t[:], sqm.t[:], m == 0, m == 7, [ones.k, sqm.k], [k_ps[7]], True)
                    TS(rs2.t, psum[7][:, 0:128], 1.0 / 1024, EPS, ALU.mult, ALU.add, [k_ps[7]], [rs2.k])
                    ACT(rs2.t, rs2.t, AF.Sqrt, [rs2.k], [rs2.k])
                    P.op("dve", lambda e: e.reciprocal(rs2.t, rs2.t), [rs2.k], [rs2.k])
                    for m in range(8):
                        if smp:
                            STT(oc.t[:, m, NP + b:NP + b + 1], yz3[:, m, 0:1], gn.t[:, m:m + 1], rs2.t[:, 0:1], ALU.mult, ALU.mult, [yz.k, gn.k, rs2.k], [oc.k])
                        else:
                            lc = (gc % 8) * 128
                            STT(oc.t[:, m, lc:lc + 128], yz3[:, m, :], gn.t[:, m:m + 1], rs2.t, ALU.mult, ALU.mult, [yz.k, gn.k, rs2.k], [oc.k])
                    for g in range(2):
                        MM(psum[2 + g][:, :], Btok.t[:, g, :], xdd.t[:, 512 * g:512 * g + 512], True, True, [Btok.k, xdd.k], [k_ps[2 + g]], True)
                        sv = Sn.t[:, 512 * g:512 * g + 512]
                        TT(sv.rearrange("p (h d) -> p h d", h=8), sv.bitcast(F32).rearrange("p (h d) -> p h d", h=8),
                           cd[:, 8 * g:8 * g + 8].unsqueeze(2).to_broadcast([128, 8, 64]), ALU.mult, [Sn.k, tk_.k], [Sn.k])
                        TT(sv, sv.bitcast(F32), psum[2 + g][:, :], ALU.add, [Sn.k, k_ps[2 + g]], [Sn.k])
                    if smp:
                        DMA(Sn.d, o_ssms[l, b], Sn.t[:, 0:1024].bitcast(F32), [Sn.k], [k_out])
                    elif gc == 15:
                        DMA(Sn.d, o_ssmp[l], Sn.t[:, 0:1024].bitcast(F32), [Sn.k], [k_out])

                for c in range(8):
                    unit(8 * t + c, None)
                if t == 0:
                    for b in range(NSMP):
                        unit(None, b)

        def phase_merge(l, t, ncol, oa, ob, oc):
            w = Lw[l]
            blocks = []
            for fo in range(16):
                blocks += [(w["ing"][fo], 2048), (w["pa"][fo], 512), (w["ing"][16 + fo], 2048), (w["pb"][fo], 512),
                           (w["ing"][32 + fo], 2048), (w["pc"][fo], 1024)]
            ws = WS(blocks)
            bi = 0
            for fo in range(16):
                macc = T[fo % 2]
                for br, (ob_, kc) in enumerate(((oa, 4), (ob, 4), (oc, 8))):
                    s_ = ws.get(bi); bi += 1
                    wv = wr.t[:, s_, :].rearrange("p (k n) -> p k n", k=16)
                    res = proj_group(lambda k: (wv[:, k, :], [k_wr[s_]]), lambda k, c0, cn: (hbuf.t[:, k, c0:c0 + cn], [k_h[k]]), 16, ncol)
                    for (ps_, c0, cn, tk) in res:
                        ACT(T[2].t[:, c0:c0 + cn], ps_[:, 0:cn], AF.Sigmoid, [tk], [T[2].k])
                    s2 = ws.get(bi); bi += 1
                    wv2 = wr.t[:, s2, 0:kc * 128].rearrange("p (k n) -> p k n", k=kc)
                    res = proj_group(lambda k: (wv2[:, k, :], [k_wr[s2]]), lambda k, c0, cn: (ob_.t[:, k, c0:c0 + cn], [ob_.k]), kc, ncol)
                    for (ps_, c0, cn, tk) in res:
                        if br == 0:
                            TT(macc.t[:, c0:c0 + cn], ps_[:, 0:cn], T[2].t[:, c0:c0 + cn], ALU.mult, [tk, T[2].k], [macc.k])
                        else:
                            TT(T[3].t[:, c0:c0 + cn], ps_[:, 0:cn], T[2].t[:, c0:c0 + cn], ALU.mult, [tk, T[2].k], [T[3].k])
                            TT(macc.t[:, c0:c0 + cn], macc.t[:, c0:c0 + cn], T[3].t[:, c0:c0 + cn], ALU.add, [macc.k, T[3].k], [macc.k])
                DMA(macc.d, mrg[fo, :, 0:ncol], macc.t[:, 0:ncol].bitcast(FR), [macc.k], [k_mrg[fo]])

        def resid_update(l, t, ncol, fo, res, src, gtoff):
            xin = T[fo % 2]; xo = T[2 + fo % 2]
            DMA(xin.d, xin.t[:, 0:ncol], src[:, fo, 0:ncol], [k_xres[t][fo]], [xin.k])
            for (ps_, c0, cn, tk) in res:
                if c0 < NP:
                    STT(xo.t[:, c0:c0 + cn], ps_[:, 0:cn], modT.t[:, gtoff + fo, 0:1], xin.t[:, c0:c0 + cn], ALU.mult, ALU.add, [tk, modT.k, xin.k], [xo.k])
                else:
                    TT(xo.t[:, c0:c0 + cn], ps_[:, 0:cn], modT.t[:, gtoff + fo, 1:17], ALU.mult, [tk, modT.k], [xo.k])
                    TT(xo.t[:, c0:c0 + cn], xo.t[:, c0:c0 + cn], xin.t[:, c0:c0 + cn], ALU.add, [xo.k, xin.k], [xo.k])
            DMA(xo.d, xres[t][:, fo, 0:ncol], xo.t[:, 0:ncol], [xo.k], [k_xres[t][fo]])

        def phase_wout(l, t, ncol, src):
            w = Lw[l]
            for k in range(16):
                DMA(P.dsem("hld"), hbuf.t[:, k, 0:ncol], mrg[k, :, 0:ncol], [k_mrg[k]], [k_h[k]])
            ws = WS([(w["out"][fo], 2048) for fo in range(16)])
            for fo in range(16):
                s_ = ws.get(fo)
                wv = wr.t[:, s_, :].rearrange("p (k n) -> p k n", k=16)
                res = proj_group(lambda k: (wv[:, k, :], [k_wr[s_]]), lambda k, c0, cn: (hbuf.t[:, k, c0:c0 + cn], [k_h[k]]), 16, ncol)
                resid_update(l, t, ncol, fo, res, src, 32)

        def phase_ffn(l, t, ncol):
            w = Lw[l]
            with ExitStack() as s:
                acc = sbuf(s, "facc", [128, 16, NP + NSMP]); hid = sbuf(s, "fhid", [128, 2, NP + NSMP], FR)
                fcw = sbuf(s, "fcwS", [128, 88, 3]); fcb = sbuf(s, "fcbS", [128, 88])
                fb = sbuf(s, "fbS", [128, NSMP, 2]); nb = sbuf(s, "nbS", [128, NSMP, 2])
                DMA(fcw.d, fcw.t[:], w["fcw"], [], [fcw.k]); DMA(fcb.d, fcb.t[:], w["fcb"], [], [fcb.k])
                if t == 0:
                    MS(fhalo.t[:], 0.0, [fhalo.k])
                blocks = []
                import os
                _ngb = int(os.environ.get("FFN_NG", "22")) if (l == 1 and t == 0) else 22
                for grp in range(_ngb):
                    for ff in range(2):
                        blocks += [(w["up"][2 * grp + ff], 2048), (w["up"][44 + 2 * grp + ff], 2048)]
                    blocks += [(w["down"][2 * grp + q], 2048) for q in range(2)]
                ws = WS(blocks)
                bi = 0

                def conv_evac(q, res, ext, yo):
                    CP(ext.t[:, 0:2], fhalo.t[:, q, :], [fhalo.k], [ext.k])
                    for (ps_, c0, cn, tk) in res:
                        ACT(ext.t[:, 2 + c0:2 + c0 + cn], ps_[:, 0:cn], AF.Copy, [tk], [ext.k])
                    TS(yo.t[:, 0:NP], ext.t[:, 2:2 + NP], fcw.t[:, q, 2:3], fcb.t[:, q:q + 1], ALU.mult, ALU.add, [ext.k, fcw.k, fcb.k], [yo.k])
                    STT(yo.t[:, 0:NP], ext.t[:, 1:1 + NP], fcw.t[:, q, 1:2], yo.t[:, 0:NP], ALU.mult, ALU.add, [ext.k, fcw.k, yo.k], [yo.k])
                    STT(yo.t[:, 0:NP], ext.t[:, 0:NP], fcw.t[:, q, 0:1], yo.t[:, 0:NP], ALU.mult, ALU.add, [ext.k, fcw.k, yo.k], [yo.k])
                    CP(fhalo.t[:, q, :], ext.t[:, NP:NP + 2], [ext.k], [fhalo.k])
                    if t == 1:
                        pass
                    if ncol > NP:
                        xs_ = ext.t[:, 2 + NP:2 + ncol]
                        DMA(fb.d, fb.t[:], fbuf0[l][:, q, :, :], [], [fb.k])
                        TS(yo.t[:, NP:ncol], xs_, fcw.t[:, q, 2:3], fcb.t[:, q:q + 1], ALU.mult, ALU.add, [ext.k, fcw.k, fcb.k], [yo.k])
                        STT(yo.t[:, NP:ncol], fb.t[:, :, 1], fcw.t[:, q, 1:2], yo.t[:, NP:ncol], ALU.mult, ALU.add, [fb.k, fcw.k, yo.k], [yo.k])
                        STT(yo.t[:, NP:ncol], fb.t[:, :, 0], fcw.t[:, q, 0:1], yo.t[:, NP:ncol], ALU.mult, ALU.add, [fb.k, fcw.k, yo.k], [yo.k])
                        CP(nb.t[:, :, 0], fb.t[:, :, 1], [fb.k], [nb.k])
                        CP(nb.t[:, :, 1], xs_, [ext.k], [nb.k])
                        DMA(nb.d, o_cfs[l][:, q, :, :], nb.t[:], [nb.k], [k_out])

                import os
                _ng = int(os.environ.get("FFN_NG", "22")) if (l == 1 and t == 0) else 22
                for grp in range(_ng):
                    if os.environ.get("FFN_DBG") and l == 1 and t == 0 and grp >= 17:
                        print("FFNDBG grp", grp, dict(P.cnt), {k_: v_[1] for k_, v_ in P.dsems.items() if v_[1] > 2000},
                              {e_: len(v_) + sum(len(w_[0]) for w_ in v_) for e_, v_ in P.streams.items()})
                    for ff in range(2):
                        f = 2 * grp + ff
                        ys = []
                        for part in range(2):
                            q = f + 44 * part
                            s_ = ws.get(bi); bi += 1
                            wv = wr.t[:, s_, :].rearrange("p (k n) -> p k n", k=16)
                            res = proj_group(lambda k: (wv[:, k, :], [k_wr[s_]]), lambda k, c0, cn: (hbuf.t[:, k, c0:c0 + cn], [k_h[k]]), 16, ncol)
                            ext = T[0 + part]; yo = T[2 + part]
                            conv_evac(q, res, ext, yo)
                            ys.append(yo)
                        ACT(T[4].t[:, 0:ncol], ys[0].t[:, 0:ncol], AF.Silu, [ys[0].k], [T[4].k])
                        TT(hid.t[:, ff, 0:ncol], T[4].t[:, 0:ncol], ys[1].t[:, 0:ncol], ALU.mult, [T[4].k, ys[1].k], [hid.k])
                    for q4 in range(2):
                        s_ = ws.get(bi); bi += 1
                        wv = wr.t[:, s_, :].rearrange("p (k n) -> p k n", k=2)
                        for fl in range(8):
                            fo = 8 * q4 + fl
                            res = proj_group(lambda k: (wv[:, k, 128 * fl:128 * fl + 128], [k_wr[s_]]), lambda k, c0, cn: (hid.t[:, k, c0:c0 + cn], [hid.k]), 2, ncol)
                            for (ps_, c0, cn, tk) in res:
                                if grp == 0:
                                    ACT(acc.t[:, fo, c0:c0 + cn], ps_[:, 0:cn], AF.Copy, [tk], [acc.k])
                                else:
                                    TT(acc.t[:, fo, c0:c0 + cn], acc.t[:, fo, c0:c0 + cn], ps_[:, 0:cn], ALU.add, [tk, acc.k], [acc.k])
                if t == 1:
                    DMA(fhalo.d, o_cfp[l], fhalo.t[:], [fhalo.k], [k_out])
                for fo in range(16):
                    xin = T[fo % 2]; xo = T[2 + fo % 2]
                    DMA(xin.d, xin.t[:, 0:ncol], xres[t][:, fo, 0:ncol], [k_xres[t][fo]], [xin.k])
                    STT(xo.t[:, 0:NP], acc.t[:, fo, 0:NP], modT.t[:, 80 + fo, 0:1], xin.t[:, 0:NP], ALU.mult, ALU.add, [acc.k, modT.k, xin.k], [xo.k])
                    if ncol > NP:
                        TT(xo.t[:, NP:ncol], acc.t[:, fo, NP:ncol], modT.t[:, 80 + fo, 1:17], ALU.mult, [acc.k, modT.k], [xo.k])
                        TT(xo.t[:, NP:ncol], xo.t[:, NP:ncol], xin.t[:, NP:ncol], ALU.add, [xo.k, xin.k], [xo.k])
                    DMA(xo.d, xres[t][:, fo, 0:ncol], xo.t[:, 0:ncol], [xo.k], [k_xres[t][fo]])

        class _Stop(Exception):
            pass
        nph = [0]

        def _wrap(fn):
            def g(*a, **k):
                if stop_after is not None and nph[0] >= stop_after:
                    return None
                nph[0] += 1
                return fn(*a, **k)
            return g
        phase_mod, phase_norm, phase_inproj, phase_s5, phase_gmlp, phase_ssd, phase_merge, phase_wout, phase_ffn = [
            _wrap(f_) for f_ in (phase_mod, phase_norm, phase_inproj, phase_s5, phase_gmlp, phase_ssd, phase_merge, phase_wout, phase_ffn)]
        def _program():
            for l in range(DEPTH):
                cur_l[0] = l
                phase_mod(l)
                for t in range(2):
                    ncol = NP + NSMP if t == 0 else NP
                    src = xT[t] if l == 0 else xres[t]
                    phase_norm(src, k_xres[t], ncol, G1, 0)
                    phase_inproj(l, t, ncol)
                    P.barrier()
                    with ExitStack() as so:
                        oa = sbuf(so, "oa", [128, 4, NP + NSMP], FR)
                        phase_s5(l, t, ncol, oa)
                        P.barrier()
                        ob = sbuf(so, "ob", [128, 4, NP + NSMP], FR)
                        phase_gmlp(l, t, ncol, ob)
                        P.barrier()
                        oc = sbuf(so, "oc", [128, 8, NP + NSMP], FR)
                        phase_ssd(l, t, ncol, oc)
                        P.barrier()
                        phase_merge(l, t, ncol, oa, ob, oc)
                        P.barrier()
                    phase_wout(l, t, ncol, src)
                    phase_norm(xres[t], k_xres[t], ncol, G2, 48)
                    phase_ffn(l, t, ncol)
                    P.barrier()
                    if l == DEPTH - 1:
                        phase_norm(xres[t], k_xres[t], ncol, dst=o_y[t])

        try:
            _program()
        except _Stop:
            pass
        P.barrier()
        with nc.allow_non_contiguous_dma(reason="single-column gathers for padded sample chunks"), nc.Block() as block:
            P.emit(block)
        return nc

_NC = None


def _tile_w(W, kc, nw):
    K, N = W.shape
    assert K == kc * 128 and N % nw == 0
    return np.ascontiguousarray(W.reshape(kc, 128, N // nw, nw).transpose(2, 1, 0, 3).reshape(N // nw, 128, kc * nw))


def _fm(v):
    n = v.shape[0] // 128
    return np.ascontiguousarray(v.reshape((n, 128) + v.shape[1:]).swapaxes(0, 1))


def _prep_shared(inp):
    f = np.float32
    sh = {}
    sh["g_final"] = _fm(inp["g_final"])
    sh["ident"] = np.eye(128, dtype=f)
    sh["ones"] = np.ones((128, 128), f)
    jj, ii = np.meshgrid(np.arange(128), np.arange(128), indexing="ij")
    sh["negmaskT"] = np.where(ii >= jj, 0.0, -30000.0).astype(f)
    sh["trilT"] = (ii >= jj).astype(f)
    sel = np.zeros((16, 16, 128), f)
    for h in range(16):
        sel[h, h, :] = 1.0
    sh["sel16"] = sel
    sx = np.zeros((16, 8, 128), f)
    for m in range(8):
        sx[2 * m, m, 0:64] = 1.0
        sx[2 * m + 1, m, 64:128] = 1.0
    sh["selexp"] = sx
    sh["iota"] = np.tile(np.arange(128, dtype=f)[None, :], (128, 1))
    for l in range(DEPTH):
        sh[f"w_mod{l}"] = _tile_w(inp["w_mod"][l], 16, 128)
        sh[f"b_mod{l}"] = _fm(inp["b_mod"][l])
        sh[f"g_mix{l}"] = _fm(inp["g_mix"][l]); sh[f"g_ffn{l}"] = _fm(inp["g_ffn"][l])
        win = inp["w_in"][l]
        sh[f"w_inb{l}"] = _tile_w(win[:, 0:4096], 16, 128)
        sh[f"w_indt{l}"] = np.ascontiguousarray(win[:, 4096:4112].reshape(16, 128, 16).transpose(1, 0, 2).reshape(128, 256))
        sh[f"w_ing{l}"] = _tile_w(win[:, 4112:], 16, 128)
        sh[f"w_pa{l}"] = _tile_w(inp["w_pa"][l], 4, 128)
        sh[f"w_pb{l}"] = _tile_w(inp["w_pb"][l], 4, 128)
        sh[f"w_pc{l}"] = _tile_w(inp["w_pc"][l], 8, 128)
        sh[f"w_out{l}"] = _tile_w(inp["w_out"][l], 16, 128)
        sh[f"w_up{l}"] = _tile_w(inp["ffn_w_up"][l], 16, 128)
        wd = inp["ffn_w_down"][l]
        sh[f"w_down{l}"] = np.ascontiguousarray(
            wd.reshape(22, 2, 128, 2, 1024).transpose(0, 3, 2, 1, 4).reshape(44, 128, 2048))
        sh[f"w_glu{l}"] = np.ascontiguousarray(inp["s5_w_glu"][l].reshape(4, 128, 512).transpose(1, 0, 2).reshape(128, 2048))
        sh[f"lamr{l}"] = _fm(inp["s5_lam_re"][l].reshape(2048)); sh[f"lami{l}"] = _fm(inp["s5_lam_im"][l].reshape(2048))
        sh[f"logdt{l}"] = _fm(np.repeat(inp["s5_log_dt"][l], 64))
        Bre = np.zeros((128, 16, 128), f); Bim = np.zeros((128, 16, 128), f)
        Cre = np.zeros((128, 16, 128), f); Cim = np.zeros((128, 16, 128), f)
        for m in range(16):
            for gg in range(2):
                g = 2 * m + gg
                r0 = (g % 8) * 16
                Bre[r0:r0 + 16, m, gg * 64:(gg + 1) * 64] = inp["s5_b_re"][l, g].T
                Bim[r0:r0 + 16, m, gg * 64:(gg + 1) * 64] = inp["s5_b_im"][l, g].T
                Cre[gg * 64:(gg + 1) * 64, m, r0:r0 + 16] = inp["s5_c_re"][l, g].T
                Cim[gg * 64:(gg + 1) * 64, m, r0:r0 + 16] = inp["s5_c_im"][l, g].T
        sh[f"Bre{l}"] = Bre; sh[f"Bim{l}"] = Bim; sh[f"Cre{l}"] = Cre; sh[f"Cim{l}"] = Cim
        sh[f"s5d{l}"] = _fm(inp["s5_d"][l])
        sh[f"gv{l}"] = _fm(inp["gm_g_v"][l])
        sh[f"wsT{l}"] = np.ascontiguousarray(inp["gm_w_s"][l].transpose(2, 0, 1))
        sh[f"bsx{l}"] = _fm(np.repeat(inp["gm_b_s"][l], 64, axis=0))
        sh[f"cw{l}"] = _fm(np.ascontiguousarray(inp["ssd_conv_w"][l].T))
        sh[f"cb{l}"] = _fm(inp["ssd_conv_b"][l])
        sh[f"dtb{l}"] = inp["ssd_dt_bias"][l].reshape(16, 1).copy(); sh[f"alog{l}"] = inp["ssd_a_log"][l].reshape(16, 1).copy()
        sh[f"ssdD{l}"] = _fm(np.repeat(inp["ssd_d"][l], 64)); sh[f"ssdg{l}"] = _fm(inp["ssd_g_norm"][l])
        sh[f"fcw{l}"] = _fm(np.ascontiguousarray(inp["ffn_conv_w"][l].T)); sh[f"fcb{l}"] = _fm(inp["ffn_conv_b"][l])
    return {k: np.ascontiguousarray(v, dtype=f) for k, v in sh.items()}


def _core_inputs(inp, sh, c):
    f = np.float32
    sq = c % 4
    rows = slice(16 * c, 16 * c + 16)
    m = dict(sh)
    xp = inp["x_prompt"][sq]
    xs = inp["x_sample"][rows, 0, :]
    x0 = np.concatenate([xp[0:NP], xs], axis=0)
    m["xT0"] = _fm(np.ascontiguousarray(x0.T)).astype(f)
    m["xT1"] = _fm(np.ascontiguousarray(xp[NP:2 * NP].T)).astype(f)
    cc = np.zeros((18, D), f)
    cc[0] = inp["c_prompt"][sq]; cc[1:17] = inp["c_sample"][rows]
    m["cT"] = _fm(np.ascontiguousarray(cc.T))
    m["s5r0"] = np.ascontiguousarray(inp["state_s5_re"][:, rows].reshape(DEPTH, 16, 16, 128).transpose(0, 1, 3, 2))
    m["s5i0"] = np.ascontiguousarray(inp["state_s5_im"][:, rows].reshape(DEPTH, 16, 16, 128).transpose(0, 1, 3, 2))
    m["ssm0"] = np.ascontiguousarray(inp["state_ssm"][:, rows].reshape(DEPTH, 16, 1024, 128).transpose(0, 1, 3, 2))
    m["cbuf0"] = np.ascontiguousarray(inp["state_ssd_conv"][:, rows].reshape(DEPTH, 16, 3, 12, 128).transpose(0, 4, 3, 1, 2))
    m["fbuf0"] = np.ascontiguousarray(inp["state_ffn_conv"][:, rows].reshape(DEPTH, 16, 2, 88, 128).transpose(0, 4, 3, 1, 2))
    return m


def _unfm(a):
    return a.swapaxes(0, 1).reshape((a.shape[0] * a.shape[1],) + a.shape[2:])


def _unpack_core(r):
    o = {}
    y0 = _unfm(r["o_y0"]).T
    o["y_s"] = y0[NP:]
    o["y_p"] = np.concatenate([y0[0:NP], _unfm(r["o_y1"]).T], axis=0)
    s5s = r["o_s5s"]
    o["s5r_s"] = s5s[:, :, 0].transpose(0, 1, 3, 2).reshape(DEPTH, 16, 32, 64)
    o["s5i_s"] = s5s[:, :, 1].transpose(0, 1, 3, 2).reshape(DEPTH, 16, 32, 64)
    o["ssm_s"] = r["o_ssms"].transpose(0, 1, 3, 2).reshape(DEPTH, 16, 16, 64, 128)
    o["cc_s"] = r["o_ccs"].transpose(0, 3, 4, 2, 1).reshape(DEPTH, 16, 3, 1536)
    o["cf_s"] = r["o_cfs"].transpose(0, 3, 4, 2, 1).reshape(DEPTH, 16, 2, 11264)
    o["gv_s"] = r["o_gv"].transpose(0, 3, 2, 1).reshape(DEPTH, 16, 512)
    s5p = r["o_s5p"]
    o["s5r_p"] = s5p[:, 0].transpose(0, 2, 1).reshape(DEPTH, 32, 64)
    o["s5i_p"] = s5p[:, 1].transpose(0, 2, 1).reshape(DEPTH, 32, 64)
    o["ssm_p"] = r["o_ssmp"].transpose(0, 2, 1).reshape(DEPTH, 16, 64, 128)
    o["cc_p"] = r["o_ccp"].transpose(0, 3, 2, 1).reshape(DEPTH, 3, 1536)
    o["cf_p"] = r["o_cfp"].transpose(0, 3, 2, 1).reshape(DEPTH, 2, 11264)
    return o


def kernel(**inp):
    global _NC
    inp = {k: np.asarray(v) for k, v in inp.items()}
    f = np.float32
    if _NC is None:
        _NC = build_program()
    nc = _NC
    sh = _prep_shared(inp)
    in_maps = [_core_inputs(inp, sh, c) for c in range(8)]
    res = run_bass_kernel_spmd(nc, in_maps, core_ids=list(range(8)))
    R = res.results
    y_p = np.zeros((4, 2048, D), f); y_s = np.zeros((128, 1, D), f)
    s5r_p = np.zeros((DEPTH, 4, 32, 64), f); s5i_p = np.zeros_like(s5r_p)
    ssm_p = np.zeros((DEPTH, 4, 16, 64, 128), f)
    cc_p = np.zeros((DEPTH, 4, 3, 1536), f); cf_p = np.zeros((DEPTH, 4, 2, 11264), f)
    s5r_s = np.zeros((DEPTH, 128, 32, 64), f); s5i_s = np.zeros_like(s5r_s)
    ssm_s = np.zeros((DEPTH, 128, 16, 64, 128), f)
    cc_s = np.zeros((DEPTH, 128, 3, 1536), f); cf_s = np.zeros((DEPTH, 128, 2, 11264), f)
    gv_s = np.zeros((DEPTH, 128, 1, 512), f)
    for c in range(8):
        o = _unpack_core(R[c])
        rows = slice(16 * c, 16 * c + 16)
        y_s[rows, 0, :] = o["y_s"]
        s5r_s[:, rows] = o["s5r_s"]; s5i_s[:, rows] = o["s5i_s"]; ssm_s[:, rows] = o["ssm_s"]
        cc_s[:, rows] = o["cc_s"]; cf_s[:, rows] = o["cf_s"]; gv_s[:, rows, 0, :] = o["gv_s"]
        if c < 4:
            y_p[c] = o["y_p"]
            s5r_p[:, c] = o["s5r_p"]; s5i_p[:, c] = o["s5i_p"]; ssm_p[:, c] = o["ssm_p"]
            cc_p[:, c] = o["cc_p"]; cf_p[:, c] = o["cf_p"]
    return (y_p, y_s, s5r_p, s5i_p, ssm_p, cc_p, cf_p, s5r_s, s5i_s, ssm_s, cc_s, cf_s, gv_s)
```

```python
import math
from contextlib import ExitStack
import numpy as np
import concourse.bass as bass
import concourse.mybir as mybir
from concourse.bass_utils import run_bass_kernel_spmd

F32 = mybir.dt.float32
FR = mybir.dt.float32r
AF = mybir.ActivationFunctionType
ALU = mybir.AluOpType

D = 2048; NP = 1024; NSMP = 16; DEPTH = 2
EPS = 1e-6
NBIN = 33
BIN_UA, BIN_UB, BIN_VB, BIN_Z, BIN_XBC, BIN_DT = 0, 4, 8, 12, 20, 32
SEQC = 2048 + NSMP
TW = 1048
PI = math.pi


class Tok:
    __slots__ = ("w", "r")

    def __init__(self):
        self.w = None
        self.r = []


class Buf:
    def __init__(self, t, P, name):
        self.t = t
        self.k = Tok()
        self._d = None
        self.P = P
        self.name = name

    @property
    def d(self):
        if self._d is None:
            self._d = self.P.dsem(self.name)
        return self._d


class Prog:
    def __init__(self, nc):
        self.nc = nc
        self.eng = {"pe": nc.tensor, "act": nc.scalar, "dve": nc.vector, "pool": nc.gpsimd, "sp": nc.sync}
        self.streams = {e: [] for e in self.eng}
        self.sem = {}
        self.cnt = {}
        self.seen = {e: {} for e in self.eng}
        self.dsems = {}
        self.pend = {e: ([], []) for e in self.eng}

    def open(self, stack):
        self.stack = stack
        for e in ("pe", "act", "dve", "pool"):
            self.sem[e] = stack.enter_context(self.nc.semaphore("s_" + e))
            self.cnt[e] = 0

    def dsem(self, name):
        if name not in self.dsems:
            s = self.stack.enter_context(self.nc.semaphore("d_" + name))
            self.dsems[name] = [s, 0, None, name]
        return self.dsems[name]

    def _waits(self, eng, R, W, extra=()):
        evs = list(extra)
        for t in R:
            if t.w is not None:
                evs.append(t.w)
        for t in W:
            if t.w is not None:
                evs.append(t.w)
            evs.extend(t.r)
        need = {}
        for (key, s, v, src) in evs:
            if src == "pe" and eng == "pe":
                continue
            if need.get(key, (None, 0))[1] < v:
                need[key] = (s, v)
        out = []
        seen = self.seen[eng]
        for key, (s, v) in need.items():
            if seen.get(key, 0) >= v:
                continue
            seen[key] = v
            out.append((s, v))
        return out

    def op(self, eng, fn, R=(), W=(), inc=True):
        waits = self._waits(eng, R, W)
        if inc:
            self.cnt[eng] += 1
            ev = (eng, self.sem[eng], self.cnt[eng], eng)
            pr, pw = self.pend[eng]
            self.pend[eng] = ([], [])
            for t in list(R) + pr:
                t.r.append(ev)
            for t in list(W) + pw:
                t.w = ev
                t.r = []
        else:
            self.pend[eng][0].extend(R)
            self.pend[eng][1].extend(W)
        self.streams[eng].append((waits, fn, (self.sem[eng], 1) if inc else None))

    def dma(self, q, ds, out, in_, R=(), W=()):
        extra = [ds[2]] if ds[2] is not None else []
        waits = self._waits(q, R, W, extra)
        ds[1] += 16
        ev = (ds[3], ds[0], ds[1], "dma")
        ds[2] = ev
        for t in R:
            t.r.append(ev)
        for t in W:
            t.w = ev
            t.r = []
        self.streams[q].append((waits, lambda e, o=out, i=in_: e.dma_start(out=o, in_=i), (ds[0], 16)))

    def barrier(self):
        evs = [(e, self.sem[e], self.cnt[e]) for e in self.sem if self.cnt[e] > 0]
        evs += [(d[3], d[0], d[1]) for d in self.dsems.values() if d[1] > 0]
        for e in self.eng:
            waits = []
            for key, s, v in evs:
                if self.seen[e].get(key, 0) < v:
                    self.seen[e][key] = v
                    waits.append((s, v))
            if waits:
                self.streams[e].append((waits, None, None))

    def emit(self, block):
        def mk(e):
            def body(engine):
                for waits, fn, inc in self.streams[e]:
                    for s, v in waits:
                        engine.wait_ge(s, v)
                    if fn is not None:
                        ins = fn(engine)
                        if inc is not None:
                            ins.then_inc(inc[0], inc[1])
            return body
        block.tensor(mk("pe"))
        block.scalar(mk("act"))
        block.vector(mk("dve"))
        block.gpsimd(mk("pool"))
        block.sync(mk("sp"))


def build_program(stop_after=None):
    nc = bass.Bass("TRN2", target_bir_lowering=False)
    nc.dge_precook = False
    P = Prog(nc)

    def din(name, shape, dt=F32):
        return nc.dram_tensor(name, list(shape), dt, kind="ExternalInput").ap()

    def dout(name, shape):
        return nc.dram_tensor(name, list(shape), F32, kind="ExternalOutput").ap()

    def dscr(name, shape, dt=F32):
        return nc.dram_tensor(name, list(shape), dt, kind="Internal").ap()

    xT = [din("xT0", [128, 16, NP + NSMP]), din("xT1", [128, 16, NP])]
    cT = din("cT", [128, 16, 18])
    g_final = din("g_final", [128, 16])
    c_ident = din("ident", [128, 128])
    c_ones = din("ones", [128, 128], FR)
    c_negmask = din("negmaskT", [128, 128])
    c_tril = din("trilT", [128, 128])
    c_sel16 = din("sel16", [16, 16, 128], FR)
    c_selexp = din("selexp", [16, 8, 128], FR)
    c_iota = din("iota", [128, 128])
    s5r0 = din("s5r0", [DEPTH, 128, 16, NSMP]); s5i0 = din("s5i0", [DEPTH, 128, 16, NSMP])
    ssm0 = din("ssm0", [DEPTH, NSMP, 128, 1024], FR)
    cbuf0 = din("cbuf0", [DEPTH, 128, 12, NSMP, 3])
    fbuf0 = din("fbuf0", [DEPTH, 128, 88, NSMP, 2])
    Lw = []
    for l in range(DEPTH):
        w = {}
        w["mod"] = din(f"w_mod{l}", [96, 128, 2048], FR)
        w["b_mod"] = din(f"b_mod{l}", [128, 96])
        w["g_mix"] = din(f"g_mix{l}", [128, 16]); w["g_ffn"] = din(f"g_ffn{l}", [128, 16])
        w["inb"] = din(f"w_inb{l}", [32, 128, 2048], FR)
        w["indt"] = din(f"w_indt{l}", [128, 256], FR)
        w["ing"] = din(f"w_ing{l}", [48, 128, 2048], FR)
        w["pa"] = din(f"w_pa{l}", [16, 128, 512], FR)
        w["pb"] = din(f"w_pb{l}", [16, 128, 512], FR)
        w["pc"] = din(f"w_pc{l}", [16, 128, 1024], FR)
        w["out"] = din(f"w_out{l}", [16, 128, 2048], FR)
        w["up"] = din(f"w_up{l}", [88, 128, 2048], FR)
        w["down"] = din(f"w_down{l}", [44, 128, 2048], FR)
        w["glu"] = din(f"w_glu{l}", [128, 2048], FR)
        w["lamr"] = din(f"lamr{l}", [128, 16]); w["lami"] = din(f"lami{l}", [128, 16])
        w["logdt"] = din(f"logdt{l}", [128, 16])
        w["Bre"] = din(f"Bre{l}", [128, 16, 128]); w["Bim"] = din(f"Bim{l}", [128, 16, 128])
        w["Cre"] = din(f"Cre{l}", [128, 16, 128], FR); w["Cim"] = din(f"Cim{l}", [128, 16, 128], FR)
        w["s5d"] = din(f"s5d{l}", [128, 4])
        w["gv"] = din(f"gv{l}", [128, 4]); w["w00"] = din(f"w00{l}", [128, 4])
        w["wsT"] = din(f"wsT{l}", [128, 8, 128]); w["bsx"] = din(f"bsx{l}", [128, 4, 128])
        w["cw"] = din(f"cw{l}", [128, 12, 4]); w["cb"] = din(f"cb{l}", [128, 12])
        w["dtb"] = din(f"dtb{l}", [16, 1]); w["alog"] = din(f"alog{l}", [16, 1])
        w["ssdD"] = din(f"ssdD{l}", [128, 8]); w["ssdg"] = din(f"ssdg{l}", [128, 8])
        w["fcw"] = din(f"fcw{l}", [128, 88, 3]); w["fcb"] = din(f"fcb{l}", [128, 88])
        Lw.append(w)
    o_y = [dout("o_y0", [128, 16, NP + NSMP]), dout("o_y1", [128, 16, NP])]
    o_s5p = dout("o_s5p", [DEPTH, 2, 128, 16])
    o_s5s = dout("o_s5s", [DEPTH, 2, 128, 16, NSMP])
    o_ssmp = dout("o_ssmp", [DEPTH, 128, 1024])
    o_ssms = dout("o_ssms", [DEPTH, NSMP, 128, 1024])
    o_ccp = dout("o_ccp", [DEPTH, 128, 12, 3])
    o_ccs = dout("o_ccs", [DEPTH, 128, 12, NSMP, 3])
    o_cfp = dout("o_cfp", [DEPTH, 128, 88, 2])
    o_cfs = dout("o_cfs", [DEPTH, 128, 88, NSMP, 2])
    o_gv = dout("o_gv", [DEPTH, 128, 4, NSMP])
    xres = [dscr("xres0", [128, 16, NP + NSMP]), dscr("xres1", [128, 16, NP])]
    binb = dscr("bin", [NBIN, 128, SEQC], FR)
    binF = binb.bitcast(F32)
    mrg = dscr("mrg", [16, 128, NP + NSMP], FR)
    k_xres = [[Tok() for _ in range(16)] for _ in range(2)]
    k_bin = [Tok() for _ in range(NBIN)]
    k_mrg = [Tok() for _ in range(16)]
    k_out = Tok()
    k_in = Tok()

    with ExitStack() as st:
        P.open(st)

        uid = [0]

        def sbuf(stack, name, shape, dt=F32):
            uid[0] += 1
            return Buf(stack.enter_context(nc.sbuf_tensor(f"{name}_{uid[0]}", list(shape), dt)), P, name)

        def ACT(out, in_, func, R, W, **kw):
            P.op("act", lambda e: e.activation(out=out, in_=in_, func=func, **kw), R, W)

        def TT(out, a, b, op, R, W, eng="dve"):
            P.op(eng, lambda e: e.tensor_tensor(out, a, b, op), R, W)

        def TS(out, a, s1, s2, op0, op1, R, W):
            if op1 is None:
                P.op("dve", lambda e: e.tensor_scalar(out, a, s1, None, op0), R, W)
            else:
                P.op("dve", lambda e: e.tensor_scalar(out, a, s1, s2, op0, op1), R, W)

        def STT(out, a, s, b, op0, op1, R, W):
            P.op("dve", lambda e: e.scalar_tensor_tensor(out, a, s, b, op0, op1), R, W)

        def CP(out, in_, R, W, eng="dve"):
            P.op(eng, lambda e: e.tensor_copy(out, in_), R, W)

        def MS(ap, val, W, eng="dve"):
            P.op(eng, lambda e: e.memset(ap, val), (), W)

        def MM(out, lhsT, rhs, start, stop, R, W, inc):
            P.op("pe", lambda e: e.matmul(out, lhsT, rhs, start=start, stop=stop), R, W, inc)

        def TR(out, in_, idn, R, W):
            P.op("pe", lambda e: e.transpose(out, in_, idn), R, W)

        def DMA(ds, out, in_, R, W, q="pool"):
            P.dma(q, ds, out, in_, R, W)

        hbuf = sbuf(st, "hbuf", [128, 16, NP + NSMP], FR); k_h = [Tok() for _ in range(16)]
        wr = sbuf(st, "wring", [128, 2, 2048], FR); k_wr = [Tok(), Tok()]
        cur_l = [0]

        def d_wr_(s_):
            return P.dsem(f"wr{s_}_{cur_l[0]}")
        modT = sbuf(st, "modT", [128, 96, 18])
        G1 = sbuf(st, "G1", [128, 16, 18]); G2 = sbuf(st, "G2", [128, 16, 18])
        ident = sbuf(st, "identS", [128, 128]); ones = sbuf(st, "onesS", [128, 128], FR)
        silc = sbuf(st, "silc", [128, 16, 18], FR)
        bmod = sbuf(st, "bmod", [128, 96]); gmix = sbuf(st, "gmixS", [128, 16]); gffn = sbuf(st, "gffnS", [128, 16])
        gfin = sbuf(st, "gfin", [128, 16])
        T = [sbuf(st, f"T{i}", [128, TW]) for i in range(5)]
        RB = [sbuf(st, f"R{i}", [128, TW], FR) for i in range(2)]
        Snp = sbuf(st, "Snp", [128, 1024], FR)
        hlp = sbuf(st, "hlp", [128, 2, 16])
        fhalo = sbuf(st, "fhalo", [128, 88, 2])
        psum = [st.enter_context(nc.psum_tensor(f"ps{i}", [128, 512], F32)) for i in range(8)]
        k_ps = [Tok() for _ in range(8)]
        onesF = ones.t[:].bitcast(F32)
        zer = sbuf(st, "zer", [128, 512])
        MS(zer.t[:], 0.0, [zer.k])

        DMA(ident.d, ident.t[:], c_ident, [], [ident.k])
        DMA(ones.d, ones.t[:], c_ones, [], [ones.k])
        DMA(gfin.d, gfin.t[:], g_final, [], [gfin.k])

        def nchunks(ncol):
            r = [(0, 512), (512, 512)]
            if ncol > 1024:
                r.append((1024, ncol - 1024))
            return r

        class WS:
            nxt = 0

            def __init__(self, blocks):
                self.blocks = blocks
                self.loaded = 0
                self.slot0 = WS.nxt

            def get(self, i):
                while self.loaded < min(len(self.blocks), i + 2):
                    j = self.loaded
                    s = (self.slot0 + j) % 2
                    ap, n = self.blocks[j]
                    P.dma("sp", d_wr_(s), wr.t[:, s, 0:n], ap, [], [k_wr[s]])
                    self.loaded += 1
                s = (self.slot0 + i) % 2
                if i == len(self.blocks) - 1:
                    WS.nxt = (s + 1) % 2
                return s

        pset = [0]

        def proj_group(lhs_fn, rhs_fn, KC, ncol, M=128):
            base = 3 * pset[0]; pset[0] ^= 1
            chs = nchunks(ncol)
            for k in range(KC):
                lt, lR = lhs_fn(k)
                for i, (c0, cn) in enumerate(chs):
                    rt, rR = rhs_fn(k, c0, cn)
                    MM(psum[base + i][0:M, 0:cn], lt, rt, k == 0, k == KC - 1, list(lR) + list(rR), [k_ps[base + i]], k == KC - 1)
            return [(psum[base + i], c0, cn, k_ps[base + i]) for i, (c0, cn) in enumerate(chs)]

        def gelu_from(src, srcR, out, outW, tA, tB):
            n = src.shape[-1] if len(src.shape) == 2 else None
            a = tA.t[0:src.shape[0], 0:src.shape[1]]; b = tB.t[0:src.shape[0], 0:src.shape[1]]
            ACT(a, src, AF.Square, srcR, [tA.k])
            TS(a, a, 0.044715, 1.0, ALU.mult, ALU.add, [tA.k], [tA.k])
            TT(a, a, src, ALU.mult, [tA.k] + srcR, [tA.k])
            ACT(b, a, AF.Sigmoid, [tA.k], [tB.k], scale=1.5957691216)
            TT(out, b, src, ALU.mult, [tB.k] + srcR, outW)

        def phase_mod(l):
            w = Lw[l]
            DMA(modT.d, modT.t[:, 0:16, :], cT, [], [modT.k])
            ACT(silc.t[:], modT.t[:, 0:16, :], AF.Silu, [modT.k], [silc.k])
            DMA(bmod.d, bmod.t[:], w["b_mod"], [], [bmod.k])
            ws = WS([(w["mod"][b], 2048) for b in range(96)])
            for fo in range(96):
                s = ws.get(fo)
                wv = wr.t[:, s, :].rearrange("p (k n) -> p k n", k=16)
                base = 6 + (fo % 2)
                for k in range(16):
                    MM(psum[base][:, 0:18], wv[:, k, :], silc.t[:, k, :], k == 0, k == 15, [k_wr[s], silc.k], [k_ps[base]], k == 15)
                ACT(modT.t[:, fo, :], psum[base][:, 0:18], AF.Identity, [k_ps[base], bmod.k], [modT.k], bias=bmod.t[:, fo:fo + 1], scale=1.0)
            DMA(gmix.d, gmix.t[:], w["g_mix"], [], [gmix.k])
            DMA(gffn.d, gffn.t[:], w["g_ffn"], [], [gffn.k])
            for (Gt, off, g) in ((G1, 16, gmix), (G2, 64, gffn)):
                TS(Gt.t[:], modT.t[:, off:off + 16, :], 1.0, None, ALU.add, None, [modT.k], [Gt.k])
                TT(Gt.t[:], Gt.t[:], g.t[:].unsqueeze(2).to_broadcast([128, 16, 18]), ALU.mult, [Gt.k, g.k], [Gt.k])

        def phase_norm(src, src_k, ncol, Gt=None, SHoff=0, dst=None):
            chs = nchunks(ncol)
            rstd = T[4]; sq = RB[0]; tA = T[2]
            for k in range(16):
                s_ = T[k % 2]
                DMA(s_.d, s_.t[:, 0:ncol], src[:, k, 0:ncol], [src_k[k]], [s_.k])
                ACT(sq.t[:, 0:ncol], s_.t[:, 0:ncol], AF.Square, [s_.k], [sq.k])
                for j, (c0, cn) in enumerate(chs):
                    MM(psum[j][:, 0:cn], ones.t[:], sq.t[:, c0:c0 + cn], k == 0, k == 15, [sq.k, ones.k], [k_ps[j]], True)
            for j, (c0, cn) in enumerate(chs):
                TS(rstd.t[:, c0:c0 + cn], psum[j][:, 0:cn], 1.0 / D, EPS, ALU.mult, ALU.add, [k_ps[j]], [rstd.k])
            ACT(rstd.t[:, 0:ncol], rstd.t[:, 0:ncol], AF.Sqrt, [rstd.k], [rstd.k])
            P.op("dve", lambda e: e.reciprocal(rstd.t[:, 0:ncol], rstd.t[:, 0:ncol]), [rstd.k], [rstd.k])
            for k in range(16):
                s_ = T[k % 2]
                DMA(s_.d, s_.t[:, 0:ncol], src[:, k, 0:ncol], [src_k[k]], [s_.k])
                TT(tA.t[:, 0:ncol], s_.t[:, 0:ncol], rstd.t[:, 0:ncol], ALU.mult, [s_.k, rstd.k], [tA.k])
                if dst is None:
                    ACT(hbuf.t[:, k, 0:NP], tA.t[:, 0:NP], AF.Identity, [tA.k, Gt.k, modT.k], [k_h[k]],
                        scale=Gt.t[:, k, 0:1], bias=modT.t[:, SHoff + k, 0:1])
                    if ncol > NP:
                        TT(tA.t[:, NP:ncol], tA.t[:, NP:ncol], Gt.t[:, k, 1:17], ALU.mult, [tA.k, Gt.k], [tA.k])
                        TT(hbuf.t[:, k, NP:ncol], tA.t[:, NP:ncol], modT.t[:, SHoff + k, 1:17], ALU.add, [tA.k, modT.k], [k_h[k]])
                else:
                    o_ = T[3]
                    ACT(o_.t[:, 0:ncol], tA.t[:, 0:ncol], AF.Identity, [tA.k, gfin.k], [o_.k], scale=gfin.t[:, k:k + 1])
                    DMA(o_.d, dst[:, k, 0:ncol], o_.t[:, 0:ncol], [o_.k], [k_out])

        def phase_inproj(l, t, ncol):
            w = Lw[l]
            gcol = NP * t
            ws = WS([(w["inb"][b], 2048) for b in range(32)] + [(w["indt"], 256)])
            sti = [0]

            def evac(fo, res, M=128):
                for (ps_, c0, cn, tk) in res:
                    s_ = T[sti[0]]; sti[0] ^= 1
                    src = ps_[0:M, 0:cn]; o = s_.t[0:M, 0:cn]
                    if BIN_UB <= fo < BIN_Z:
                        gelu_from(src, [tk], o, [s_.k], T[2], T[3])
                    elif BIN_Z <= fo < BIN_XBC:
                        ACT(o, src, AF.Silu, [tk], [s_.k])
                    else:
                        ACT(o, src, AF.Copy, [tk], [s_.k])
                    dcol = gcol + c0 if c0 < NP else 2048
                    DMA(s_.d, binb[fo, 0:M, dcol:dcol + cn], o.bitcast(FR), [s_.k], [k_bin[fo]])

            for b in range(33):
                s = ws.get(b)
                if b < 32:
                    wv = wr.t[:, s, :].rearrange("p (k n) -> p k n", k=16)
                    res = proj_group(lambda k: (wv[:, k, :], [k_wr[s]]),
                                     lambda k, c0, cn: (hbuf.t[:, k, c0:c0 + cn], [k_h[k]]), 16, ncol)
                    evac(b, res)
                else:
                    wv = wr.t[:, s, 0:256].rearrange("p (k n) -> p k n", k=16)
                    res = proj_group(lambda k: (wv[:, k, :], [k_wr[s]]),
                                     lambda k, c0, cn: (hbuf.t[:, k, c0:c0 + cn], [k_h[k]]), 16, ncol, M=16)
                    evac(BIN_DT, res, M=16)

        def phase_s5(l, t, ncol, oa):
            w = Lw[l]
            with ExitStack() as s:
                cosT = sbuf(s, "cosT", [128, 16, 128]); sinT = sbuf(s, "sinT", [128, 16, 128])
                rtab = sbuf(s, "rtab", [128, 16, 128])
                Bbr = sbuf(s, "Bbr", [128, 16, 128], FR); Bbi = sbuf(s, "Bbi", [128, 16, 128], FR)
                Cre = sbuf(s, "CreS", [128, 16, 128], FR); Cim = sbuf(s, "CimS", [128, 16, 128], FR)
                prm = sbuf(s, "s5prm", [128, 16, 16])
                dg = sbuf(s, "s5dg", [128, 2, 128], FR)
                class _W:
                    def __init__(self, b_, ap):
                        self.t = ap; self.k = b_.k; self.d = b_.d
                bl = _W(T[2], T[2].t[:, 0:256].rearrange("p (g i) -> p g i", g=2))
                bt = _W(T[3], T[3].t[:, 0:256].rearrange("p (g i) -> p g i", g=2))
                u = sbuf(s, "s5u", [128, 4, 128], FR)
                hl = sbuf(s, "s5hl", [128, 2, 16]); car = sbuf(s, "s5car", [128, 2, 16]); ct = sbuf(s, "s5ct", [128, 4, 16])
                s5d = sbuf(s, "s5dS", [128, 4]); iot = sbuf(s, "iotS", [128, 128])
                pk = prm.k
                LR, LI, LD, DT_, LRD, TH, RR, AR, AI, DEN, NR, KR, KI, X1, X2 = [prm.t[:, i, :] for i in range(15)]
                DMA(prm.d, LR, w["lamr"], [], [pk]); DMA(prm.d, LI, w["lami"], [], [pk]); DMA(prm.d, LD, w["logdt"], [], [pk])
                DMA(Cre.d, Cre.t[:], w["Cre"], [], [Cre.k]); DMA(Cim.d, Cim.t[:], w["Cim"], [], [Cim.k])
                DMA(s5d.d, s5d.t[:], w["s5d"], [], [s5d.k])
                DMA(iot.d, iot.t[:], c_iota, [], [iot.k])
                ACT(DT_, LD, AF.Exp, [pk], [pk])
                TT(LRD, LR, DT_, ALU.mult, [pk], [pk]); TT(TH, LI, DT_, ALU.mult, [pk], [pk])
                ACT(RR, LRD, AF.Exp, [pk], [pk])
                ki32 = sbuf(s, "s5ki", [128, 128], mybir.dt.int32)
                hpi = sbuf(s, "s5hpi", [128, 1])
                MS(hpi.t[:], 0.5 * PI, [hpi.k])
                for m in range(16):
                    a_ = T[0].t[:, 0:128]; b_ = T[1].t[:, 0:128]; s_ = T[0].t[:, 128:256]; c_ = T[1].t[:, 128:256]
                    TS(a_, iot.t[:], TH[:, m:m + 1], None, ALU.mult, None, [iot.k, pk], [T[0].k])
                    TS(b_, a_, 1.0 / (2 * PI), None, ALU.mult, None, [T[0].k], [T[1].k])
                    CP(ki32.t[:], b_, [T[1].k], [ki32.k])
                    CP(b_, ki32.t[:], [ki32.k], [T[1].k])
                    STT(b_, b_, -2.0 * PI, a_, ALU.mult, ALU.add, [T[1].k, T[0].k], [T[1].k])
                    ACT(s_, b_, AF.Sin, [T[1].k], [T[0].k], scale=0.5)
                    ACT(c_, b_, AF.Sin, [T[1].k, hpi.k], [T[1].k], scale=-0.5, bias=hpi.t[:, 0:1])
                    STT(sinT.t[:, m, :], s_, 2.0, c_, ALU.mult, ALU.mult, [T[0].k, T[1].k], [sinT.k])
                    TT(c_, s_, s_, ALU.mult, [T[0].k], [T[1].k])
                    TS(cosT.t[:, m, :], c_, -2.0, 1.0, ALU.mult, ALU.add, [T[1].k], [cosT.k])
                    TS(rtab.t[:, m, :], onesF, RR[:, m:m + 1], None, ALU.mult, None, [ones.k, pk], [rtab.k])
                MS(rtab.t[:, :, 0:1], 0.0, [rtab.k])
                TT(AR, RR, cosT.t[:, :, 1], ALU.mult, [pk, cosT.k], [pk]); TT(AI, RR, sinT.t[:, :, 1], ALU.mult, [pk, sinT.k], [pk])
                TT(DEN, LR, LR, ALU.mult, [pk], [pk]); TT(X1, LI, LI, ALU.mult, [pk], [pk]); TT(DEN, DEN, X1, ALU.add, [pk], [pk])
                P.op("dve", lambda e: e.reciprocal(DEN, DEN), [pk], [pk])
                TS(NR, AR, -1.0, None, ALU.add, None, [pk], [pk])
                TT(X1, NR, LR, ALU.mult, [pk], [pk]); TT(X2, AI, LI, ALU.mult, [pk], [pk]); TT(X1, X1, X2, ALU.add, [pk], [pk]); TT(KR, X1, DEN, ALU.mult, [pk], [pk])
                TT(X1, AI, LR, ALU.mult, [pk], [pk]); TT(X2, NR, LI, ALU.mult, [pk], [pk]); TT(X1, X1, X2, ALU.subtract, [pk], [pk]); TT(KI, X1, DEN, ALU.mult, [pk], [pk])
                for m in range(16):
                    TS(dg.t[:, 0, :], ident.t[:], KR[:, m:m + 1], None, ALU.mult, None, [ident.k, pk], [dg.k])
                    TS(dg.t[:, 1, :], ident.t[:], KI[:, m:m + 1], None, ALU.mult, None, [ident.k, pk], [dg.k])
                    MM(psum[7][:, 0:128], ones.t[:], dg.t[:, 0, :], True, True, [ones.k, dg.k], [k_ps[7]], False)
                    MM(psum[7][:, 128:256], ones.t[:], dg.t[:, 1, :], True, True, [ones.k, dg.k], [k_ps[7]], True)
                    DMA(bl.d, bl.t[:, 0, :], w["Bre"][:, m, :], [], [bl.k]); DMA(bl.d, bl.t[:, 1, :], w["Bim"][:, m, :], [], [bl.k])
                    kr_b = psum[7][:, 0:128]; ki_b = psum[7][:, 128:256]
                    TT(bt.t[:, 0, :], kr_b, bl.t[:, 0, :], ALU.mult, [k_ps[7], bl.k], [bt.k])
                    TT(bt.t[:, 1, :], ki_b, bl.t[:, 1, :], ALU.mult, [k_ps[7], bl.k], [bt.k])
                    TT(Bbr.t[:, m, :], bt.t[:, 0, :], bt.t[:, 1, :], ALU.subtract, [bt.k], [Bbr.k])
                    TT(bt.t[:, 0, :], kr_b, bl.t[:, 1, :], ALU.mult, [k_ps[7], bl.k], [bt.k])
                    TT(bt.t[:, 1, :], ki_b, bl.t[:, 0, :], ALU.mult, [k_ps[7], bl.k], [bt.k])
                    TT(Bbi.t[:, m, :], bt.t[:, 0, :], bt.t[:, 1, :], ALU.add, [bt.k], [Bbi.k])

                def carry_from(hsrc, hk):
                    c0, c1, c2, c3 = [ct.t[:, i, :] for i in range(4)]
                    TT(c0, AR, hsrc[:, 0, :], ALU.mult, [pk, hk], [ct.k]); TT(c1, AI, hsrc[:, 1, :], ALU.mult, [pk, hk], [ct.k])
                    TT(car.t[:, 0, :], c0, c1, ALU.subtract, [ct.k], [car.k])
                    TT(c2, AR, hsrc[:, 1, :], ALU.mult, [pk, hk], [ct.k]); TT(c3, AI, hsrc[:, 0, :], ALU.mult, [pk, hk], [ct.k])
                    TT(car.t[:, 1, :], c2, c3, ALU.add, [ct.k], [car.k])

                def unit(gc, b):
                    smp = b is not None
                    qi = 0 if smp else 127
                    hst = hl if smp else hlp
                    if smp:
                        CP(u.t[:].rearrange("p k i -> p (k i)"), zer.t[:], [zer.k], [u.k])
                        DMA(u.d, u.t[:, :, 0:1], binb[BIN_UA:BIN_UA + 4, :, 2048 + b:2049 + b].rearrange("k p c -> p k c"), [k_bin[i] for i in range(4)], [u.k])
                        DMA(hl.d, hl.t[:, 0, :], s5r0[l, b], [], [hl.k]); DMA(hl.d, hl.t[:, 1, :], s5i0[l, b], [], [hl.k])
                    else:
                        DMA(u.d, u.t[:], binb[BIN_UA:BIN_UA + 4, :, gc * 128:gc * 128 + 128].rearrange("k p c -> p k c"), [k_bin[i] for i in range(4)], [u.k])
                        if gc == 0:
                            MS(hlp.t[:], 0.0, [hlp.k])
                    carry_from(hst.t, hst.k)
                    for hf in range(2):
                        cosv = cosT.t[:, 8 * hf:8 * hf + 8, :]; sinv = sinT.t[:, 8 * hf:8 * hf + 8, :]
                        v3 = lambda B_: B_.t[:, 0:1024].rearrange("p (m i) -> p m i", m=8)
                        for mm in range(8):
                            m = 8 * hf + mm
                            bank = mm // 4; c0 = 128 * (mm % 4)
                            MM(psum[bank][:, c0:c0 + 128], Bbr.t[:, m, :], u.t[:, m // 4, :], True, True, [Bbr.k, u.k], [k_ps[bank]], mm % 4 == 3)
                            MM(psum[2 + bank][:, c0:c0 + 128], Bbi.t[:, m, :], u.t[:, m // 4, :], True, True, [Bbi.k, u.k], [k_ps[2 + bank]], mm % 4 == 3)
                        for bank in range(2):
                            cs_ = cosT.t[:, 8 * hf + 4 * bank:8 * hf + 4 * bank + 4, :]; sn_ = sinT.t[:, 8 * hf + 4 * bank:8 * hf + 4 * bank + 4, :]
                            sl = slice(512 * bank, 512 * bank + 512)
                            pr = psum[bank][:, :].rearrange("p (m i) -> p m i", m=4); pi_ = psum[2 + bank][:, :].rearrange("p (m i) -> p m i", m=4)
                            v4 = lambda B_: B_.t[:, sl].rearrange("p (m i) -> p m i", m=4)
                            TT(v4(T[0]), pr, cs_, ALU.mult, [k_ps[bank], cosT.k], [T[0].k])
                            TT(v4(T[1]), pi_, sn_, ALU.mult, [k_ps[2 + bank], sinT.k], [T[1].k])
                            TT(v4(T[2]), v4(T[0]), v4(T[1]), ALU.add, [T[0].k, T[1].k], [T[2].k], eng="pool")
                            TT(v4(T[0]), pi_, cs_, ALU.mult, [k_ps[2 + bank], cosT.k], [T[0].k])
                            TT(v4(T[1]), pr, sn_, ALU.mult, [k_ps[bank], sinT.k], [T[1].k])
                            TT(v4(T[3]), v4(T[0]), v4(T[1]), ALU.subtract, [T[0].k, T[1].k], [T[3].k], eng="pool")
                        TT(v3(T[2])[:, :, 0], v3(T[2])[:, :, 0], car.t[:, 0, 8 * hf:8 * hf + 8], ALU.add, [T[2].k, car.k], [T[2].k])
                        TT(v3(T[3])[:, :, 0], v3(T[3])[:, :, 0], car.t[:, 1, 8 * hf:8 * hf + 8], ALU.add, [T[3].k, car.k], [T[3].k])
                        rt = rtab.t[:, 8 * hf:8 * hf + 8, :].rearrange("p m i -> p (m i)")
                        P.op("dve", lambda e, rt=rt: e.tensor_tensor_scan(T[2].t[:, 0:1024], rt, T[2].t[:, 0:1024], 0.0, ALU.mult, ALU.add), [rtab.k, T[2].k], [T[2].k])
                        P.op("dve", lambda e, rt=rt: e.tensor_tensor_scan(T[3].t[:, 0:1024], rt, T[3].t[:, 0:1024], 0.0, ALU.mult, ALU.add), [rtab.k, T[3].k], [T[3].k])
                        gr = v3(T[2]); gi = v3(T[3])
                        hr = RB[0]; hin = RB[1]
                        v3f = lambda B_: B_.t[:, 0:1024].bitcast(F32).rearrange("p (m i) -> p m i", m=8)
                        TT(v3(T[0]), gr, cosv, ALU.mult, [T[2].k, cosT.k], [T[0].k])
                        TT(v3(T[1]), gi, sinv, ALU.mult, [T[3].k, sinT.k], [T[1].k])
                        TT(v3(hr), v3(T[0]), v3(T[1]), ALU.subtract, [T[0].k, T[1].k], [hr.k], eng="pool")
                        TT(v3(T[0]), gr, sinv, ALU.mult, [T[2].k, sinT.k, hr.k], [T[0].k])
                        TT(v3(T[1]), gi, cosv, ALU.mult, [T[3].k, cosT.k, hr.k], [T[1].k])
                        STT(v3(hin), v3(T[0]), -1.0, v3(T[1]), ALU.mult, ALU.subtract, [T[0].k, T[1].k], [hin.k])
                        CP(hst.t[:, 0, 8 * hf:8 * hf + 8], v3f(hr)[:, :, qi], [hr.k], [hst.k])
                        TS(hst.t[:, 1, 8 * hf:8 * hf + 8], v3f(hin)[:, :, qi], -1.0, None, ALU.mult, None, [hin.k], [hst.k])
                        for jj in range(2):
                            j = 2 * hf + jj
                            for q in range(4):
                                mm = 4 * jj + q; m = 8 * hf + mm
                                MM(psum[4][:, 128 * jj:128 * jj + 128], Cre.t[:, m, :], hr.t[:, 128 * mm:128 * mm + 128], q == 0, False, [Cre.k, hr.k], [k_ps[4]], False)
                                MM(psum[4][:, 128 * jj:128 * jj + 128], Cim.t[:, m, :], hin.t[:, 128 * mm:128 * mm + 128], False, q == 3, [Cim.k, hin.k], [k_ps[4]], q == 3)
                            yv = T[0].t[:, 0:128]
                            STT(yv, u.t[:, j, :].bitcast(F32), s5d.t[:, j:j + 1], psum[4][:, 128 * jj:128 * jj + 128], ALU.mult, ALU.add, [u.k, s5d.k, k_ps[4]], [T[0].k])
                            if smp:
                                _g1(T[0].t[:, 0:1], T[0].k, oa.t[:, j, NP + b:NP + b + 1], oa.k)
                            else:
                                lc = (gc % 8) * 128
                                _g1(yv, T[0].k, oa.t[:, j, lc:lc + 128], oa.k)
                    if smp:
                        DMA(hl.d, o_s5s[l, b].rearrange("r p m -> p r m"), hl.t[:], [hl.k], [k_out])
                    elif gc == 15:
                        DMA(hlp.d, o_s5p[l].rearrange("r p m -> p r m"), hlp.t[:], [hlp.k], [k_out])

                gtmp = _W(T[4], T[4].t[:, 0:256].rearrange("p (g i) -> p g i", g=2))

                def _g1(src, srck, out, outk):
                    n = src.shape[1]
                    a = gtmp.t[:, 0, 0:n]; b_ = gtmp.t[:, 1, 0:n]
                    ACT(a, src, AF.Square, [srck], [gtmp.k])
                    TS(a, a, 0.044715, 1.0, ALU.mult, ALU.add, [gtmp.k], [gtmp.k])
                    TT(a, a, src, ALU.mult, [gtmp.k, srck], [gtmp.k])
                    ACT(b_, a, AF.Sigmoid, [gtmp.k], [gtmp.k], scale=1.5957691216)
                    TT(out, b_, src, ALU.mult, [gtmp.k, srck], [outk])

                def s5_samples():
                    uS = sbuf(s, "s5uS", [128, 4, NSMP], FR)
                    v4 = lambda ap: ap.rearrange("p (r m b) -> p r m b", r=2, m=16)
                    h0 = _W(T[0], v4(T[0].t[:, 0:512])); hn = _W(RB[0], v4(RB[0].t[:, 0:512]))
                    cS = _W(T[1], v4(T[1].t[:, 0:512])); tS = _W(T[2], v4(T[2].t[:, 0:512]))
                    yS = _W(T[3], T[3].t[:, 0:NSMP])
                    DMA(uS.d, uS.t[:], binb[BIN_UA:BIN_UA + 4, :, 2048:2048 + NSMP].rearrange("k p c -> p k c"), [k_bin[i] for i in range(4)], [uS.k])
                    DMA(h0.d, h0.t[:, 0], s5r0[l], [], [h0.k]); DMA(h0.d, h0.t[:, 1], s5i0[l], [], [h0.k])
                    arb = AR.unsqueeze(2).to_broadcast([128, 16, NSMP]); aib = AI.unsqueeze(2).to_broadcast([128, 16, NSMP])
                    TT(tS.t[:, 0], h0.t[:, 0], arb, ALU.mult, [h0.k, pk], [tS.k]); TT(tS.t[:, 1], h0.t[:, 1], aib, ALU.mult, [h0.k, pk], [tS.k])
                    TT(cS.t[:, 0], tS.t[:, 0], tS.t[:, 1], ALU.subtract, [tS.k], [cS.k])
                    TT(tS.t[:, 0], h0.t[:, 1], arb, ALU.mult, [h0.k, pk, cS.k], [tS.k]); TT(tS.t[:, 1], h0.t[:, 0], aib, ALU.mult, [h0.k, pk, cS.k], [tS.k])
                    TT(cS.t[:, 1], tS.t[:, 0], tS.t[:, 1], ALU.add, [tS.k], [cS.k])
                    for m in range(16):
                        MM(psum[0][:, 16 * m:16 * m + 16], Bbr.t[:, m, :], uS.t[:, m // 4, :], True, True, [Bbr.k, uS.k], [k_ps[0]], False)
                        MM(psum[0][:, 256 + 16 * m:256 + 16 * m + 16], Bbi.t[:, m, :], uS.t[:, m // 4, :], True, True, [Bbi.k, uS.k], [k_ps[0]], m == 15)
                    pv = psum[0][:, :].rearrange("p (r m b) -> p r m b", r=2, m=16)
                    TT(hn.t[:, 0], pv[:, 0], cS.t[:, 0], ALU.add, [k_ps[0], cS.k], [hn.k])
                    TT(tS.t[:, 1], pv[:, 1], cS.t[:, 1], ALU.add, [k_ps[0], cS.k], [tS.k])
                    TS(hn.t[:, 1], tS.t[:, 1], -1.0, None, ALU.mult, None, [tS.k], [hn.k])
                    CP(tS.t[:, 0], hn.t[:, 0].bitcast(F32), [hn.k], [tS.k])
                    DMA(tS.d, o_s5s[l].rearrange("r p m b -> p r m b"), tS.t, [tS.k], [k_out])
                    for j in range(4):
                        for q in range(4):
                            m = 4 * j + q
                            MM(psum[1][:, 16 * j:16 * j + 16], Cre.t[:, m, :], hn.t[:, 0, m, :], q == 0, False, [Cre.k, hn.k], [k_ps[1]], False)
                            MM(psum[1][:, 16 * j:16 * j + 16], Cim.t[:, m, :], hn.t[:, 1, m, :], False, q == 3, [Cim.k, hn.k], [k_ps[1]], q == 3)
                        STT(yS.t, uS.t[:, j, :].bitcast(F32), s5d.t[:, j:j + 1], psum[1][:, 16 * j:16 * j + 16], ALU.mult, ALU.add, [uS.k, s5d.k, k_ps[1]], [yS.k])
                        _g1(yS.t, yS.k, oa.t[:, j, NP:NP + NSMP], oa.k)

                for c in range(8):
                    unit(8 * t + c, None)
                if t == 0:
                    s5_samples()
                ws = WS([(w["glu"], 2048)])
                s_ = ws.get(0)
                wv = wr.t[:, s_, :].rearrange("p (k n) -> p k n", k=4)
                for fo in range(4):
                    res = proj_group(lambda k: (wv[:, k, 128 * fo:128 * fo + 128], [k_wr[s_]]),
                                     lambda k, c0, cn: (oa.t[:, k, c0:c0 + cn], [oa.k]), 4, ncol)
                    for (ps_, c0, cn, tk) in res:
                        ACT(T[fo].t[:, c0:c0 + cn], ps_[:, 0:cn], AF.Sigmoid, [tk], [T[fo].k])
                for fo in range(4):
                    TT(oa.t[:, fo, 0:ncol], oa.t[:, fo, 0:ncol].bitcast(F32), T[fo].t[:, 0:ncol], ALU.mult, [oa.k, T[fo].k], [oa.k])

        def phase_gmlp(l, t, ncol, ob):
            w = Lw[l]
            with ExitStack() as s:
                wsm = sbuf(s, "wsm", [128, 8, 128], FR); bsx = sbuf(s, "bsxS", [128, 4, 128]); tri = sbuf(s, "triS", [128, 128])
                gv = sbuf(s, "gvS", [128, 4])
                ug = sbuf(s, "ug", [128, 4, 128]); vg = sbuf(s, "vg", [128, 4, 128]); vn = sbuf(s, "vn", [128, 4, 128])
                vtk = sbuf(s, "vtk", [128, 512], FR); sq = sbuf(s, "gsq", [128, 128], FR); rs = sbuf(s, "grs", [128, 128]); sv = sbuf(s, "gsv", [128, 128])
                DMA(bsx.d, bsx.t[:], w["bsx"], [], [bsx.k]); DMA(tri.d, tri.t[:], c_tril, [], [tri.k]); DMA(gv.d, gv.t[:], w["gv"], [], [gv.k])
                wtmp = T[0].t[:, 0:1024].rearrange("p (h i) -> p h i", h=8)
                DMA(T[0].d, wtmp, w["wsT"], [], [T[0].k])
                TT(wsm.t[:], wtmp, tri.t[:].unsqueeze(1).to_broadcast([128, 8, 128]), ALU.mult, [T[0].k, tri.k], [wsm.k])

                def unit(gc, b):
                    smp = b is not None
                    if smp:
                        MS(ug.t[:], 0.0, [ug.k]); MS(vg.t[:], 0.0, [vg.k])
                        DMA(ug.d, ug.t[:, :, 0:1], binF[BIN_UB:BIN_UB + 4, :, 2048 + b:2049 + b].rearrange("k p c -> p k c"), [k_bin[BIN_UB + i] for i in range(4)], [ug.k])
                        DMA(vg.d, vg.t[:, :, 0:1], binF[BIN_VB:BIN_VB + 4, :, 2048 + b:2049 + b].rearrange("k p c -> p k c"), [k_bin[BIN_VB + i] for i in range(4)], [vg.k])
                    else:
                        c0 = gc * 128
                        DMA(ug.d, ug.t[:], binF[BIN_UB:BIN_UB + 4, :, c0:c0 + 128].rearrange("k p c -> p k c"), [k_bin[BIN_UB + i] for i in range(4)], [ug.k])
                        DMA(vg.d, vg.t[:], binF[BIN_VB:BIN_VB + 4, :, c0:c0 + 128].rearrange("k p c -> p k c"), [k_bin[BIN_VB + i] for i in range(4)], [vg.k])
                    for k in range(4):
                        ACT(sq.t[:], vg.t[:, k, :], AF.Square, [vg.k], [sq.k])
                        MM(psum[6][:, 0:128], ones.t[:], sq.t[:], k == 0, k == 3, [ones.k, sq.k], [k_ps[6]], True)
                    TS(rs.t[:], psum[6][:, 0:128], 1.0 / 512, EPS, ALU.mult, ALU.add, [k_ps[6]], [rs.k])
                    ACT(rs.t[:], rs.t[:], AF.Sqrt, [rs.k], [rs.k])
                    P.op("dve", lambda e: e.reciprocal(rs.t[:], rs.t[:]), [rs.k], [rs.k])
                    for k in range(4):
                        STT(vn.t[:, k, :], vg.t[:, k, :], gv.t[:, k:k + 1], rs.t[:], ALU.mult, ALU.mult, [vg.k, gv.k, rs.k], [vn.k])
                    if smp:
                        DMA(vn.d, o_gv[l][:, :, b:b + 1], vn.t[:, :, 0:1], [vn.k], [k_out])
                    for k in range(4):
                        TR(psum[7][:, 128 * k:128 * k + 128], vn.t[:, k, :], ident.t[:], [vn.k, ident.k], [k_ps[7]])
                    ACT(vtk.t[:], psum[7][:, :], AF.Copy, [k_ps[7]], [vtk.k])
                    for k in range(4):
                        bank = 4 + (k % 2)
                        MM(psum[bank][:, 0:128], vtk.t[:, 128 * k:128 * k + 128], wsm.t[:, 2 * k, :], True, True, [vtk.k, wsm.k], [k_ps[bank]], False)
                        MM(psum[bank][:, 128:256], vtk.t[:, 128 * k:128 * k + 128], wsm.t[:, 2 * k + 1, :], True, True, [vtk.k, wsm.k], [k_ps[bank]], True)
                        TT(sv.t[0:64, :], psum[bank][0:64, 0:128], bsx.t[0:64, k, :], ALU.add, [k_ps[bank], bsx.k], [sv.k])
                        TT(sv.t[64:128, :], psum[bank][64:128, 128:256], bsx.t[64:128, k, :], ALU.add, [k_ps[bank], bsx.k], [sv.k])
                        if smp:
                            TT(ob.t[:, k, NP + b:NP + b + 1], sv.t[:, 0:1], ug.t[:, k, 0:1], ALU.mult, [sv.k, ug.k], [ob.k])
                        else:
                            lc = (gc % 8) * 128
                            TT(ob.t[:, k, lc:lc + 128], sv.t[:], ug.t[:, k, :], ALU.mult, [sv.k, ug.k], [ob.k])

                def gm_samples():
                    uS = sbuf(s, "gmuS", [128, 4, NSMP]); vS = sbuf(s, "gmvS", [128, 4, NSMP]); sqS = sbuf(s, "gmsqS", [128, 4, NSMP], FR)
                    rS = sbuf(s, "gmrS", [128, NSMP]); w00 = sbuf(s, "gmw00", [128, 4]); sS = sbuf(s, "gmsS", [128, NSMP])
                    DMA(uS.d, uS.t[:], binF[BIN_UB:BIN_UB + 4, :, 2048:2048 + NSMP].rearrange("k p c -> p k c"), [k_bin[BIN_UB + i] for i in range(4)], [uS.k])
                    DMA(vS.d, vS.t[:], binF[BIN_VB:BIN_VB + 4, :, 2048:2048 + NSMP].rearrange("k p c -> p k c"), [k_bin[BIN_VB + i] for i in range(4)], [vS.k])
                    DMA(w00.d, w00.t[:], w["w00"], [], [w00.k])
                    ACT(sqS.t[:], vS.t[:], AF.Square, [vS.k], [sqS.k])
                    for k in range(4):
                        MM(psum[6][:, 0:NSMP], ones.t[:], sqS.t[:, k, :], k == 0, k == 3, [ones.k, sqS.k], [k_ps[6]], k == 3)
                    TS(rS.t[:], psum[6][:, 0:NSMP], 1.0 / 512, EPS, ALU.mult, ALU.add, [k_ps[6]], [rS.k])
                    ACT(rS.t[:], rS.t[:], AF.Sqrt, [rS.k], [rS.k])
                    P.op("dve", lambda e: e.reciprocal(rS.t[:], rS.t[:]), [rS.k], [rS.k])
                    for k in range(4):
                        STT(vS.t[:, k, :], vS.t[:, k, :], gv.t[:, k:k + 1], rS.t[:], ALU.mult, ALU.mult, [vS.k, gv.k, rS.k], [vS.k])
                    DMA(vS.d, o_gv[l], vS.t[:], [vS.k], [k_out])
                    for k in range(4):
                        TS(sS.t[:], vS.t[:, k, :], w00.t[:, k:k + 1], bsx.t[:, k, 0:1], ALU.mult, ALU.add, [vS.k, w00.k, bsx.k], [sS.k])
                        TT(ob.t[:, k, NP:NP + NSMP], sS.t[:], uS.t[:, k, :], ALU.mult, [sS.k, uS.k], [ob.k])

                for c in range(8):
                    unit(8 * t + c, None)
                if t == 0:
                    gm_samples()

        def phase_ssd(l, t, ncol, oc):
            w = Lw[l]
            with ExitStack() as s:
                xe = sbuf(s, "xe", [128, 12, 131]); bc = sbuf(s, "bcS", [128, 4, 128], FR)
                class _V:
                    def __init__(self, t, d=None):
                        self.t = t; self.k = Tok(); self.d = d
                MT = _V(wr.t[:, 0, :].rearrange("p (h i) -> p h i", h=16))
                sex = _V(wr.t[0:16, 1, 1024:2048].rearrange("p (m i) -> p m i", m=8), d_wr_(1))
                xdd = _V(wr.t[:, 1, 0:1024])
                sm = _V(T[3].t[0:16, 0:640].rearrange("p (m i) -> p m i", m=5), T[3].d)
                GT = _V(T[3].t[:, 640:896].rearrange("p (g i) -> p g i", g=2))
                LT = _V(T[4].t[:, 0:256].rearrange("p (g i) -> p g i", g=2))
                ecs = _V(T[4].t[:, 256:384]); yt = _V(T[4].t[:, 384:512])
                cva = _V(T[4].t[:, 512:768].rearrange("p (g i) -> p g i", g=2)); rs2 = _V(T[4].t[:, 768:896])
                csm = sbuf(s, "csm", [16, 2, 128], FR)
                negm = sbuf(s, "negm", [128, 128])
                cw = sbuf(s, "cwS", [128, 12, 4]); cb = sbuf(s, "cbS", [128, 12])
                csr = sbuf(s, "csr", [16, 128], FR)
                p16 = sbuf(s, "ssdp16", [16, 4])
                Dx = sbuf(s, "DxS", [128, 8]); gn = sbuf(s, "gnS", [128, 8])
                tk_ = sbuf(s, "ssdtok", [128, 6, 16])
                Btok = sbuf(s, "Btok", [128, 2, 128], FR)
                sqm = sbuf(s, "sqm", [128, 128], FR)
                Sns = _V(RB[1].t[:, 0:1024], RB[1].d)
                b32 = sbuf(s, "b32", [128, 2, 128])
                DMA(sex.d, sex.t, c_selexp, [], [sex.k]); DMA(negm.d, negm.t[:], c_negmask, [], [negm.k])
                DMA(cw.d, cw.t[:], w["cw"], [], [cw.k]); DMA(cb.d, cb.t[:], w["cb"], [], [cb.k])
                DMA(p16.d, p16.t[:, 0:1], w["dtb"], [], [p16.k]); DMA(p16.d, p16.t[:, 1:2], w["alog"], [], [p16.k])
                DMA(Dx.d, Dx.t[:], w["ssdD"], [], [Dx.k]); DMA(gn.d, gn.t[:], w["ssdg"], [], [gn.k])
                ACT(p16.t[:, 2:3], p16.t[:, 1:2], AF.Exp, [p16.k], [p16.k])
                TS(p16.t[:, 2:3], p16.t[:, 2:3], -1.0, None, ALU.mult, None, [p16.k], [p16.k])
                xs = T[0]; zs = T[1]; yz = T[2]; xdt = RB[0]
                xs3 = xs.t[:, 0:1024].rearrange("p (m i) -> p m i", m=8)
                zs3 = zs.t[:, 0:1024].rearrange("p (m i) -> p m i", m=8)
                yz3 = yz.t[:, 0:1024].rearrange("p (m i) -> p m i", m=8)
                dtr, ee, dt32, dA, cs32 = [sm.t[:, i, :] for i in range(5)]
                dt_tok, ncs_tok, csl, de_tok, cd, tk5 = [tk_.t[:, i, :] for i in range(6)]

                def unit(gc, b):
                    smp = b is not None
                    Sn = Sns if smp else Snp
                    if smp:
                        MS(xe.t[:], 0.0, [xe.k]); MS(zs.t[:, 0:1024], 0.0, [zs.k]); MS(dtr, 0.0, [sm.k])
                        DMA(xe.d, xe.t[:, :, 0:3], cbuf0[l][:, :, b, :], [], [xe.k])
                        DMA(xe.d, xe.t[:, :, 3:4], binF[BIN_XBC:BIN_XBC + 12, :, 2048 + b:2049 + b].rearrange("k p c -> p k c"), [k_bin[BIN_XBC + i] for i in range(12)], [xe.k])
                        DMA(zs.d, zs3[:, :, 0:1], binF[BIN_Z:BIN_Z + 8, :, 2048 + b:2049 + b].rearrange("k p c -> p k c"), [k_bin[BIN_Z + i] for i in range(8)], [zs.k])
                        DMA(sm.d, dtr[:, 0:1], binF[BIN_DT, 0:16, 2048 + b:2049 + b], [k_bin[BIN_DT]], [sm.k])
                        DMA(Sn.d, Sn.t[:, 0:1024], ssm0[l, b], [], [Sn.k])
                    else:
                        c0 = gc * 128
                        if gc == 0:
                            MS(xe.t[:, :, 0:3], 0.0, [xe.k])
                            DMA(xe.d, xe.t[:, :, 3:131], binF[BIN_XBC:BIN_XBC + 12, :, 0:128].rearrange("k p c -> p k c"), [k_bin[BIN_XBC + i] for i in range(12)], [xe.k])
                            CP(Sn.t[:, 0:512], zer.t[:], [zer.k], [Sn.k]); CP(Sn.t[:, 512:1024], zer.t[:], [zer.k], [Sn.k])
                        else:
                            DMA(xe.d, xe.t[:], binF[BIN_XBC:BIN_XBC + 12, :, c0 - 3:c0 + 128].rearrange("k p c -> p k c"), [k_bin[BIN_XBC + i] for i in range(12)], [xe.k])
                        DMA(zs.d, zs3, binF[BIN_Z:BIN_Z + 8, :, c0:c0 + 128].rearrange("k p c -> p k c"), [k_bin[BIN_Z + i] for i in range(8)], [zs.k])
                        DMA(sm.d, dtr, binF[BIN_DT, 0:16, c0:c0 + 128], [k_bin[BIN_DT]], [sm.k])
                    if smp:
                        DMA(xe.d, o_ccs[l][:, :, b, :], xe.t[:, :, 1:4], [xe.k], [k_out])
                    elif gc == 15:
                        DMA(xe.d, o_ccp[l], xe.t[:, :, 128:131], [xe.k], [k_out])
                    for k in range(12):
                        cv = cva.t[:, k % 2, :]
                        TS(cv, xe.t[:, k, 3:131], cw.t[:, k, 3:4], cb.t[:, k:k + 1], ALU.mult, ALU.add, [xe.k, cw.k, cb.k], [cva.k])
                        for j_ in (2, 1, 0):
                            STT(cv, xe.t[:, k, j_:j_ + 128], cw.t[:, k, j_:j_ + 1], cv, ALU.mult, ALU.add, [xe.k, cw.k, cva.k], [cva.k])
                        if k < 8:
                            ACT(xs3[:, k, :], cv, AF.Silu, [cva.k], [xs.k])
                        else:
                            ACT(bc.t[:, k - 8, :], cv, AF.Silu, [cva.k], [bc.k])
                            if k < 10:
                                ACT(b32.t[:, k - 8, :], cv, AF.Silu, [cva.k], [b32.k])
                    ACT(ee, dtr, AF.Exp, [sm.k, p16.k], [sm.k], bias=p16.t[:, 0:1], scale=1.0)
                    TS(ee, ee, 1.0, None, ALU.add, None, [sm.k], [sm.k])
                    ACT(dt32, ee, AF.Ln, [sm.k], [sm.k])
                    if smp:
                        MS(dt32[:, 1:128], 0.0, [sm.k])
                    TS(dA, dt32, p16.t[:, 2:3], None, ALU.mult, None, [sm.k, p16.k], [sm.k])
                    P.op("dve", lambda e: e.tensor_tensor_scan(cs32, onesF[0:16, :], dA, 0.0, ALU.mult, ALU.add), [sm.k, ones.k], [sm.k])
                    ACT(csr.t[:], cs32, AF.Copy, [sm.k], [csr.k])
                    TR(psum[0][:, 0:16], dt32, ident.t[0:16, 0:16], [sm.k, ident.k], [k_ps[0]])
                    TR(psum[0][:, 16:32], cs32, ident.t[0:16, 0:16], [sm.k, ident.k], [k_ps[0]])
                    ACT(dt_tok, psum[0][:, 0:16], AF.Copy, [k_ps[0]], [tk_.k])
                    TS(ncs_tok, psum[0][:, 16:32], -1.0, None, ALU.mult, None, [k_ps[0]], [tk_.k])
                    for g in range(2):
                        MM(psum[0][:, 128 + 128 * g:256 + 128 * g], bc.t[:, g, :], bc.t[:, 2 + g, :], True, True, [bc.k], [k_ps[0]], g == 1)
                        TR(psum[1][:, 128 * g:128 * g + 128], b32.t[:, g, :], ident.t[:], [b32.k, ident.k], [k_ps[1]])
                    ACT(GT.t, psum[0][:, 128:384].rearrange("p (g i) -> p g i", g=2), AF.Copy, [k_ps[0]], [GT.k])
                    ACT(Btok.t[:], psum[1][:, 0:256].rearrange("p (g i) -> p g i", g=2), AF.Copy, [k_ps[1]], [Btok.k])
                    for q in range(4):
                        for hh in range(4):
                            h = 4 * q + hh
                            TS(csm.t[:, hh % 2, :], cs32, ident.t[0:16, h:h + 1], None, ALU.mult, None, [sm.k, ident.k], [csm.k])
                            MM(psum[4][:, 128 * hh:128 * hh + 128], ones.t[0:16, :], csm.t[:, hh % 2, :], True, True, [ones.k, csm.k], [k_ps[4]], True)
                        CP(csl[:, 4 * q:4 * q + 4], psum[4][:, :].rearrange("p (h i) -> p h i", h=4)[:, :, 127], [k_ps[4]], [tk_.k])
                        for hh in range(4):
                            h = 4 * q + hh
                            lt = LT.t[:, hh % 2, :]
                            TT(lt, psum[4][:, 128 * hh:128 * hh + 128], negm.t[:], ALU.add, [k_ps[4], negm.k], [LT.k])
                            ACT(lt, lt, AF.Exp, [LT.k, tk_.k], [LT.k], bias=ncs_tok[:, h:h + 1], scale=1.0)
                            TT(MT.t[:, h, :], GT.t[:, h // 8, :], lt, ALU.mult, [GT.k, LT.k], [MT.k])
                    TT(tk5, csl, ncs_tok, ALU.add, [tk_.k], [tk_.k])
                    ACT(de_tok, tk5, AF.Exp, [tk_.k], [tk_.k])
                    ACT(cd, csl, AF.Exp, [tk_.k], [tk_.k])
                    for m in range(8):
                        bank = 2 + m // 4
                        TR(psum[bank][:, 128 * (m % 4):128 * (m % 4) + 128], xs3[:, m, :], ident.t[:], [xs.k, ident.k], [k_ps[bank]])
                    for hb in range(2):
                        TT(xdt.t[:, 512 * hb:512 * hb + 512].rearrange("p (h d) -> p h d", h=8),
                           psum[2 + hb][:, :].rearrange("p (h d) -> p h d", h=8),
                           dt_tok[:, 8 * hb:8 * hb + 8].unsqueeze(2).to_broadcast([128, 8, 64]), ALU.mult, [k_ps[2 + hb], tk_.k], [xdt.k])
                    TT(xdd.t.rearrange("p (h d) -> p h d", h=16), xdt.t[:, 0:1024].bitcast(F32).rearrange("p (h d) -> p h d", h=16),
                       de_tok.unsqueeze(2).to_broadcast([128, 16, 64]), ALU.mult, [xdt.k, tk_.k], [xdd.k])
                    for m in range(8):
                        g = m // 4
                        bank = 5 + (m % 2)
                        pY = psum[bank]
                        MM(pY[:, 0:128], Sn.t[:, 128 * m:128 * m + 128], bc.t[:, 2 + g, :], True, True, [Sn.k, bc.k], [k_ps[bank]], False)
                        MM(pY[:, 128:256], xdt.t[:, 128 * m:128 * m + 128], MT.t[:, 2 * m, :], True, True, [xdt.k, MT.k], [k_ps[bank]], False)
                        MM(pY[:, 256:384], xdt.t[:, 128 * m:128 * m + 128], MT.t[:, 2 * m + 1, :], True, True, [xdt.k, MT.k], [k_ps[bank]], False)
                        MM(pY[:, 384:512], sex.t[:, m, :], csr.t[:], True, True, [sex.k, csr.k], [k_ps[bank]], True)
                        ACT(ecs.t, pY[:, 384:512], AF.Exp, [k_ps[bank]], [ecs.k])
                        TT(yt.t, pY[:, 0:128], ecs.t, ALU.mult, [k_ps[bank], ecs.k], [yt.k])
                        TT(yt.t[0:64, :], yt.t[0:64, :], pY[0:64, 128:256], ALU.add, [yt.k, k_ps[bank]], [yt.k])
                        TT(yt.t[64:128, :], yt.t[64:128, :], pY[64:128, 256:384], ALU.add, [yt.k, k_ps[bank]], [yt.k])
                        STT(yt.t, xs3[:, m, :], Dx.t[:, m:m + 1], yt.t, ALU.mult, ALU.add, [xs.k, Dx.k, yt.k], [yt.k])
                        TT(yz3[:, m, :], yt.t, zs3[:, m, :], ALU.mult, [yt.k, zs.k], [yz.k])
                        ACT(sqm.t[:], yz3[:, m, :], AF.Square, [yz.k], [sqm.k])
                        MM(psum[7][:, 0:128], ones.t[:], sqm.t[:], m == 0, m == 7, [ones.k, sqm.k], [k_ps[7]], True)
                    TS(rs2.t, psum[7][:, 0:128], 1.0 / 1024, EPS, ALU.mult, ALU.add, [k_ps[7]], [rs2.k])
                    ACT(rs2.t, rs2.t, AF.Sqrt, [rs2.k], [rs2.k])
                    P.op("dve", lambda e: e.reciprocal(rs2.t, rs2.t), [rs2.k], [rs2.k])
                    for m in range(8):
                        if smp:
                            STT(oc.t[:, m, NP + b:NP + b + 1], yz3[:, m, 0:1], gn.t[:, m:m + 1], rs2.t[:, 0:1], ALU.mult, ALU.mult, [yz.k, gn.k, rs2.k], [oc.k])
                        else:
                            lc = (gc % 8) * 128
                            STT(oc.t[:, m, lc:lc + 128], yz3[:, m, :], gn.t[:, m:m + 1], rs2.t, ALU.mult, ALU.mult, [yz.k, gn.k, rs2.k], [oc.k])
                    for g in range(2):
                        MM(psum[2 + g][:, :], Btok.t[:, g, :], xdd.t[:, 512 * g:512 * g + 512], True, True, [Btok.k, xdd.k], [k_ps[2 + g]], True)
                        sv = Sn.t[:, 512 * g:512 * g + 512]
                        TT(sv.rearrange("p (h d) -> p h d", h=8), sv.bitcast(F32).rearrange("p (h d) -> p h d", h=8),
                           cd[:, 8 * g:8 * g + 8].unsqueeze(2).to_broadcast([128, 8, 64]), ALU.mult, [Sn.k, tk_.k], [Sn.k])
                        TT(sv, sv.bitcast(F32), psum[2 + g][:, :], ALU.add, [Sn.k, k_ps[2 + g]], [Sn.k])
                    if smp:
                        DMA(Sn.d, o_ssms[l, b], Sn.t[:, 0:1024].bitcast(F32), [Sn.k], [k_out])
                    elif gc == 15:
                        DMA(Sn.d, o_ssmp[l], Sn.t[:, 0:1024].bitcast(F32), [Sn.k], [k_out])

                def ssd_samples():
                    class _A:
                        def __init__(self, o_, ap):
                            self.t = ap; self.k = o_.k; self.d = o_.d
                    NS = NSMP
                    xef = xe.t[:].rearrange("p k c -> p (k c)")
                    cbS = _A(xe, xef[:, 0:576].rearrange("p (k b j) -> p k b j", k=12, b=NS))
                    xS = _A(xe, xef[:, 576:768].rearrange("p (k b) -> p k b", k=12))
                    ncb = _A(xe, xef[:, 768:1344].rearrange("p (k b j) -> p k b j", k=12, b=NS))
                    cv = _A(xs, xs.t[:, 0:192].rearrange("p (k b) -> p k b", k=12))
                    xsS = _A(xs, xs.t[:, 192:320].rearrange("p (k b) -> p k b", k=8))
                    ctmp = _A(xs, xs.t[:, 320:512].rearrange("p (k b) -> p k b", k=12))
                    zsS = _A(zs, zs.t[:, 0:128].rearrange("p (k b) -> p k b", k=8))
                    yS = _A(yz, yz.t[:, 0:128].rearrange("p (k b) -> p k b", k=8))
                    yzS = _A(yz, yz.t[:, 128:256].rearrange("p (k b) -> p k b", k=8))
                    xdtS = _A(xdd, xdd.t[0:16, :])
                    diagE = _A(MT, wr.t[0:16, 0, 0:256])
                    cdS = _A(LT, T[4].t[:, 0:256].rearrange("p (b h) -> p b h", b=NS))
                    sqS = _A(sqm, sqm.t[:].rearrange("p (k b) -> p k b", k=8))
                    rsS = _A(rs2, T[4].t[:, 768:768 + NS])
                    SnAB = [_A(Sns, RB[1].t[:, 0:1024]), _A(xdt, RB[0].t[:, 0:1024])]
                    SnAB[1].d = P.dsem("R0")
                    dtrS, eeS, dtS, dAS, edA = [sm.t[:, i, 0:NS] for i in range(5)]
                    DMA(xe.d, cbS.t, cbuf0[l], [], [xe.k])
                    DMA(xe.d, xS.t, binF[BIN_XBC:BIN_XBC + 12, :, 2048:2048 + NS].rearrange("k p c -> p k c"), [k_bin[BIN_XBC + i] for i in range(12)], [xe.k])
                    DMA(zs.d, zsS.t, binF[BIN_Z:BIN_Z + 8, :, 2048:2048 + NS].rearrange("k p c -> p k c"), [k_bin[BIN_Z + i] for i in range(8)], [zs.k])
                    DMA(sm.d, dtrS, binF[BIN_DT, 0:16, 2048:2048 + NS], [k_bin[BIN_DT]], [sm.k])
                    wb = lambda j: cw.t[:, :, j:j + 1].to_broadcast([128, 12, NS])
                    TT(cv.t, xS.t, wb(3), ALU.mult, [xe.k, cw.k], [xs.k])
                    for j in (2, 1, 0):
                        TT(ctmp.t, cbS.t[:, :, :, j], wb(j), ALU.mult, [xe.k, cw.k, xs.k], [xs.k])
                        TT(cv.t, cv.t, ctmp.t, ALU.add, [xs.k], [xs.k])
                    TT(cv.t, cv.t, cb.t[:].unsqueeze(2).to_broadcast([128, 12, NS]), ALU.add, [xs.k, cb.k], [xs.k])
                    ACT(xsS.t, cv.t[:, 0:8, :], AF.Silu, [xs.k], [xs.k])
                    ACT(bc.t[:, :, 0:NS], cv.t[:, 8:12, :], AF.Silu, [xs.k], [bc.k])
                    ACT(b32.t[:, :, 0:NS], cv.t[:, 8:10, :], AF.Silu, [xs.k], [b32.k])
                    CP(ncb.t[:, :, :, 0:2], cbS.t[:, :, :, 1:3], [xe.k], [xe.k])
                    CP(ncb.t[:, :, :, 2], xS.t, [xe.k], [xe.k])
                    DMA(xe.d, o_ccs[l], ncb.t, [xe.k], [k_out])
                    ACT(eeS, dtrS, AF.Exp, [sm.k, p16.k], [sm.k], bias=p16.t[:, 0:1], scale=1.0)
                    TS(eeS, eeS, 1.0, None, ALU.add, None, [sm.k], [sm.k])
                    ACT(dtS, eeS, AF.Ln, [sm.k], [sm.k])
                    TS(dAS, dtS, p16.t[:, 2:3], None, ALU.mult, None, [sm.k, p16.k], [sm.k])
                    ACT(edA, dAS, AF.Exp, [sm.k], [sm.k])
                    TT(diagE.t.rearrange("p (b h) -> p b h", b=NS), ident.t[0:16, 0:16].unsqueeze(1).to_broadcast([16, NS, 16]),
                       edA.unsqueeze(2).to_broadcast([16, NS, 16]), ALU.mult, [ident.k, sm.k], [MT.k])
                    MM(psum[4][:, 0:256], ones.t[0:16, :], diagE.t, True, True, [ones.k, MT.k], [k_ps[4]], True)
                    CP(cdS.t, psum[4][:, 0:256].rearrange("p (b h) -> p b h", b=NS), [k_ps[4]], [LT.k])
                    TR(psum[0][0:16, 0:16], dtS, ident.t[0:16, 0:16], [sm.k, ident.k], [k_ps[0]])
                    CP(tk_.t[0:16, 0, :], psum[0][0:16, 0:16], [k_ps[0]], [tk_.k])
                    for m in range(8):
                        bank = 2 + m // 4
                        TR(psum[bank][0:16, 128 * (m % 4):128 * (m % 4) + 128], xsS.t[:, m, :], ident.t[:], [xs.k, ident.k], [k_ps[bank]])
                    for hb in range(2):
                        TT(xdtS.t[:, 512 * hb:512 * hb + 512].rearrange("p (h d) -> p h d", h=8), psum[2 + hb][0:16, :].rearrange("p (h d) -> p h d", h=8),
                           tk_.t[0:16, 0, 8 * hb:8 * hb + 8].unsqueeze(2).to_broadcast([16, 8, 64]), ALU.mult, [k_ps[2 + hb], tk_.k], [xdd.k])
                    for g in range(2):
                        TR(psum[1][0:16, 128 * g:128 * g + 128], b32.t[:, g, 0:NS], ident.t[:], [b32.k, ident.k], [k_ps[1]])
                    ACT(Btok.t[0:16, :, :], psum[1][0:16, 0:256].rearrange("p (g i) -> p g i", g=2), AF.Copy, [k_ps[1]], [Btok.k])
                    for b in range(NS):
                        Sn = SnAB[b % 2]
                        DMA(Sn.d, Sn.t, ssm0[l, b], [], [Sn.k])
                        TS(csm.t[:], Btok.t[0:16, :, :].bitcast(F32), ident.t[0:16, b:b + 1], None, ALU.mult, None, [Btok.k, ident.k], [csm.k])
                        for g in range(2):
                            MM(psum[2 + g][:, :], csm.t[:, g, :], xdtS.t[:, 512 * g:512 * g + 512], True, True, [csm.k, xdd.k], [k_ps[2 + g]], True)
                            sv = Sn.t[:, 512 * g:512 * g + 512]
                            TT(sv.rearrange("p (h d) -> p h d", h=8), sv.bitcast(F32).rearrange("p (h d) -> p h d", h=8),
                               cdS.t[:, b, 8 * g:8 * g + 8].unsqueeze(2).to_broadcast([128, 8, 64]), ALU.mult, [Sn.k, LT.k], [Sn.k])
                            TT(sv, sv.bitcast(F32), psum[2 + g][:, :], ALU.add, [Sn.k, k_ps[2 + g]], [Sn.k])
                        pb = 5 + (b % 2)
                        for m in range(8):
                            MM(psum[pb][:, 2 * m:2 * m + 2], Sn.t[:, 128 * m:128 * m + 128], bc.t[:, 2 + m // 4, b:b + 2], True, True, [Sn.k, bc.k], [k_ps[pb]], m == 7)
                        CP(yS.t[:, :, b], psum[pb][:, 0:16].rearrange("p (m c) -> p m c", m=8)[:, :, 0], [k_ps[pb]], [yz.k])
                        DMA(Sn.d, o_ssms[l, b], Sn.t.bitcast(F32), [Sn.k], [k_out])
                    TT(yzS.t, xsS.t, Dx.t[:].unsqueeze(2).to_broadcast([128, 8, NS]), ALU.mult, [xs.k, Dx.k], [yz.k])
                    TT(yzS.t, yzS.t, yS.t, ALU.add, [yz.k], [yz.k])
                    TT(yzS.t, yzS.t, zsS.t, ALU.mult, [yz.k, zs.k], [yz.k])
                    ACT(sqS.t, yzS.t, AF.Square, [yz.k], [sqm.k])
                    for m in range(8):
                        MM(psum[7][:, 0:NS], ones.t[:], sqS.t[:, m, :], m == 0, m == 7, [ones.k, sqm.k], [k_ps[7]], m == 7)
                    TS(rsS.t, psum[7][:, 0:NS], 1.0 / 1024, EPS, ALU.mult, ALU.add, [k_ps[7]], [rs2.k])
                    ACT(rsS.t, rsS.t, AF.Sqrt, [rs2.k], [rs2.k])
                    P.op("dve", lambda e: e.reciprocal(rsS.t, rsS.t), [rs2.k], [rs2.k])
                    for m in range(8):
                        STT(oc.t[:, m, NP:NP + NS], yzS.t[:, m, :], gn.t[:, m:m + 1], rsS.t, ALU.mult, ALU.mult, [yz.k, gn.k, rs2.k], [oc.k])

                for c in range(8):
                    unit(8 * t + c, None)
                if t == 0:
                    ssd_samples()

        def phase_merge(l, t, ncol, oa, ob, oc):
            w = Lw[l]
            blocks = []
            for fo in range(16):
                blocks += [(w["ing"][fo], 2048), (w["pa"][fo], 512), (w["ing"][16 + fo], 2048), (w["pb"][fo], 512),
                           (w["ing"][32 + fo], 2048), (w["pc"][fo], 1024)]
            ws = WS(blocks)
            bi = 0
            for fo in range(16):
                macc = T[fo % 2]
                for br, (ob_, kc) in enumerate(((oa, 4), (ob, 4), (oc, 8))):
                    s_ = ws.get(bi); bi += 1
                    wv = wr.t[:, s_, :].rearrange("p (k n) -> p k n", k=16)
                    res = proj_group(lambda k: (wv[:, k, :], [k_wr[s_]]), lambda k, c0, cn: (hbuf.t[:, k, c0:c0 + cn], [k_h[k]]), 16, ncol)
                    for (ps_, c0, cn, tk) in res:
                        ACT(T[2].t[:, c0:c0 + cn], ps_[:, 0:cn], AF.Sigmoid, [tk], [T[2].k])
                    s2 = ws.get(bi); bi += 1
                    wv2 = wr.t[:, s2, 0:kc * 128].rearrange("p (k n) -> p k n", k=kc)
                    res = proj_group(lambda k: (wv2[:, k, :], [k_wr[s2]]), lambda k, c0, cn: (ob_.t[:, k, c0:c0 + cn], [ob_.k]), kc, ncol)
                    for (ps_, c0, cn, tk) in res:
                        if br == 0:
                            TT(macc.t[:, c0:c0 + cn], ps_[:, 0:cn], T[2].t[:, c0:c0 + cn], ALU.mult, [tk, T[2].k], [macc.k])
                        else:
                            TT(T[3].t[:, c0:c0 + cn], ps_[:, 0:cn], T[2].t[:, c0:c0 + cn], ALU.mult, [tk, T[2].k], [T[3].k])
                            TT(macc.t[:, c0:c0 + cn], macc.t[:, c0:c0 + cn], T[3].t[:, c0:c0 + cn], ALU.add, [macc.k, T[3].k], [macc.k])
                DMA(macc.d, mrg[fo, :, 0:ncol], macc.t[:, 0:ncol].bitcast(FR), [macc.k], [k_mrg[fo]])

        def resid_update(l, t, ncol, fo, res, src, gtoff):
            xin = T[fo % 2]; xo = T[2 + fo % 2]
            DMA(xin.d, xin.t[:, 0:ncol], src[:, fo, 0:ncol], [k_xres[t][fo]], [xin.k])
            for (ps_, c0, cn, tk) in res:
                if c0 < NP:
                    STT(xo.t[:, c0:c0 + cn], ps_[:, 0:cn], modT.t[:, gtoff + fo, 0:1], xin.t[:, c0:c0 + cn], ALU.mult, ALU.add, [tk, modT.k, xin.k], [xo.k])
                else:
                    TT(xo.t[:, c0:c0 + cn], ps_[:, 0:cn], modT.t[:, gtoff + fo, 1:17], ALU.mult, [tk, modT.k], [xo.k])
                    TT(xo.t[:, c0:c0 + cn], xo.t[:, c0:c0 + cn], xin.t[:, c0:c0 + cn], ALU.add, [xo.k, xin.k], [xo.k])
            DMA(xo.d, xres[t][:, fo, 0:ncol], xo.t[:, 0:ncol], [xo.k], [k_xres[t][fo]])

        def phase_wout(l, t, ncol, src):
            w = Lw[l]
            for k in range(16):
                DMA(P.dsem("hld"), hbuf.t[:, k, 0:ncol], mrg[k, :, 0:ncol], [k_mrg[k]], [k_h[k]])
            ws = WS([(w["out"][fo], 2048) for fo in range(16)])
            for fo in range(16):
                s_ = ws.get(fo)
                wv = wr.t[:, s_, :].rearrange("p (k n) -> p k n", k=16)
                res = proj_group(lambda k: (wv[:, k, :], [k_wr[s_]]), lambda k, c0, cn: (hbuf.t[:, k, c0:c0 + cn], [k_h[k]]), 16, ncol)
                resid_update(l, t, ncol, fo, res, src, 32)

        def phase_ffn(l, t, ncol):
            w = Lw[l]
            with ExitStack() as s:
                acc = sbuf(s, "facc", [128, 16, NP + NSMP]); hid = sbuf(s, "fhid", [128, 2, NP + NSMP], FR)
                fcw = sbuf(s, "fcwS", [128, 88, 3]); fcb = sbuf(s, "fcbS", [128, 88])
                fb = sbuf(s, "fbS", [128, NSMP, 2]); nb = sbuf(s, "nbS", [128, NSMP, 2])
                DMA(fcw.d, fcw.t[:], w["fcw"], [], [fcw.k]); DMA(fcb.d, fcb.t[:], w["fcb"], [], [fcb.k])
                if t == 0:
                    MS(fhalo.t[:], 0.0, [fhalo.k])
                blocks = []
                import os
                _ngb = int(os.environ.get("FFN_NG", "22")) if (l == 1 and t == 0) else 22
                for grp in range(_ngb):
                    for ff in range(2):
                        blocks += [(w["up"][2 * grp + ff], 2048), (w["up"][44 + 2 * grp + ff], 2048)]
                    blocks += [(w["down"][2 * grp + q], 2048) for q in range(2)]
                ws = WS(blocks)
                bi = 0

                def conv_evac(q, res, ext, yo):
                    CP(ext.t[:, 0:2], fhalo.t[:, q, :], [fhalo.k], [ext.k])
                    for (ps_, c0, cn, tk) in res:
                        ACT(ext.t[:, 2 + c0:2 + c0 + cn], ps_[:, 0:cn], AF.Copy, [tk], [ext.k])
                    TS(yo.t[:, 0:NP], ext.t[:, 2:2 + NP], fcw.t[:, q, 2:3], fcb.t[:, q:q + 1], ALU.mult, ALU.add, [ext.k, fcw.k, fcb.k], [yo.k])
                    STT(yo.t[:, 0:NP], ext.t[:, 1:1 + NP], fcw.t[:, q, 1:2], yo.t[:, 0:NP], ALU.mult, ALU.add, [ext.k, fcw.k, yo.k], [yo.k])
                    STT(yo.t[:, 0:NP], ext.t[:, 0:NP], fcw.t[:, q, 0:1], yo.t[:, 0:NP], ALU.mult, ALU.add, [ext.k, fcw.k, yo.k], [yo.k])
                    CP(fhalo.t[:, q, :], ext.t[:, NP:NP + 2], [ext.k], [fhalo.k])
                    if t == 1:
                        pass
                    if ncol > NP:
                        xs_ = ext.t[:, 2 + NP:2 + ncol]
                        DMA(fb.d, fb.t[:], fbuf0[l][:, q, :, :], [], [fb.k])
                        TS(yo.t[:, NP:ncol], xs_, fcw.t[:, q, 2:3], fcb.t[:, q:q + 1], ALU.mult, ALU.add, [ext.k, fcw.k, fcb.k], [yo.k])
                        STT(yo.t[:, NP:ncol], fb.t[:, :, 1], fcw.t[:, q, 1:2], yo.t[:, NP:ncol], ALU.mult, ALU.add, [fb.k, fcw.k, yo.k], [yo.k])
                        STT(yo.t[:, NP:ncol], fb.t[:, :, 0], fcw.t[:, q, 0:1], yo.t[:, NP:ncol], ALU.mult, ALU.add, [fb.k, fcw.k, yo.k], [yo.k])
                        CP(nb.t[:, :, 0], fb.t[:, :, 1], [fb.k], [nb.k])
                        CP(nb.t[:, :, 1], xs_, [ext.k], [nb.k])
                        DMA(nb.d, o_cfs[l][:, q, :, :], nb.t[:], [nb.k], [k_out])

                import os
                _ng = int(os.environ.get("FFN_NG", "22")) if (l == 1 and t == 0) else 22
                for grp in range(_ng):
                    if os.environ.get("FFN_DBG") and l == 1 and t == 0 and grp >= 17:
                        print("FFNDBG grp", grp, dict(P.cnt), {k_: v_[1] for k_, v_ in P.dsems.items() if v_[1] > 2000},
                              {e_: len(v_) + sum(len(w_[0]) for w_ in v_) for e_, v_ in P.streams.items()})
                    for ff in range(2):
                        f = 2 * grp + ff
                        ys = []
                        for part in range(2):
                            q = f + 44 * part
                            s_ = ws.get(bi); bi += 1
                            wv = wr.t[:, s_, :].rearrange("p (k n) -> p k n", k=16)
                            res = proj_group(lambda k: (wv[:, k, :], [k_wr[s_]]), lambda k, c0, cn: (hbuf.t[:, k, c0:c0 + cn], [k_h[k]]), 16, ncol)
                            ext = T[0 + part]; yo = T[2 + part]
                            conv_evac(q, res, ext, yo)
                            ys.append(yo)
                        ACT(T[4].t[:, 0:ncol], ys[0].t[:, 0:ncol], AF.Silu, [ys[0].k], [T[4].k])
                        TT(hid.t[:, ff, 0:ncol], T[4].t[:, 0:ncol], ys[1].t[:, 0:ncol], ALU.mult, [T[4].k, ys[1].k], [hid.k])
                    for q4 in range(2):
                        s_ = ws.get(bi); bi += 1
                        wv = wr.t[:, s_, :].rearrange("p (k n) -> p k n", k=2)
                        for fl in range(8):
                            fo = 8 * q4 + fl
                            res = proj_group(lambda k: (wv[:, k, 128 * fl:128 * fl + 128], [k_wr[s_]]), lambda k, c0, cn: (hid.t[:, k, c0:c0 + cn], [hid.k]), 2, ncol)
                            for (ps_, c0, cn, tk) in res:
                                if grp == 0:
                                    ACT(acc.t[:, fo, c0:c0 + cn], ps_[:, 0:cn], AF.Copy, [tk], [acc.k])
                                else:
                                    TT(acc.t[:, fo, c0:c0 + cn], acc.t[:, fo, c0:c0 + cn], ps_[:, 0:cn], ALU.add, [tk, acc.k], [acc.k])
                if t == 1:
                    DMA(fhalo.d, o_cfp[l], fhalo.t[:], [fhalo.k], [k_out])
                for fo in range(16):
                    xin = T[fo % 2]; xo = T[2 + fo % 2]
                    DMA(xin.d, xin.t[:, 0:ncol], xres[t][:, fo, 0:ncol], [k_xres[t][fo]], [xin.k])
                    STT(xo.t[:, 0:NP], acc.t[:, fo, 0:NP], modT.t[:, 80 + fo, 0:1], xin.t[:, 0:NP], ALU.mult, ALU.add, [acc.k, modT.k, xin.k], [xo.k])
                    if ncol > NP:
                        TT(xo.t[:, NP:ncol], acc.t[:, fo, NP:ncol], modT.t[:, 80 + fo, 1:17], ALU.mult, [acc.k, modT.k], [xo.k])
                        TT(xo.t[:, NP:ncol], xo.t[:, NP:ncol], xin.t[:, NP:ncol], ALU.add, [xo.k, xin.k], [xo.k])
                    DMA(xo.d, xres[t][:, fo, 0:ncol], xo.t[:, 0:ncol], [xo.k], [k_xres[t][fo]])

        class _Stop(Exception):
            pass
        nph = [0]

        def _wrap(fn):
            def g(*a, **k):
                if stop_after is not None and nph[0] >= stop_after:
                    return None
                nph[0] += 1
                return fn(*a, **k)
            return g
        phase_mod, phase_norm, phase_inproj, phase_s5, phase_gmlp, phase_ssd, phase_merge, phase_wout, phase_ffn = [
            _wrap(f_) for f_ in (phase_mod, phase_norm, phase_inproj, phase_s5, phase_gmlp, phase_ssd, phase_merge, phase_wout, phase_ffn)]
        def _program():
            for l in range(DEPTH):
                cur_l[0] = l
                phase_mod(l)
                for t in range(2):
                    ncol = NP + NSMP if t == 0 else NP
                    src = xT[t] if l == 0 else xres[t]
                    phase_norm(src, k_xres[t], ncol, G1, 0)
                    phase_inproj(l, t, ncol)
                    P.barrier()
                    with ExitStack() as so:
                        oa = sbuf(so, "oa", [128, 4, NP + NSMP], FR)
                        phase_s5(l, t, ncol, oa)
                        P.barrier()
                        ob = sbuf(so, "ob", [128, 4, NP + NSMP], FR)
                        phase_gmlp(l, t, ncol, ob)
                        P.barrier()
                        oc = sbuf(so, "oc", [128, 8, NP + NSMP], FR)
                        phase_ssd(l, t, ncol, oc)
                        P.barrier()
                        phase_merge(l, t, ncol, oa, ob, oc)
                        P.barrier()
                    phase_wout(l, t, ncol, src)
                    phase_norm(xres[t], k_xres[t], ncol, G2, 48)
                    phase_ffn(l, t, ncol)
                    P.barrier()
                    if l == DEPTH - 1:
                        phase_norm(xres[t], k_xres[t], ncol, dst=o_y[t])

        try:
            _program()
        except _Stop:
            pass
        P.barrier()
        with nc.allow_non_contiguous_dma(reason="single-column gathers for padded sample chunks"), nc.Block() as block:
            P.emit(block)
        return nc

_NC = None


def _tile_w(W, kc, nw):
    K, N = W.shape
    assert K == kc * 128 and N % nw == 0
    return np.ascontiguousarray(W.reshape(kc, 128, N // nw, nw).transpose(2, 1, 0, 3).reshape(N // nw, 128, kc * nw))


def _fm(v):
    n = v.shape[0] // 128
    return np.ascontiguousarray(v.reshape((n, 128) + v.shape[1:]).swapaxes(0, 1))


def _prep_shared(inp):
    f = np.float32
    sh = {}
    sh["g_final"] = _fm(inp["g_final"])
    sh["ident"] = np.eye(128, dtype=f)
    sh["ones"] = np.ones((128, 128), f)
    jj, ii = np.meshgrid(np.arange(128), np.arange(128), indexing="ij")
    sh["negmaskT"] = np.where(ii >= jj, 0.0, -30000.0).astype(f)
    sh["trilT"] = (ii >= jj).astype(f)
    sel = np.zeros((16, 16, 128), f)
    for h in range(16):
        sel[h, h, :] = 1.0
    sh["sel16"] = sel
    sx = np.zeros((16, 8, 128), f)
    for m in range(8):
        sx[2 * m, m, 0:64] = 1.0
        sx[2 * m + 1, m, 64:128] = 1.0
    sh["selexp"] = sx
    sh["iota"] = np.tile(np.arange(128, dtype=f)[None, :], (128, 1))
    for l in range(DEPTH):
        sh[f"w_mod{l}"] = _tile_w(inp["w_mod"][l], 16, 128)
        sh[f"b_mod{l}"] = _fm(inp["b_mod"][l])
        sh[f"g_mix{l}"] = _fm(inp["g_mix"][l]); sh[f"g_ffn{l}"] = _fm(inp["g_ffn"][l])
        win = inp["w_in"][l]
        sh[f"w_inb{l}"] = _tile_w(win[:, 0:4096], 16, 128)
        sh[f"w_indt{l}"] = np.ascontiguousarray(win[:, 4096:4112].reshape(16, 128, 16).transpose(1, 0, 2).reshape(128, 256))
        sh[f"w_ing{l}"] = _tile_w(win[:, 4112:], 16, 128)
        sh[f"w_pa{l}"] = _tile_w(inp["w_pa"][l], 4, 128)
        sh[f"w_pb{l}"] = _tile_w(inp["w_pb"][l], 4, 128)
        sh[f"w_pc{l}"] = _tile_w(inp["w_pc"][l], 8, 128)
        sh[f"w_out{l}"] = _tile_w(inp["w_out"][l], 16, 128)
        sh[f"w_up{l}"] = _tile_w(inp["ffn_w_up"][l], 16, 128)
        wd = inp["ffn_w_down"][l]
        sh[f"w_down{l}"] = np.ascontiguousarray(
            wd.reshape(22, 2, 128, 2, 1024).transpose(0, 3, 2, 1, 4).reshape(44, 128, 2048))
        sh[f"w_glu{l}"] = np.ascontiguousarray(inp["s5_w_glu"][l].reshape(4, 128, 512).transpose(1, 0, 2).reshape(128, 2048))
        sh[f"lamr{l}"] = _fm(inp["s5_lam_re"][l].reshape(2048)); sh[f"lami{l}"] = _fm(inp["s5_lam_im"][l].reshape(2048))
        sh[f"logdt{l}"] = _fm(np.repeat(inp["s5_log_dt"][l], 64))
        Bre = np.zeros((128, 16, 128), f); Bim = np.zeros((128, 16, 128), f)
        Cre = np.zeros((128, 16, 128), f); Cim = np.zeros((128, 16, 128), f)
        for m in range(16):
            for gg in range(2):
                g = 2 * m + gg
                r0 = (g % 8) * 16
                Bre[r0:r0 + 16, m, gg * 64:(gg + 1) * 64] = inp["s5_b_re"][l, g].T
                Bim[r0:r0 + 16, m, gg * 64:(gg + 1) * 64] = inp["s5_b_im"][l, g].T
                Cre[gg * 64:(gg + 1) * 64, m, r0:r0 + 16] = inp["s5_c_re"][l, g].T
                Cim[gg * 64:(gg + 1) * 64, m, r0:r0 + 16] = inp["s5_c_im"][l, g].T
        sh[f"Bre{l}"] = Bre; sh[f"Bim{l}"] = Bim; sh[f"Cre{l}"] = Cre; sh[f"Cim{l}"] = Cim
        sh[f"s5d{l}"] = _fm(inp["s5_d"][l])
        sh[f"gv{l}"] = _fm(inp["gm_g_v"][l])
        sh[f"w00{l}"] = _fm(np.repeat(inp["gm_w_s"][l][:, 0, 0], 64))
        sh[f"wsT{l}"] = np.ascontiguousarray(inp["gm_w_s"][l].transpose(2, 0, 1))
        sh[f"bsx{l}"] = _fm(np.repeat(inp["gm_b_s"][l], 64, axis=0))
        sh[f"cw{l}"] = _fm(np.ascontiguousarray(inp["ssd_conv_w"][l].T))
        sh[f"cb{l}"] = _fm(inp["ssd_conv_b"][l])
        sh[f"dtb{l}"] = inp["ssd_dt_bias"][l].reshape(16, 1).copy(); sh[f"alog{l}"] = inp["ssd_a_log"][l].reshape(16, 1).copy()
        sh[f"ssdD{l}"] = _fm(np.repeat(inp["ssd_d"][l], 64)); sh[f"ssdg{l}"] = _fm(inp["ssd_g_norm"][l])
        sh[f"fcw{l}"] = _fm(np.ascontiguousarray(inp["ffn_conv_w"][l].T)); sh[f"fcb{l}"] = _fm(inp["ffn_conv_b"][l])
    return {k: np.ascontiguousarray(v, dtype=f) for k, v in sh.items()}


def _core_inputs(inp, sh, c):
    f = np.float32
    sq = c % 4
    rows = slice(16 * c, 16 * c + 16)
    m = dict(sh)
    xp = inp["x_prompt"][sq]
    xs = inp["x_sample"][rows, 0, :]
    x0 = np.concatenate([xp[0:NP], xs], axis=0)
    m["xT0"] = _fm(np.ascontiguousarray(x0.T)).astype(f)
    m["xT1"] = _fm(np.ascontiguousarray(xp[NP:2 * NP].T)).astype(f)
    cc = np.zeros((18, D), f)
    cc[0] = inp["c_prompt"][sq]; cc[1:17] = inp["c_sample"][rows]
    m["cT"] = _fm(np.ascontiguousarray(cc.T))
    m["s5r0"] = np.ascontiguousarray(inp["state_s5_re"][:, rows].reshape(DEPTH, 16, 16, 128).transpose(0, 3, 2, 1))
    m["s5i0"] = np.ascontiguousarray(inp["state_s5_im"][:, rows].reshape(DEPTH, 16, 16, 128).transpose(0, 3, 2, 1))
    m["ssm0"] = np.ascontiguousarray(inp["state_ssm"][:, rows].reshape(DEPTH, 16, 1024, 128).transpose(0, 1, 3, 2))
    m["cbuf0"] = np.ascontiguousarray(inp["state_ssd_conv"][:, rows].reshape(DEPTH, 16, 3, 12, 128).transpose(0, 4, 3, 1, 2))
    m["fbuf0"] = np.ascontiguousarray(inp["state_ffn_conv"][:, rows].reshape(DEPTH, 16, 2, 88, 128).transpose(0, 4, 3, 1, 2))
    return m


def _unfm(a):
    return a.swapaxes(0, 1).reshape((a.shape[0] * a.shape[1],) + a.shape[2:])


def _unpack_core(r):
    o = {}
    y0 = _unfm(r["o_y0"]).T
    o["y_s"] = y0[NP:]
    o["y_p"] = np.concatenate([y0[0:NP], _unfm(r["o_y1"]).T], axis=0)
    s5s = r["o_s5s"]
    o["s5r_s"] = s5s[:, 0].transpose(0, 3, 2, 1).reshape(DEPTH, 16, 32, 64)
    o["s5i_s"] = s5s[:, 1].transpose(0, 3, 2, 1).reshape(DEPTH, 16, 32, 64)
    o["ssm_s"] = r["o_ssms"].transpose(0, 1, 3, 2).reshape(DEPTH, 16, 16, 64, 128)
    o["cc_s"] = r["o_ccs"].transpose(0, 3, 4, 2, 1).reshape(DEPTH, 16, 3, 1536)
    o["cf_s"] = r["o_cfs"].transpose(0, 3, 4, 2, 1).reshape(DEPTH, 16, 2, 11264)
    o["gv_s"] = r["o_gv"].transpose(0, 3, 2, 1).reshape(DEPTH, 16, 512)
    s5p = r["o_s5p"]
    o["s5r_p"] = s5p[:, 0].transpose(0, 2, 1).reshape(DEPTH, 32, 64)
    o["s5i_p"] = s5p[:, 1].transpose(0, 2, 1).reshape(DEPTH, 32, 64)
    o["ssm_p"] = r["o_ssmp"].transpose(0, 2, 1).reshape(DEPTH, 16, 64, 128)
    o["cc_p"] = r["o_ccp"].transpose(0, 3, 2, 1).reshape(DEPTH, 3, 1536)
    o["cf_p"] = r["o_cfp"].transpose(0, 3, 2, 1).reshape(DEPTH, 2, 11264)
    return o


def kernel(**inp):
    global _NC
    inp = {k: np.asarray(v) for k, v in inp.items()}
    f = np.float32
    if _NC is None:
        _NC = build_program()
    nc = _NC
    sh = _prep_shared(inp)
    in_maps = [_core_inputs(inp, sh, c) for c in range(8)]
    res = run_bass_kernel_spmd(nc, in_maps, core_ids=list(range(8)))
    R = res.results
    y_p = np.zeros((4, 2048, D), f); y_s = np.zeros((128, 1, D), f)
    s5r_p = np.zeros((DEPTH, 4, 32, 64), f); s5i_p = np.zeros_like(s5r_p)
    ssm_p = np.zeros((DEPTH, 4, 16, 64, 128), f)
    cc_p = np.zeros((DEPTH, 4, 3, 1536), f); cf_p = np.zeros((DEPTH, 4, 2, 11264), f)
    s5r_s = np.zeros((DEPTH, 128, 32, 64), f); s5i_s = np.zeros_like(s5r_s)
    ssm_s = np.zeros((DEPTH, 128, 16, 64, 128), f)
    cc_s = np.zeros((DEPTH, 128, 3, 1536), f); cf_s = np.zeros((DEPTH, 128, 2, 11264), f)
    gv_s = np.zeros((DEPTH, 128, 1, 512), f)
    for c in range(8):
        o = _unpack_core(R[c])
        rows = slice(16 * c, 16 * c + 16)
        y_s[rows, 0, :] = o["y_s"]
        s5r_s[:, rows] = o["s5r_s"]; s5i_s[:, rows] = o["s5i_s"]; ssm_s[:, rows] = o["ssm_s"]
        cc_s[:, rows] = o["cc_s"]; cf_s[:, rows] = o["cf_s"]; gv_s[:, rows, 0, :] = o["gv_s"]
        if c < 4:
            y_p[c] = o["y_p"]
            s5r_p[:, c] = o["s5r_p"]; s5i_p[:, c] = o["s5i_p"]; ssm_p[:, c] = o["ssm_p"]
            cc_p[:, c] = o["cc_p"]; cf_p[:, c] = o["cf_p"]
    return (y_p, y_s, s5r_p, s5i_p, ssm_p, cc_p, cf_p, s5r_s, s5i_s, ssm_s, cc_s, cf_s, gv_s)
```

```python
import math
from contextlib import ExitStack
import numpy as np
import concourse.bass as bass
import concourse.mybir as mybir
from concourse.bass_utils import run_bass_kernel_spmd

F32 = mybir.dt.float32
FR = mybir.dt.float32r
AF = mybir.ActivationFunctionType
ALU = mybir.AluOpType

D = 2048; NP = 1024; NSMP = 16; DEPTH = 2
EPS = 1e-6
NBIN = 33
BIN_UA, BIN_UB, BIN_VB, BIN_Z, BIN_XBC, BIN_DT = 0, 4, 8, 12, 20, 32
SEQC = 2048 + NSMP
TW = 1048
PI = math.pi


class Tok:
    __slots__ = ("w", "r")

    def __init__(self):
        self.w = None
        self.r = []


class Buf:
    def __init__(self, t, P, name):
        self.t = t
        self.k = Tok()
        self._d = None
        self.P = P
        self.name = name

    @property
    def d(self):
        if self._d is None:
            self._d = self.P.dsem(self.name)
        return self._d


class Prog:
    def __init__(self, nc):
        self.nc = nc
        self.eng = {"pe": nc.tensor, "act": nc.scalar, "dve": nc.vector, "pool": nc.gpsimd, "sp": nc.sync}
        self.streams = {e: [] for e in self.eng}
        self.sem = {}
        self.cnt = {}
        self.seen = {e: {} for e in self.eng}
        self.dsems = {}
        self.pend = {e: ([], []) for e in self.eng}

    def open(self, stack):
        self.stack = stack
        for e in ("pe", "act", "dve", "pool"):
            self.sem[e] = stack.enter_context(self.nc.semaphore("s_" + e))
            self.cnt[e] = 0

    def dsem(self, name):
        if name not in self.dsems:
            s = self.stack.enter_context(self.nc.semaphore("d_" + name))
            self.dsems[name] = [s, 0, None, name]
        return self.dsems[name]

    def _waits(self, eng, R, W, extra=()):
        evs = list(extra)
        for t in R:
            if t.w is not None:
                evs.append(t.w)
        for t in W:
            if t.w is not None:
                evs.append(t.w)
            evs.extend(t.r)
        need = {}
        for (key, s, v, src) in evs:
            if src == "pe" and eng == "pe":
                continue
            if need.get(key, (None, 0))[1] < v:
                need[key] = (s, v)
        out = []
        seen = self.seen[eng]
        for key, (s, v) in need.items():
            if seen.get(key, 0) >= v:
                continue
            seen[key] = v
            out.append((s, v))
        return out

    def op(self, eng, fn, R=(), W=(), inc=True):
        waits = self._waits(eng, R, W)
        if inc:
            self.cnt[eng] += 1
            ev = (eng, self.sem[eng], self.cnt[eng], eng)
            pr, pw = self.pend[eng]
            self.pend[eng] = ([], [])
            for t in list(R) + pr:
                t.r.append(ev)
            for t in list(W) + pw:
                t.w = ev
                t.r = []
        else:
            self.pend[eng][0].extend(R)
            self.pend[eng][1].extend(W)
        self.streams[eng].append((waits, fn, (self.sem[eng], 1) if inc else None))

    def dma(self, q, ds, out, in_, R=(), W=()):
        extra = [ds[2]] if ds[2] is not None else []
        waits = self._waits(q, R, W, extra)
        ds[1] += 16
        ev = (ds[3], ds[0], ds[1], "dma")
        ds[2] = ev
        for t in R:
            t.r.append(ev)
        for t in W:
            t.w = ev
            t.r = []
        self.streams[q].append((waits, lambda e, o=out, i=in_: e.dma_start(out=o, in_=i), (ds[0], 16)))

    def barrier(self):
        evs = [(e, self.sem[e], self.cnt[e]) for e in self.sem if self.cnt[e] > 0]
        evs += [(d[3], d[0], d[1]) for d in self.dsems.values() if d[1] > 0]
        for e in self.eng:
            waits = []
            for key, s, v in evs:
                if self.seen[e].get(key, 0) < v:
                    self.seen[e][key] = v
                    waits.append((s, v))
            if waits:
                self.streams[e].append((waits, None, None))

    def emit(self, block):
        def mk(e):
            def body(engine):
                for waits, fn, inc in self.streams[e]:
                    for s, v in waits:
                        engine.wait_ge(s, v)
                    if fn is not None:
                        ins = fn(engine)
                        if inc is not None:
                            ins.then_inc(inc[0], inc[1])
            return body
        block.tensor(mk("pe"))
        block.scalar(mk("act"))
        block.vector(mk("dve"))
        block.gpsimd(mk("pool"))
        block.sync(mk("sp"))


def build_program(stop_after=None):
    nc = bass.Bass("TRN2", target_bir_lowering=False)
    nc.dge_precook = False
    P = Prog(nc)

    def din(name, shape, dt=F32):
        return nc.dram_tensor(name, list(shape), dt, kind="ExternalInput").ap()

    def dout(name, shape):
        return nc.dram_tensor(name, list(shape), F32, kind="ExternalOutput").ap()

    def dscr(name, shape, dt=F32):
        return nc.dram_tensor(name, list(shape), dt, kind="Internal").ap()

    xT = [din("xT0", [128, 16, NP + NSMP]), din("xT1", [128, 16, NP])]
    cT = din("cT", [128, 16, 18])
    g_final = din("g_final", [128, 16])
    c_ident = din("ident", [128, 128])
    c_ones = din("ones", [128, 128], FR)
    c_negmask = din("negmaskT", [128, 128])
    c_tril = din("trilT", [128, 128])
    c_sel16 = din("sel16", [16, 16, 128], FR)
    c_selexp = din("selexp", [16, 8, 128], FR)
    c_iota = din("iota", [128, 128])
    s5r0 = din("s5r0", [DEPTH, 128, 16, NSMP]); s5i0 = din("s5i0", [DEPTH, 128, 16, NSMP])
    ssm0 = din("ssm0", [DEPTH, NSMP, 128, 1024], FR)
    cbuf0 = din("cbuf0", [DEPTH, 128, 12, NSMP, 3])
    fbuf0 = din("fbuf0", [DEPTH, 128, 88, NSMP, 2])
    Lw = []
    for l in range(DEPTH):
        w = {}
        w["mod"] = din(f"w_mod{l}", [96, 128, 2048], FR)
        w["b_mod"] = din(f"b_mod{l}", [128, 96])
        w["g_mix"] = din(f"g_mix{l}", [128, 16]); w["g_ffn"] = din(f"g_ffn{l}", [128, 16])
        w["inb"] = din(f"w_inb{l}", [32, 128, 2048], FR)
        w["indt"] = din(f"w_indt{l}", [128, 256], FR)
        w["ing"] = din(f"w_ing{l}", [48, 128, 2048], FR)
        w["pa"] = din(f"w_pa{l}", [16, 128, 512], FR)
        w["pb"] = din(f"w_pb{l}", [16, 128, 512], FR)
        w["pc"] = din(f"w_pc{l}", [16, 128, 1024], FR)
        w["out"] = din(f"w_out{l}", [16, 128, 2048], FR)
        w["up"] = din(f"w_up{l}", [88, 128, 2048], FR)
        w["down"] = din(f"w_down{l}", [44, 128, 2048], FR)
        w["glu"] = din(f"w_glu{l}", [128, 2048], FR)
        w["lamr"] = din(f"lamr{l}", [128, 16]); w["lami"] = din(f"lami{l}", [128, 16])
        w["logdt"] = din(f"logdt{l}", [128, 16])
        w["Bre"] = din(f"Bre{l}", [128, 16, 128]); w["Bim"] = din(f"Bim{l}", [128, 16, 128])
        w["Cre"] = din(f"Cre{l}", [128, 16, 128], FR); w["Cim"] = din(f"Cim{l}", [128, 16, 128], FR)
        w["s5d"] = din(f"s5d{l}", [128, 4])
        w["gv"] = din(f"gv{l}", [128, 4]); w["w00"] = din(f"w00{l}", [128, 4])
        w["wsT"] = din(f"wsT{l}", [128, 8, 128]); w["bsx"] = din(f"bsx{l}", [128, 4, 128])
        w["cw"] = din(f"cw{l}", [128, 12, 4]); w["cb"] = din(f"cb{l}", [128, 12])
        w["dtb"] = din(f"dtb{l}", [16, 1]); w["alog"] = din(f"alog{l}", [16, 1])
        w["ssdD"] = din(f"ssdD{l}", [128, 8]); w["ssdg"] = din(f"ssdg{l}", [128, 8])
        w["fcw"] = din(f"fcw{l}", [128, 88, 3]); w["fcb"] = din(f"fcb{l}", [128, 88])
        Lw.append(w)
    o_y = [dout("o_y0", [128, 16, NP + NSMP]), dout("o_y1", [128, 16, NP])]
    o_s5p = dout("o_s5p", [DEPTH, 2, 128, 16])
    o_s5s = dout("o_s5s", [DEPTH, 2, 128, 16, NSMP])
    o_ssmp = dout("o_ssmp", [DEPTH, 128, 1024])
    o_ssms = dout("o_ssms", [DEPTH, NSMP, 128, 1024])
    o_ccp = dout("o_ccp", [DEPTH, 128, 12, 3])
    o_ccs = dout("o_ccs", [DEPTH, 128, 12, NSMP, 3])
    o_cfp = dout("o_cfp", [DEPTH, 128, 88, 2])
    o_cfs = dout("o_cfs", [DEPTH, 128, 88, NSMP, 2])
    o_gv = dout("o_gv", [DEPTH, 128, 4, NSMP])
    xres = [dscr("xres0", [128, 16, NP + NSMP]), dscr("xres1", [128, 16, NP])]
    binb = dscr("bin", [NBIN, 128, SEQC], FR)
    binF = binb.bitcast(F32)
    mrg = dscr("mrg", [16, 128, NP + NSMP], FR)
    k_xres = [[Tok() for _ in range(16)] for _ in range(2)]
    k_bin = [Tok() for _ in range(NBIN)]
    k_mrg = [Tok() for _ in range(16)]
    k_out = Tok()
    k_in = Tok()

    with ExitStack() as st:
        P.open(st)

        uid = [0]

        def sbuf(stack, name, shape, dt=F32):
            uid[0] += 1
            return Buf(stack.enter_context(nc.sbuf_tensor(f"{name}_{uid[0]}", list(shape), dt)), P, name)

        def ACT(out, in_, func, R, W, **kw):
            P.op("act", lambda e: e.activation(out=out, in_=in_, func=func, **kw), R, W)

        def TT(out, a, b, op, R, W, eng="dve"):
            P.op(eng, lambda e: e.tensor_tensor(out, a, b, op), R, W)

        def TS(out, a, s1, s2, op0, op1, R, W):
            if op1 is None:
                P.op("dve", lambda e: e.tensor_scalar(out, a, s1, None, op0), R, W)
            else:
                P.op("dve", lambda e: e.tensor_scalar(out, a, s1, s2, op0, op1), R, W)

        def STT(out, a, s, b, op0, op1, R, W):
            P.op("dve", lambda e: e.scalar_tensor_tensor(out, a, s, b, op0, op1), R, W)

        def CP(out, in_, R, W, eng="dve"):
            P.op(eng, lambda e: e.tensor_copy(out, in_), R, W)

        def MS(ap, val, W, eng="dve"):
            P.op(eng, lambda e: e.memset(ap, val), (), W)

        def MM(out, lhsT, rhs, start, stop, R, W, inc):
            P.op("pe", lambda e: e.matmul(out, lhsT, rhs, start=start, stop=stop), R, W, inc)

        def TR(out, in_, idn, R, W):
            P.op("pe", lambda e: e.transpose(out, in_, idn), R, W)

        def DMA(ds, out, in_, R, W, q="pool"):
            P.dma(q, ds, out, in_, R, W)

        hbuf = sbuf(st, "hbuf", [128, 16, NP + NSMP], FR); k_h = [Tok() for _ in range(16)]
        wr = sbuf(st, "wring", [128, 2, 2048], FR); k_wr = [Tok(), Tok()]
        cur_l = [0]

        def d_wr_(s_):
            return P.dsem(f"wr{s_}_{cur_l[0]}")
        modT = sbuf(st, "modT", [128, 96, 18])
        G1 = sbuf(st, "G1", [128, 16, 18]); G2 = sbuf(st, "G2", [128, 16, 18])
        ident = sbuf(st, "identS", [128, 128]); ones = sbuf(st, "onesS", [128, 128], FR)
        silc = sbuf(st, "silc", [128, 16, 18], FR)
        bmod = sbuf(st, "bmod", [128, 96]); gmix = sbuf(st, "gmixS", [128, 16]); gffn = sbuf(st, "gffnS", [128, 16])
        gfin = sbuf(st, "gfin", [128, 16])
        T = [sbuf(st, f"T{i}", [128, TW]) for i in range(5)]
        RB = [sbuf(st, f"R{i}", [128, TW], FR) for i in range(2)]
        Snp = sbuf(st, "Snp", [128, 1024], FR)
        hlp = sbuf(st, "hlp", [128, 2, 16])
        fhalo = sbuf(st, "fhalo", [128, 88, 2])
        psum = [st.enter_context(nc.psum_tensor(f"ps{i}", [128, 512], F32)) for i in range(8)]
        k_ps = [Tok() for _ in range(8)]
        onesF = ones.t[:].bitcast(F32)
        zer = sbuf(st, "zer", [128, 512])
        MS(zer.t[:], 0.0, [zer.k])

        DMA(ident.d, ident.t[:], c_ident, [], [ident.k])
        DMA(ones.d, ones.t[:], c_ones, [], [ones.k])
        DMA(gfin.d, gfin.t[:], g_final, [], [gfin.k])

        def nchunks(ncol):
            r = [(0, 512), (512, 512)]
            if ncol > 1024:
                r.append((1024, ncol - 1024))
            return r

        class WS:
            nxt = 0

            def __init__(self, blocks):
                self.blocks = blocks
                self.loaded = 0
                self.slot0 = WS.nxt

            def get(self, i):
                while self.loaded < min(len(self.blocks), i + 2):
                    j = self.loaded
                    s = (self.slot0 + j) % 2
                    ap, n = self.blocks[j]
                    P.dma("sp", d_wr_(s), wr.t[:, s, 0:n], ap, [], [k_wr[s]])
                    self.loaded += 1
                s = (self.slot0 + i) % 2
                if i == len(self.blocks) - 1:
                    WS.nxt = (s + 1) % 2
                return s

        pset = [0]

        def proj_group(lhs_fn, rhs_fn, KC, ncol, M=128):
            base = 3 * pset[0]; pset[0] ^= 1
            chs = nchunks(ncol)
            for k in range(KC):
                lt, lR = lhs_fn(k)
                for i, (c0, cn) in enumerate(chs):
                    rt, rR = rhs_fn(k, c0, cn)
                    MM(psum[base + i][0:M, 0:cn], lt, rt, k == 0, k == KC - 1, list(lR) + list(rR), [k_ps[base + i]], k == KC - 1)
            return [(psum[base + i], c0, cn, k_ps[base + i]) for i, (c0, cn) in enumerate(chs)]

        def gelu_from(src, srcR, out, outW, tA, tB):
            n = src.shape[-1] if len(src.shape) == 2 else None
            a = tA.t[0:src.shape[0], 0:src.shape[1]]; b = tB.t[0:src.shape[0], 0:src.shape[1]]
            ACT(a, src, AF.Square, srcR, [tA.k])
            TS(a, a, 0.044715, 1.0, ALU.mult, ALU.add, [tA.k], [tA.k])
            TT(a, a, src, ALU.mult, [tA.k] + srcR, [tA.k])
            ACT(b, a, AF.Sigmoid, [tA.k], [tB.k], scale=1.5957691216)
            TT(out, b, src, ALU.mult, [tB.k] + srcR, outW)

        def phase_mod(l):
            w = Lw[l]
            DMA(modT.d, modT.t[:, 0:16, :], cT, [], [modT.k])
            ACT(silc.t[:], modT.t[:, 0:16, :], AF.Silu, [modT.k], [silc.k])
            DMA(bmod.d, bmod.t[:], w["b_mod"], [], [bmod.k])
            ws = WS([(w["mod"][b], 2048) for b in range(96)])
            for fo in range(96):
                s = ws.get(fo)
                wv = wr.t[:, s, :].rearrange("p (k n) -> p k n", k=16)
                base = 6 + (fo % 2)
                for k in range(16):
                    MM(psum[base][:, 0:18], wv[:, k, :], silc.t[:, k, :], k == 0, k == 15, [k_wr[s], silc.k], [k_ps[base]], k == 15)
                ACT(modT.t[:, fo, :], psum[base][:, 0:18], AF.Identity, [k_ps[base], bmod.k], [modT.k], bias=bmod.t[:, fo:fo + 1], scale=1.0)
            DMA(gmix.d, gmix.t[:], w["g_mix"], [], [gmix.k])
            DMA(gffn.d, gffn.t[:], w["g_ffn"], [], [gffn.k])
            for (Gt, off, g) in ((G1, 16, gmix), (G2, 64, gffn)):
                TS(Gt.t[:], modT.t[:, off:off + 16, :], 1.0, None, ALU.add, None, [modT.k], [Gt.k])
                TT(Gt.t[:], Gt.t[:], g.t[:].unsqueeze(2).to_broadcast([128, 16, 18]), ALU.mult, [Gt.k, g.k], [Gt.k])

        def phase_norm(src, src_k, ncol, Gt=None, SHoff=0, dst=None):
            chs = nchunks(ncol)
            rstd = T[4]; sq = RB[0]; tA = T[2]
            for k in range(16):
                s_ = T[k % 2]
                DMA(s_.d, s_.t[:, 0:ncol], src[:, k, 0:ncol], [src_k[k]], [s_.k])
                ACT(sq.t[:, 0:ncol], s_.t[:, 0:ncol], AF.Square, [s_.k], [sq.k])
                for j, (c0, cn) in enumerate(chs):
                    MM(psum[j][:, 0:cn], ones.t[:], sq.t[:, c0:c0 + cn], k == 0, k == 15, [sq.k, ones.k], [k_ps[j]], True)
            for j, (c0, cn) in enumerate(chs):
                TS(rstd.t[:, c0:c0 + cn], psum[j][:, 0:cn], 1.0 / D, EPS, ALU.mult, ALU.add, [k_ps[j]], [rstd.k])
            ACT(rstd.t[:, 0:ncol], rstd.t[:, 0:ncol], AF.Sqrt, [rstd.k], [rstd.k])
            P.op("dve", lambda e: e.reciprocal(rstd.t[:, 0:ncol], rstd.t[:, 0:ncol]), [rstd.k], [rstd.k])
            for k in range(16):
                s_ = T[k % 2]
                DMA(s_.d, s_.t[:, 0:ncol], src[:, k, 0:ncol], [src_k[k]], [s_.k])
                TT(tA.t[:, 0:ncol], s_.t[:, 0:ncol], rstd.t[:, 0:ncol], ALU.mult, [s_.k, rstd.k], [tA.k])
                if dst is None:
                    ACT(hbuf.t[:, k, 0:NP], tA.t[:, 0:NP], AF.Identity, [tA.k, Gt.k, modT.k], [k_h[k]],
                        scale=Gt.t[:, k, 0:1], bias=modT.t[:, SHoff + k, 0:1])
                    if ncol > NP:
                        TT(tA.t[:, NP:ncol], tA.t[:, NP:ncol], Gt.t[:, k, 1:17], ALU.mult, [tA.k, Gt.k], [tA.k])
                        TT(hbuf.t[:, k, NP:ncol], tA.t[:, NP:ncol], modT.t[:, SHoff + k, 1:17], ALU.add, [tA.k, modT.k], [k_h[k]])
                else:
                    o_ = T[3]
                    ACT(o_.t[:, 0:ncol], tA.t[:, 0:ncol], AF.Identity, [tA.k, gfin.k], [o_.k], scale=gfin.t[:, k:k + 1])
                    DMA(o_.d, dst[:, k, 0:ncol], o_.t[:, 0:ncol], [o_.k], [k_out])

        def phase_inproj(l, t, ncol):
            w = Lw[l]
            gcol = NP * t
            ws = WS([(w["inb"][b], 2048) for b in range(32)] + [(w["indt"], 256)])
            sti = [0]

            def evac(fo, res, M=128):
                for (ps_, c0, cn, tk) in res:
                    s_ = T[sti[0]]; sti[0] ^= 1
                    src = ps_[0:M, 0:cn]; o = s_.t[0:M, 0:cn]
                    if BIN_UB <= fo < BIN_Z:
                        gelu_from(src, [tk], o, [s_.k], T[2], T[3])
                    elif BIN_Z <= fo < BIN_XBC:
                        ACT(o, src, AF.Silu, [tk], [s_.k])
                    else:
                        ACT(o, src, AF.Copy, [tk], [s_.k])
                    dcol = gcol + c0 if c0 < NP else 2048
                    DMA(s_.d, binb[fo, 0:M, dcol:dcol + cn], o.bitcast(FR), [s_.k], [k_bin[fo]])

            for b in range(33):
                s = ws.get(b)
                if b < 32:
                    wv = wr.t[:, s, :].rearrange("p (k n) -> p k n", k=16)
                    res = proj_group(lambda k: (wv[:, k, :], [k_wr[s]]),
                                     lambda k, c0, cn: (hbuf.t[:, k, c0:c0 + cn], [k_h[k]]), 16, ncol)
                    evac(b, res)
                else:
                    wv = wr.t[:, s, 0:256].rearrange("p (k n) -> p k n", k=16)
                    res = proj_group(lambda k: (wv[:, k, :], [k_wr[s]]),
                                     lambda k, c0, cn: (hbuf.t[:, k, c0:c0 + cn], [k_h[k]]), 16, ncol, M=16)
                    evac(BIN_DT, res, M=16)

        def phase_s5(l, t, ncol, oa):
            w = Lw[l]
            with ExitStack() as s:
                cosT = sbuf(s, "cosT", [128, 16, 128]); sinT = sbuf(s, "sinT", [128, 16, 128])
                rtab = sbuf(s, "rtab", [128, 16, 128])
                Bbr = sbuf(s, "Bbr", [128, 16, 128], FR); Bbi = sbuf(s, "Bbi", [128, 16, 128], FR)
                Cre = sbuf(s, "CreS", [128, 16, 128], FR); Cim = sbuf(s, "CimS", [128, 16, 128], FR)
                prm = sbuf(s, "s5prm", [128, 16, 16])
                dg = sbuf(s, "s5dg", [128, 2, 128], FR)
                class _W:
                    def __init__(self, b_, ap):
                        self.t = ap; self.k = b_.k; self.d = b_.d
                bl = _W(T[2], T[2].t[:, 0:256].rearrange("p (g i) -> p g i", g=2))
                bt = _W(T[3], T[3].t[:, 0:256].rearrange("p (g i) -> p g i", g=2))
                u = sbuf(s, "s5u", [128, 4, 128], FR)
                hl = sbuf(s, "s5hl", [128, 2, 16]); car = sbuf(s, "s5car", [128, 2, 16]); ct = sbuf(s, "s5ct", [128, 4, 16])
                s5d = sbuf(s, "s5dS", [128, 4]); iot = sbuf(s, "iotS", [128, 128])
                pk = prm.k
                LR, LI, LD, DT_, LRD, TH, RR, AR, AI, DEN, NR, KR, KI, X1, X2 = [prm.t[:, i, :] for i in range(15)]
                DMA(prm.d, LR, w["lamr"], [], [pk]); DMA(prm.d, LI, w["lami"], [], [pk]); DMA(prm.d, LD, w["logdt"], [], [pk])
                DMA(Cre.d, Cre.t[:], w["Cre"], [], [Cre.k]); DMA(Cim.d, Cim.t[:], w["Cim"], [], [Cim.k])
                DMA(s5d.d, s5d.t[:], w["s5d"], [], [s5d.k])
                DMA(iot.d, iot.t[:], c_iota, [], [iot.k])
                ACT(DT_, LD, AF.Exp, [pk], [pk])
                TT(LRD, LR, DT_, ALU.mult, [pk], [pk]); TT(TH, LI, DT_, ALU.mult, [pk], [pk])
                ACT(RR, LRD, AF.Exp, [pk], [pk])
                ki32 = sbuf(s, "s5ki", [128, 128], mybir.dt.int32)
                hpi = sbuf(s, "s5hpi", [128, 1])
                MS(hpi.t[:], 0.5 * PI, [hpi.k])
                for m in range(16):
                    a_ = T[0].t[:, 0:128]; b_ = T[1].t[:, 0:128]; s_ = T[0].t[:, 128:256]; c_ = T[1].t[:, 128:256]
                    TS(a_, iot.t[:], TH[:, m:m + 1], None, ALU.mult, None, [iot.k, pk], [T[0].k])
                    TS(b_, a_, 1.0 / (2 * PI), None, ALU.mult, None, [T[0].k], [T[1].k])
                    CP(ki32.t[:], b_, [T[1].k], [ki32.k])
                    CP(b_, ki32.t[:], [ki32.k], [T[1].k])
                    STT(b_, b_, -2.0 * PI, a_, ALU.mult, ALU.add, [T[1].k, T[0].k], [T[1].k])
                    ACT(s_, b_, AF.Sin, [T[1].k], [T[0].k], scale=0.5)
                    ACT(c_, b_, AF.Sin, [T[1].k, hpi.k], [T[1].k], scale=-0.5, bias=hpi.t[:, 0:1])
                    STT(sinT.t[:, m, :], s_, 2.0, c_, ALU.mult, ALU.mult, [T[0].k, T[1].k], [sinT.k])
                    TT(c_, s_, s_, ALU.mult, [T[0].k], [T[1].k])
                    TS(cosT.t[:, m, :], c_, -2.0, 1.0, ALU.mult, ALU.add, [T[1].k], [cosT.k])
                    TS(rtab.t[:, m, :], onesF, RR[:, m:m + 1], None, ALU.mult, None, [ones.k, pk], [rtab.k])
                MS(rtab.t[:, :, 0:1], 0.0, [rtab.k])
                TT(AR, RR, cosT.t[:, :, 1], ALU.mult, [pk, cosT.k], [pk]); TT(AI, RR, sinT.t[:, :, 1], ALU.mult, [pk, sinT.k], [pk])
                TT(DEN, LR, LR, ALU.mult, [pk], [pk]); TT(X1, LI, LI, ALU.mult, [pk], [pk]); TT(DEN, DEN, X1, ALU.add, [pk], [pk])
                P.op("dve", lambda e: e.reciprocal(DEN, DEN), [pk], [pk])
                TS(NR, AR, -1.0, None, ALU.add, None, [pk], [pk])
                TT(X1, NR, LR, ALU.mult, [pk], [pk]); TT(X2, AI, LI, ALU.mult, [pk], [pk]); TT(X1, X1, X2, ALU.add, [pk], [pk]); TT(KR, X1, DEN, ALU.mult, [pk], [pk])
                TT(X1, AI, LR, ALU.mult, [pk], [pk]); TT(X2, NR, LI, ALU.mult, [pk], [pk]); TT(X1, X1, X2, ALU.subtract, [pk], [pk]); TT(KI, X1, DEN, ALU.mult, [pk], [pk])
                for m in range(16):
                    TS(dg.t[:, 0, :], ident.t[:], KR[:, m:m + 1], None, ALU.mult, None, [ident.k, pk], [dg.k])
                    TS(dg.t[:, 1, :], ident.t[:], KI[:, m:m + 1], None, ALU.mult, None, [ident.k, pk], [dg.k])
                    MM(psum[7][:, 0:128], ones.t[:], dg.t[:, 0, :], True, True, [ones.k, dg.k], [k_ps[7]], False)
                    MM(psum[7][:, 128:256], ones.t[:], dg.t[:, 1, :], True, True, [ones.k, dg.k], [k_ps[7]], True)
                    DMA(bl.d, bl.t[:, 0, :], w["Bre"][:, m, :], [], [bl.k]); DMA(bl.d, bl.t[:, 1, :], w["Bim"][:, m, :], [], [bl.k])
                    kr_b = psum[7][:, 0:128]; ki_b = psum[7][:, 128:256]
                    TT(bt.t[:, 0, :], kr_b, bl.t[:, 0, :], ALU.mult, [k_ps[7], bl.k], [bt.k])
                    TT(bt.t[:, 1, :], ki_b, bl.t[:, 1, :], ALU.mult, [k_ps[7], bl.k], [bt.k])
                    TT(Bbr.t[:, m, :], bt.t[:, 0, :], bt.t[:, 1, :], ALU.subtract, [bt.k], [Bbr.k])
                    TT(bt.t[:, 0, :], kr_b, bl.t[:, 1, :], ALU.mult, [k_ps[7], bl.k], [bt.k])
                    TT(bt.t[:, 1, :], ki_b, bl.t[:, 0, :], ALU.mult, [k_ps[7], bl.k], [bt.k])
                    TT(Bbi.t[:, m, :], bt.t[:, 0, :], bt.t[:, 1, :], ALU.add, [bt.k], [Bbi.k])

                def carry_from(hsrc, hk):
                    c0, c1, c2, c3 = [ct.t[:, i, :] for i in range(4)]
                    TT(c0, AR, hsrc[:, 0, :], ALU.mult, [pk, hk], [ct.k]); TT(c1, AI, hsrc[:, 1, :], ALU.mult, [pk, hk], [ct.k])
                    TT(car.t[:, 0, :], c0, c1, ALU.subtract, [ct.k], [car.k])
                    TT(c2, AR, hsrc[:, 1, :], ALU.mult, [pk, hk], [ct.k]); TT(c3, AI, hsrc[:, 0, :], ALU.mult, [pk, hk], [ct.k])
                    TT(car.t[:, 1, :], c2, c3, ALU.add, [ct.k], [car.k])

                def unit(gc, b):
                    smp = b is not None
                    qi = 0 if smp else 127
                    hst = hl if smp else hlp
                    if smp:
                        CP(u.t[:].rearrange("p k i -> p (k i)"), zer.t[:], [zer.k], [u.k])
                        DMA(u.d, u.t[:, :, 0:1], binb[BIN_UA:BIN_UA + 4, :, 2048 + b:2049 + b].rearrange("k p c -> p k c"), [k_bin[i] for i in range(4)], [u.k])
                        DMA(hl.d, hl.t[:, 0, :], s5r0[l, b], [], [hl.k]); DMA(hl.d, hl.t[:, 1, :], s5i0[l, b], [], [hl.k])
                    else:
                        DMA(u.d, u.t[:], binb[BIN_UA:BIN_UA + 4, :, gc * 128:gc * 128 + 128].rearrange("k p c -> p k c"), [k_bin[i] for i in range(4)], [u.k])
                        if gc == 0:
                            MS(hlp.t[:], 0.0, [hlp.k])
                    carry_from(hst.t, hst.k)
                    for hf in range(2):
                        cosv = cosT.t[:, 8 * hf:8 * hf + 8, :]; sinv = sinT.t[:, 8 * hf:8 * hf + 8, :]
                        v3 = lambda B_: B_.t[:, 0:1024].rearrange("p (m i) -> p m i", m=8)
                        for mm in range(8):
                            m = 8 * hf + mm
                            bank = mm // 4; c0 = 128 * (mm % 4)
                            MM(psum[bank][:, c0:c0 + 128], Bbr.t[:, m, :], u.t[:, m // 4, :], True, True, [Bbr.k, u.k], [k_ps[bank]], mm % 4 == 3)
                            MM(psum[2 + bank][:, c0:c0 + 128], Bbi.t[:, m, :], u.t[:, m // 4, :], True, True, [Bbi.k, u.k], [k_ps[2 + bank]], mm % 4 == 3)
                        for bank in range(2):
                            cs_ = cosT.t[:, 8 * hf + 4 * bank:8 * hf + 4 * bank + 4, :]; sn_ = sinT.t[:, 8 * hf + 4 * bank:8 * hf + 4 * bank + 4, :]
                            sl = slice(512 * bank, 512 * bank + 512)
                            pr = psum[bank][:, :].rearrange("p (m i) -> p m i", m=4); pi_ = psum[2 + bank][:, :].rearrange("p (m i) -> p m i", m=4)
                            v4 = lambda B_: B_.t[:, sl].rearrange("p (m i) -> p m i", m=4)
                            TT(v4(T[0]), pr, cs_, ALU.mult, [k_ps[bank], cosT.k], [T[0].k])
                            TT(v4(T[1]), pi_, sn_, ALU.mult, [k_ps[2 + bank], sinT.k], [T[1].k])
                            TT(v4(T[2]), v4(T[0]), v4(T[1]), ALU.add, [T[0].k, T[1].k], [T[2].k], eng="pool")
                            TT(v4(T[0]), pi_, cs_, ALU.mult, [k_ps[2 + bank], cosT.k], [T[0].k])
                            TT(v4(T[1]), pr, sn_, ALU.mult, [k_ps[bank], sinT.k], [T[1].k])
                            TT(v4(T[3]), v4(T[0]), v4(T[1]), ALU.subtract, [T[0].k, T[1].k], [T[3].k], eng="pool")
                        TT(v3(T[2])[:, :, 0], v3(T[2])[:, :, 0], car.t[:, 0, 8 * hf:8 * hf + 8], ALU.add, [T[2].k, car.k], [T[2].k])
                        TT(v3(T[3])[:, :, 0], v3(T[3])[:, :, 0], car.t[:, 1, 8 * hf:8 * hf + 8], ALU.add, [T[3].k, car.k], [T[3].k])
                        rt = rtab.t[:, 8 * hf:8 * hf + 8, :].rearrange("p m i -> p (m i)")
                        P.op("dve", lambda e, rt=rt: e.tensor_tensor_scan(T[2].t[:, 0:1024], rt, T[2].t[:, 0:1024], 0.0, ALU.mult, ALU.add), [rtab.k, T[2].k], [T[2].k])
                        P.op("dve", lambda e, rt=rt: e.tensor_tensor_scan(T[3].t[:, 0:1024], rt, T[3].t[:, 0:1024], 0.0, ALU.mult, ALU.add), [rtab.k, T[3].k], [T[3].k])
                        gr = v3(T[2]); gi = v3(T[3])
                        hr = RB[0]; hin = RB[1]
                        v3f = lambda B_: B_.t[:, 0:1024].bitcast(F32).rearrange("p (m i) -> p m i", m=8)
                        TT(v3(T[0]), gr, cosv, ALU.mult, [T[2].k, cosT.k], [T[0].k])
                        TT(v3(T[1]), gi, sinv, ALU.mult, [T[3].k, sinT.k], [T[1].k])
                        TT(v3(hr), v3(T[0]), v3(T[1]), ALU.subtract, [T[0].k, T[1].k], [hr.k], eng="pool")
                        TT(v3(T[0]), gr, sinv, ALU.mult, [T[2].k, sinT.k, hr.k], [T[0].k])
                        TT(v3(T[1]), gi, cosv, ALU.mult, [T[3].k, cosT.k, hr.k], [T[1].k])
                        STT(v3(hin), v3(T[0]), -1.0, v3(T[1]), ALU.mult, ALU.subtract, [T[0].k, T[1].k], [hin.k])
                        CP(hst.t[:, 0, 8 * hf:8 * hf + 8], v3f(hr)[:, :, qi], [hr.k], [hst.k])
                        TS(hst.t[:, 1, 8 * hf:8 * hf + 8], v3f(hin)[:, :, qi], -1.0, None, ALU.mult, None, [hin.k], [hst.k])
                        for jj in range(2):
                            j = 2 * hf + jj
                            for q in range(4):
                                mm = 4 * jj + q; m = 8 * hf + mm
                                MM(psum[4][:, 128 * jj:128 * jj + 128], Cre.t[:, m, :], hr.t[:, 128 * mm:128 * mm + 128], q == 0, False, [Cre.k, hr.k], [k_ps[4]], False)
                                MM(psum[4][:, 128 * jj:128 * jj + 128], Cim.t[:, m, :], hin.t[:, 128 * mm:128 * mm + 128], False, q == 3, [Cim.k, hin.k], [k_ps[4]], q == 3)
                            yv = T[0].t[:, 0:128]
                            STT(yv, u.t[:, j, :].bitcast(F32), s5d.t[:, j:j + 1], psum[4][:, 128 * jj:128 * jj + 128], ALU.mult, ALU.add, [u.k, s5d.k, k_ps[4]], [T[0].k])
                            if smp:
                                _g1(T[0].t[:, 0:1], T[0].k, oa.t[:, j, NP + b:NP + b + 1], oa.k)
                            else:
                                lc = (gc % 8) * 128
                                _g1(yv, T[0].k, oa.t[:, j, lc:lc + 128], oa.k)
                    if smp:
                        DMA(hl.d, o_s5s[l, b].rearrange("r p m -> p r m"), hl.t[:], [hl.k], [k_out])
                    elif gc == 15:
                        DMA(hlp.d, o_s5p[l].rearrange("r p m -> p r m"), hlp.t[:], [hlp.k], [k_out])

                gtmp = _W(T[4], T[4].t[:, 0:256].rearrange("p (g i) -> p g i", g=2))

                def _g1(src, srck, out, outk):
                    n = src.shape[1]
                    a = gtmp.t[:, 0, 0:n]; b_ = gtmp.t[:, 1, 0:n]
                    ACT(a, src, AF.Square, [srck], [gtmp.k])
                    TS(a, a, 0.044715, 1.0, ALU.mult, ALU.add, [gtmp.k], [gtmp.k])
                    TT(a, a, src, ALU.mult, [gtmp.k, srck], [gtmp.k])
                    ACT(b_, a, AF.Sigmoid, [gtmp.k], [gtmp.k], scale=1.5957691216)
                    TT(out, b_, src, ALU.mult, [gtmp.k, srck], [outk])

                def s5_samples():
                    uS = sbuf(s, "s5uS", [128, 4, NSMP], FR)
                    v4 = lambda ap: ap.rearrange("p (r m b) -> p r m b", r=2, m=16)
                    h0 = _W(T[0], v4(T[0].t[:, 0:512])); hn = _W(RB[0], v4(RB[0].t[:, 0:512]))
                    cS = _W(T[1], v4(T[1].t[:, 0:512])); tS = _W(T[2], v4(T[2].t[:, 0:512]))
                    yS = _W(T[3], T[3].t[:, 0:NSMP])
                    DMA(uS.d, uS.t[:], binb[BIN_UA:BIN_UA + 4, :, 2048:2048 + NSMP].rearrange("k p c -> p k c"), [k_bin[i] for i in range(4)], [uS.k])
                    DMA(h0.d, h0.t[:, 0], s5r0[l], [], [h0.k]); DMA(h0.d, h0.t[:, 1], s5i0[l], [], [h0.k])
                    arb = AR.unsqueeze(2).to_broadcast([128, 16, NSMP]); aib = AI.unsqueeze(2).to_broadcast([128, 16, NSMP])
                    TT(tS.t[:, 0], h0.t[:, 0], arb, ALU.mult, [h0.k, pk], [tS.k]); TT(tS.t[:, 1], h0.t[:, 1], aib, ALU.mult, [h0.k, pk], [tS.k])
                    TT(cS.t[:, 0], tS.t[:, 0], tS.t[:, 1], ALU.subtract, [tS.k], [cS.k])
                    TT(tS.t[:, 0], h0.t[:, 1], arb, ALU.mult, [h0.k, pk, cS.k], [tS.k]); TT(tS.t[:, 1], h0.t[:, 0], aib, ALU.mult, [h0.k, pk, cS.k], [tS.k])
                    TT(cS.t[:, 1], tS.t[:, 0], tS.t[:, 1], ALU.add, [tS.k], [cS.k])
                    for m in range(16):
                        MM(psum[0][:, 16 * m:16 * m + 16], Bbr.t[:, m, :], uS.t[:, m // 4, :], True, True, [Bbr.k, uS.k], [k_ps[0]], False)
                        MM(psum[0][:, 256 + 16 * m:256 + 16 * m + 16], Bbi.t[:, m, :], uS.t[:, m // 4, :], True, True, [Bbi.k, uS.k], [k_ps[0]], m == 15)
                    pv = psum[0][:, :].rearrange("p (r m b) -> p r m b", r=2, m=16)
                    TT(hn.t[:, 0], pv[:, 0], cS.t[:, 0], ALU.add, [k_ps[0], cS.k], [hn.k])
                    TT(tS.t[:, 1], pv[:, 1], cS.t[:, 1], ALU.add, [k_ps[0], cS.k], [tS.k])
                    TS(hn.t[:, 1], tS.t[:, 1], -1.0, None, ALU.mult, None, [tS.k], [hn.k])
                    CP(tS.t[:, 0], hn.t[:, 0].bitcast(F32), [hn.k], [tS.k])
                    DMA(tS.d, o_s5s[l].rearrange("r p m b -> p r m b"), tS.t, [tS.k], [k_out])
                    for j in range(4):
                        for q in range(4):
                            m = 4 * j + q
                            MM(psum[1][:, 16 * j:16 * j + 16], Cre.t[:, m, :], hn.t[:, 0, m, :], q == 0, False, [Cre.k, hn.k], [k_ps[1]], False)
                            MM(psum[1][:, 16 * j:16 * j + 16], Cim.t[:, m, :], hn.t[:, 1, m, :], False, q == 3, [Cim.k, hn.k], [k_ps[1]], q == 3)
                        STT(yS.t, uS.t[:, j, :].bitcast(F32), s5d.t[:, j:j + 1], psum[1][:, 16 * j:16 * j + 16], ALU.mult, ALU.add, [uS.k, s5d.k, k_ps[1]], [yS.k])
                        _g1(yS.t, yS.k, oa.t[:, j, NP:NP + NSMP], oa.k)

                for c in range(8):
                    unit(8 * t + c, None)
                if t == 0:
                    s5_samples()
                ws = WS([(w["glu"], 2048)])
                s_ = ws.get(0)
                wv = wr.t[:, s_, :].rearrange("p (k n) -> p k n", k=4)
                for fo in range(4):
                    res = proj_group(lambda k: (wv[:, k, 128 * fo:128 * fo + 128], [k_wr[s_]]),
                                     lambda k, c0, cn: (oa.t[:, k, c0:c0 + cn], [oa.k]), 4, ncol)
                    for (ps_, c0, cn, tk) in res:
                        ACT(T[fo].t[:, c0:c0 + cn], ps_[:, 0:cn], AF.Sigmoid, [tk], [T[fo].k])
                for fo in range(4):
                    TT(oa.t[:, fo, 0:ncol], oa.t[:, fo, 0:ncol].bitcast(F32), T[fo].t[:, 0:ncol], ALU.mult, [oa.k, T[fo].k], [oa.k])

        def phase_gmlp(l, t, ncol, ob):
            w = Lw[l]
            with ExitStack() as s:
                wsm = sbuf(s, "wsm", [128, 8, 128], FR); bsx = sbuf(s, "bsxS", [128, 4, 128]); tri = sbuf(s, "triS", [128, 128])
                gv = sbuf(s, "gvS", [128, 4])
                ug = sbuf(s, "ug", [128, 4, 128]); vg = sbuf(s, "vg", [128, 4, 128]); vn = sbuf(s, "vn", [128, 4, 128])
                vtk = sbuf(s, "vtk", [128, 512], FR); sq = sbuf(s, "gsq", [128, 128], FR); rs = sbuf(s, "grs", [128, 128]); sv = sbuf(s, "gsv", [128, 128])
                DMA(bsx.d, bsx.t[:], w["bsx"], [], [bsx.k]); DMA(tri.d, tri.t[:], c_tril, [], [tri.k]); DMA(gv.d, gv.t[:], w["gv"], [], [gv.k])
                wtmp = T[0].t[:, 0:1024].rearrange("p (h i) -> p h i", h=8)
                DMA(T[0].d, wtmp, w["wsT"], [], [T[0].k])
                TT(wsm.t[:], wtmp, tri.t[:].unsqueeze(1).to_broadcast([128, 8, 128]), ALU.mult, [T[0].k, tri.k], [wsm.k])

                def unit(gc, b):
                    smp = b is not None
                    if smp:
                        MS(ug.t[:], 0.0, [ug.k]); MS(vg.t[:], 0.0, [vg.k])
                        DMA(ug.d, ug.t[:, :, 0:1], binF[BIN_UB:BIN_UB + 4, :, 2048 + b:2049 + b].rearrange("k p c -> p k c"), [k_bin[BIN_UB + i] for i in range(4)], [ug.k])
                        DMA(vg.d, vg.t[:, :, 0:1], binF[BIN_VB:BIN_VB + 4, :, 2048 + b:2049 + b].rearrange("k p c -> p k c"), [k_bin[BIN_VB + i] for i in range(4)], [vg.k])
                    else:
                        c0 = gc * 128
                        DMA(ug.d, ug.t[:], binF[BIN_UB:BIN_UB + 4, :, c0:c0 + 128].rearrange("k p c -> p k c"), [k_bin[BIN_UB + i] for i in range(4)], [ug.k])
                        DMA(vg.d, vg.t[:], binF[BIN_VB:BIN_VB + 4, :, c0:c0 + 128].rearrange("k p c -> p k c"), [k_bin[BIN_VB + i] for i in range(4)], [vg.k])
                    for k in range(4):
                        ACT(sq.t[:], vg.t[:, k, :], AF.Square, [vg.k], [sq.k])
                        MM(psum[6][:, 0:128], ones.t[:], sq.t[:], k == 0, k == 3, [ones.k, sq.k], [k_ps[6]], True)
                    TS(rs.t[:], psum[6][:, 0:128], 1.0 / 512, EPS, ALU.mult, ALU.add, [k_ps[6]], [rs.k])
                    ACT(rs.t[:], rs.t[:], AF.Sqrt, [rs.k], [rs.k])
                    P.op("dve", lambda e: e.reciprocal(rs.t[:], rs.t[:]), [rs.k], [rs.k])
                    for k in range(4):
                        STT(vn.t[:, k, :], vg.t[:, k, :], gv.t[:, k:k + 1], rs.t[:], ALU.mult, ALU.mult, [vg.k, gv.k, rs.k], [vn.k])
                    if smp:
                        DMA(vn.d, o_gv[l][:, :, b:b + 1], vn.t[:, :, 0:1], [vn.k], [k_out])
                    for k in range(4):
                        TR(psum[7][:, 128 * k:128 * k + 128], vn.t[:, k, :], ident.t[:], [vn.k, ident.k], [k_ps[7]])
                    ACT(vtk.t[:], psum[7][:, :], AF.Copy, [k_ps[7]], [vtk.k])
                    for k in range(4):
                        bank = 4 + (k % 2)
                        MM(psum[bank][:, 0:128], vtk.t[:, 128 * k:128 * k + 128], wsm.t[:, 2 * k, :], True, True, [vtk.k, wsm.k], [k_ps[bank]], False)
                        MM(psum[bank][:, 128:256], vtk.t[:, 128 * k:128 * k + 128], wsm.t[:, 2 * k + 1, :], True, True, [vtk.k, wsm.k], [k_ps[bank]], True)
                        TT(sv.t[0:64, :], psum[bank][0:64, 0:128], bsx.t[0:64, k, :], ALU.add, [k_ps[bank], bsx.k], [sv.k])
                        TT(sv.t[64:128, :], psum[bank][64:128, 128:256], bsx.t[64:128, k, :], ALU.add, [k_ps[bank], bsx.k], [sv.k])
                        if smp:
                            TT(ob.t[:, k, NP + b:NP + b + 1], sv.t[:, 0:1], ug.t[:, k, 0:1], ALU.mult, [sv.k, ug.k], [ob.k])
                        else:
                            lc = (gc % 8) * 128
                            TT(ob.t[:, k, lc:lc + 128], sv.t[:], ug.t[:, k, :], ALU.mult, [sv.k, ug.k], [ob.k])

                def gm_samples():
                    uS = sbuf(s, "gmuS", [128, 4, NSMP]); vS = sbuf(s, "gmvS", [128, 4, NSMP]); sqS = sbuf(s, "gmsqS", [128, 4, NSMP], FR)
                    rS = sbuf(s, "gmrS", [128, NSMP]); w00 = sbuf(s, "gmw00", [128, 4]); sS = sbuf(s, "gmsS", [128, NSMP])
                    DMA(uS.d, uS.t[:], binF[BIN_UB:BIN_UB + 4, :, 2048:2048 + NSMP].rearrange("k p c -> p k c"), [k_bin[BIN_UB + i] for i in range(4)], [uS.k])
                    DMA(vS.d, vS.t[:], binF[BIN_VB:BIN_VB + 4, :, 2048:2048 + NSMP].rearrange("k p c -> p k c"), [k_bin[BIN_VB + i] for i in range(4)], [vS.k])
                    DMA(w00.d, w00.t[:], w["w00"], [], [w00.k])
                    ACT(sqS.t[:], vS.t[:], AF.Square, [vS.k], [sqS.k])
                    for k in range(4):
                        MM(psum[6][:, 0:NSMP], ones.t[:], sqS.t[:, k, :], k == 0, k == 3, [ones.k, sqS.k], [k_ps[6]], k == 3)
                    TS(rS.t[:], psum[6][:, 0:NSMP], 1.0 / 512, EPS, ALU.mult, ALU.add, [k_ps[6]], [rS.k])
                    ACT(rS.t[:], rS.t[:], AF.Sqrt, [rS.k], [rS.k])
                    P.op("dve", lambda e: e.reciprocal(rS.t[:], rS.t[:]), [rS.k], [rS.k])
                    for k in range(4):
                        STT(vS.t[:, k, :], vS.t[:, k, :], gv.t[:, k:k + 1], rS.t[:], ALU.mult, ALU.mult, [vS.k, gv.k, rS.k], [vS.k])
                    DMA(vS.d, o_gv[l], vS.t[:], [vS.k], [k_out])
                    for k in range(4):
                        TS(sS.t[:], vS.t[:, k, :], w00.t[:, k:k + 1], bsx.t[:, k, 0:1], ALU.mult, ALU.add, [vS.k, w00.k, bsx.k], [sS.k])
                        TT(ob.t[:, k, NP:NP + NSMP], sS.t[:], uS.t[:, k, :], ALU.mult, [sS.k, uS.k], [ob.k])

                for c in range(8):
                    unit(8 * t + c, None)
                if t == 0:
                    gm_samples()

        def phase_ssd(l, t, ncol, oc):
            w = Lw[l]
            with ExitStack() as s:
                xe = sbuf(s, "xe", [128, 12, 131]); bc = sbuf(s, "bcS", [128, 4, 128], FR)
                class _V:
                    def __init__(self, t, d=None):
                        self.t = t; self.k = Tok(); self.d = d
                MT = _V(wr.t[:, 0, :].rearrange("p (h i) -> p h i", h=16))
                sex = _V(wr.t[0:16, 1, 1024:2048].rearrange("p (m i) -> p m i", m=8), d_wr_(1))
                xdd = _V(wr.t[:, 1, 0:1024])
                sm = _V(T[3].t[0:16, 0:640].rearrange("p (m i) -> p m i", m=5), T[3].d)
                GT = _V(T[3].t[:, 640:896].rearrange("p (g i) -> p g i", g=2))
                LT = _V(T[4].t[:, 0:256].rearrange("p (g i) -> p g i", g=2))
                LTs = [_V(T[4].t[:, 0:128]), _V(T[4].t[:, 128:256])]
                q4k = [Tok() for _ in range(4)]
                ecs = _V(T[4].t[:, 256:384]); yt = _V(T[4].t[:, 384:512])
                cvas = [_V(T[4].t[:, 512:640]), _V(T[4].t[:, 640:768])]; rs2 = _V(T[4].t[:, 768:896])
                csm = sbuf(s, "csm", [16, 2, 128], FR)
                csms = [sbuf(s, "csm0", [16, 128], FR), sbuf(s, "csm1", [16, 128], FR)]
                negm = sbuf(s, "negm", [128, 128])
                cw = sbuf(s, "cwS", [128, 12, 4]); cb = sbuf(s, "cbS", [128, 12])
                csr = sbuf(s, "csr", [16, 128], FR)
                p16 = sbuf(s, "ssdp16", [16, 4])
                Dx = sbuf(s, "DxS", [128, 8]); gn = sbuf(s, "gnS", [128, 8])
                tk_ = sbuf(s, "ssdtok", [128, 6, 16])
                Btok = sbuf(s, "Btok", [128, 2, 128], FR)
                sqm = sbuf(s, "sqm", [128, 128], FR)
                Sns = _V(RB[1].t[:, 0:1024], RB[1].d)
                b32 = sbuf(s, "b32", [128, 2, 128])
                DMA(sex.d, sex.t, c_selexp, [], [sex.k]); DMA(negm.d, negm.t[:], c_negmask, [], [negm.k])
                DMA(cw.d, cw.t[:], w["cw"], [], [cw.k]); DMA(cb.d, cb.t[:], w["cb"], [], [cb.k])
                DMA(p16.d, p16.t[:, 0:1], w["dtb"], [], [p16.k]); DMA(p16.d, p16.t[:, 1:2], w["alog"], [], [p16.k])
                DMA(Dx.d, Dx.t[:], w["ssdD"], [], [Dx.k]); DMA(gn.d, gn.t[:], w["ssdg"], [], [gn.k])
                ACT(p16.t[:, 2:3], p16.t[:, 1:2], AF.Exp, [p16.k], [p16.k])
                TS(p16.t[:, 2:3], p16.t[:, 2:3], -1.0, None, ALU.mult, None, [p16.k], [p16.k])
                xs = T[0]; zs = T[1]; yz = T[2]; xdt = RB[0]
                xs3 = xs.t[:, 0:1024].rearrange("p (m i) -> p m i", m=8)
                zs3 = zs.t[:, 0:1024].rearrange("p (m i) -> p m i", m=8)
                yz3 = yz.t[:, 0:1024].rearrange("p (m i) -> p m i", m=8)
                dtr, ee, dt32, dA, cs32 = [sm.t[:, i, :] for i in range(5)]
                dt_tok, ncs_tok, csl, de_tok, cd, tk5 = [tk_.t[:, i, :] for i in range(6)]

                def unit(gc, b):
                    smp = b is not None
                    Sn = Sns if smp else Snp
                    if smp:
                        MS(xe.t[:], 0.0, [xe.k]); MS(zs.t[:, 0:1024], 0.0, [zs.k]); MS(dtr, 0.0, [sm.k])
                        DMA(xe.d, xe.t[:, :, 0:3], cbuf0[l][:, :, b, :], [], [xe.k])
                        DMA(xe.d, xe.t[:, :, 3:4], binF[BIN_XBC:BIN_XBC + 12, :, 2048 + b:2049 + b].rearrange("k p c -> p k c"), [k_bin[BIN_XBC + i] for i in range(12)], [xe.k])
                        DMA(zs.d, zs3[:, :, 0:1], binF[BIN_Z:BIN_Z + 8, :, 2048 + b:2049 + b].rearrange("k p c -> p k c"), [k_bin[BIN_Z + i] for i in range(8)], [zs.k])
                        DMA(sm.d, dtr[:, 0:1], binF[BIN_DT, 0:16, 2048 + b:2049 + b], [k_bin[BIN_DT]], [sm.k])
                        DMA(Sn.d, Sn.t[:, 0:1024], ssm0[l, b], [], [Sn.k])
                    else:
                        c0 = gc * 128
                        if gc == 0:
                            MS(xe.t[:, :, 0:3], 0.0, [xe.k])
                            DMA(xe.d, xe.t[:, :, 3:131], binF[BIN_XBC:BIN_XBC + 12, :, 0:128].rearrange("k p c -> p k c"), [k_bin[BIN_XBC + i] for i in range(12)], [xe.k])
                            CP(Sn.t[:, 0:512], zer.t[:], [zer.k], [Sn.k]); CP(Sn.t[:, 512:1024], zer.t[:], [zer.k], [Sn.k])
                        else:
                            DMA(xe.d, xe.t[:], binF[BIN_XBC:BIN_XBC + 12, :, c0 - 3:c0 + 128].rearrange("k p c -> p k c"), [k_bin[BIN_XBC + i] for i in range(12)], [xe.k])
                        DMA(zs.d, zs3, binF[BIN_Z:BIN_Z + 8, :, c0:c0 + 128].rearrange("k p c -> p k c"), [k_bin[BIN_Z + i] for i in range(8)], [zs.k])
                        DMA(sm.d, dtr, binF[BIN_DT, 0:16, c0:c0 + 128], [k_bin[BIN_DT]], [sm.k])
                    if smp:
                        DMA(xe.d, o_ccs[l][:, :, b, :], xe.t[:, :, 1:4], [xe.k], [k_out])
                    elif gc == 15:
                        DMA(xe.d, o_ccp[l], xe.t[:, :, 128:131], [xe.k], [k_out])
                    for k in range(12):
                        cva = cvas[k % 2]
                        cv = cva.t
                        TS(cv, xe.t[:, k, 3:131], cw.t[:, k, 3:4], cb.t[:, k:k + 1], ALU.mult, ALU.add, [xe.k, cw.k, cb.k], [cva.k])
                        for j_ in (2, 1, 0):
                            STT(cv, xe.t[:, k, j_:j_ + 128], cw.t[:, k, j_:j_ + 1], cv, ALU.mult, ALU.add, [xe.k, cw.k, cva.k], [cva.k])
                        if k < 8:
                            ACT(xs3[:, k, :], cv, AF.Silu, [cva.k], [xs.k])
                        else:
                            ACT(bc.t[:, k - 8, :], cv, AF.Silu, [cva.k], [bc.k])
                            if k < 10:
                                ACT(b32.t[:, k - 8, :], cv, AF.Silu, [cva.k], [b32.k])
                    ACT(ee, dtr, AF.Exp, [sm.k, p16.k], [sm.k], bias=p16.t[:, 0:1], scale=1.0)
                    TS(ee, ee, 1.0, None, ALU.add, None, [sm.k], [sm.k])
                    ACT(dt32, ee, AF.Ln, [sm.k], [sm.k])
                    if smp:
                        MS(dt32[:, 1:128], 0.0, [sm.k])
                    TS(dA, dt32, p16.t[:, 2:3], None, ALU.mult, None, [sm.k, p16.k], [sm.k])
                    P.op("dve", lambda e: e.tensor_tensor_scan(cs32, onesF[0:16, :], dA, 0.0, ALU.mult, ALU.add), [sm.k, ones.k], [sm.k])
                    ACT(csr.t[:], cs32, AF.Copy, [sm.k], [csr.k])
                    TR(psum[0][:, 0:16], dt32, ident.t[0:16, 0:16], [sm.k, ident.k], [k_ps[0]])
                    TR(psum[0][:, 16:32], cs32, ident.t[0:16, 0:16], [sm.k, ident.k], [k_ps[0]])
                    ACT(dt_tok, psum[0][:, 0:16], AF.Copy, [k_ps[0]], [tk_.k])
                    TS(ncs_tok, psum[0][:, 16:32], -1.0, None, ALU.mult, None, [k_ps[0]], [tk_.k])
                    for g in range(2):
                        MM(psum[0][:, 128 + 128 * g:256 + 128 * g], bc.t[:, g, :], bc.t[:, 2 + g, :], True, True, [bc.k], [k_ps[0]], g == 1)
                        TR(psum[1][:, 128 * g:128 * g + 128], b32.t[:, g, :], ident.t[:], [b32.k, ident.k], [k_ps[1]])
                    ACT(GT.t, psum[0][:, 128:384].rearrange("p (g i) -> p g i", g=2), AF.Copy, [k_ps[0]], [GT.k])
                    ACT(Btok.t[:], psum[1][:, 0:256].rearrange("p (g i) -> p g i", g=2), AF.Copy, [k_ps[1]], [Btok.k])
                    for q in range(4):
                        for hh in range(4):
                            h = 4 * q + hh
                            cm = csms[hh % 2]
                            TS(cm.t[:], cs32, ident.t[0:16, h:h + 1], None, ALU.mult, None, [sm.k, ident.k], [cm.k])
                            MM(psum[4][:, 128 * hh:128 * hh + 128], ones.t[0:16, :], cm.t[:], True, True, [ones.k, cm.k], [k_ps[4], q4k[hh]], True)
                        CP(csl[:, 4 * q:4 * q + 4], psum[4][:, :].rearrange("p (h i) -> p h i", h=4)[:, :, 127], [k_ps[4]] + q4k, [tk_.k])
                        for hh in range(4):
                            h = 4 * q + hh
                            LTh = LTs[hh % 2]
                            lt = LTh.t
                            TT(lt, psum[4][:, 128 * hh:128 * hh + 128], negm.t[:], ALU.add, [q4k[hh], negm.k], [LTh.k])
                            ACT(lt, lt, AF.Exp, [LTh.k, tk_.k], [LTh.k], bias=ncs_tok[:, h:h + 1], scale=1.0)
                            TT(MT.t[:, h, :], GT.t[:, h // 8, :], lt, ALU.mult, [GT.k, LTh.k], [MT.k])
                    TT(tk5, csl, ncs_tok, ALU.add, [tk_.k], [tk_.k])
                    ACT(de_tok, tk5, AF.Exp, [tk_.k], [tk_.k])
                    ACT(cd, csl, AF.Exp, [tk_.k], [tk_.k])
                    for m in range(8):
                        bank = 2 + m // 4
                        TR(psum[bank][:, 128 * (m % 4):128 * (m % 4) + 128], xs3[:, m, :], ident.t[:], [xs.k, ident.k], [k_ps[bank]])
                    for hb in range(2):
                        TT(xdt.t[:, 512 * hb:512 * hb + 512].rearrange("p (h d) -> p h d", h=8),
                           psum[2 + hb][:, :].rearrange("p (h d) -> p h d", h=8),
                           dt_tok[:, 8 * hb:8 * hb + 8].unsqueeze(2).to_broadcast([128, 8, 64]), ALU.mult, [k_ps[2 + hb], tk_.k], [xdt.k])
                    TT(xdd.t.rearrange("p (h d) -> p h d", h=16), xdt.t[:, 0:1024].bitcast(F32).rearrange("p (h d) -> p h d", h=16),
                       de_tok.unsqueeze(2).to_broadcast([128, 16, 64]), ALU.mult, [xdt.k, tk_.k], [xdd.k])
                    for m in range(8):
                        g = m // 4
                        bank = 5 + (m % 2)
                        pY = psum[bank]
                        MM(pY[:, 0:128], Sn.t[:, 128 * m:128 * m + 128], bc.t[:, 2 + g, :], True, True, [Sn.k, bc.k], [k_ps[bank]], False)
                        MM(pY[:, 128:256], xdt.t[:, 128 * m:128 * m + 128], MT.t[:, 2 * m, :], True, True, [xdt.k, MT.k], [k_ps[bank]], False)
                        MM(pY[:, 256:384], xdt.t[:, 128 * m:128 * m + 128], MT.t[:, 2 * m + 1, :], True, True, [xdt.k, MT.k], [k_ps[bank]], False)
                        MM(pY[:, 384:512], sex.t[:, m, :], csr.t[:], True, True, [sex.k, csr.k], [k_ps[bank]], True)
                        ACT(ecs.t, pY[:, 384:512], AF.Exp, [k_ps[bank]], [ecs.k])
                        TT(yt.t, pY[:, 0:128], ecs.t, ALU.mult, [k_ps[bank], ecs.k], [yt.k])
                        TT(yt.t[0:64, :], yt.t[0:64, :], pY[0:64, 128:256], ALU.add, [yt.k, k_ps[bank]], [yt.k])
                        TT(yt.t[64:128, :], yt.t[64:128, :], pY[64:128, 256:384], ALU.add, [yt.k, k_ps[bank]], [yt.k])
                        STT(yt.t, xs3[:, m, :], Dx.t[:, m:m + 1], yt.t, ALU.mult, ALU.add, [xs.k, Dx.k, yt.k], [yt.k])
                        TT(yz3[:, m, :], yt.t, zs3[:, m, :], ALU.mult, [yt.k, zs.k], [yz.k])
                        ACT(sqm.t[:], yz3[:, m, :], AF.Square, [yz.k], [sqm.k])
                        MM(psum[7][:, 0:128], ones.t[:], sqm.t[:], m == 0, m == 7, [ones.k, sqm.k], [k_ps[7]], True)
                    TS(rs2.t, psum[7][:, 0:128], 1.0 / 1024, EPS, ALU.mult, ALU.add, [k_ps[7]], [rs2.k])
                    ACT(rs2.t, rs2.t, AF.Sqrt, [rs2.k], [rs2.k])
                    P.op("dve", lambda e: e.reciprocal(rs2.t, rs2.t), [rs2.k], [rs2.k])
                    for m in range(8):
                        if smp:
                            STT(oc.t[:, m, NP + b:NP + b + 1], yz3[:, m, 0:1], gn.t[:, m:m + 1], rs2.t[:, 0:1], ALU.mult, ALU.mult, [yz.k, gn.k, rs2.k], [oc.k])
                        else:
                            lc = (gc % 8) * 128
                            STT(oc.t[:, m, lc:lc + 128], yz3[:, m, :], gn.t[:, m:m + 1], rs2.t, ALU.mult, ALU.mult, [yz.k, gn.k, rs2.k], [oc.k])
                    for g in range(2):
                        MM(psum[2 + g][:, :], Btok.t[:, g, :], xdd.t[:, 512 * g:512 * g + 512], True, True, [Btok.k, xdd.k], [k_ps[2 + g]], True)
                        sv = Sn.t[:, 512 * g:512 * g + 512]
                        TT(sv.rearrange("p (h d) -> p h d", h=8), sv.bitcast(F32).rearrange("p (h d) -> p h d", h=8),
                           cd[:, 8 * g:8 * g + 8].unsqueeze(2).to_broadcast([128, 8, 64]), ALU.mult, [Sn.k, tk_.k], [Sn.k])
                        TT(sv, sv.bitcast(F32), psum[2 + g][:, :], ALU.add, [Sn.k, k_ps[2 + g]], [Sn.k])
                    if smp:
                        DMA(Sn.d, o_ssms[l, b], Sn.t[:, 0:1024].bitcast(F32), [Sn.k], [k_out])
                    elif gc == 15:
                        DMA(Sn.d, o_ssmp[l], Sn.t[:, 0:1024].bitcast(F32), [Sn.k], [k_out])

                def ssd_samples():
                    class _A:
                        def __init__(self, o_, ap):
                            self.t = ap; self.k = o_.k; self.d = o_.d
                    NS = NSMP
                    xef = xe.t[:].rearrange("p k c -> p (k c)")
                    cbS = _A(xe, xef[:, 0:576].rearrange("p (k b j) -> p k b j", k=12, b=NS))
                    xS = _A(xe, xef[:, 576:768].rearrange("p (k b) -> p k b", k=12))
                    ncb = _A(xe, xef[:, 768:1344].rearrange("p (k b j) -> p k b j", k=12, b=NS))
                    cv = _A(xs, xs.t[:, 0:192].rearrange("p (k b) -> p k b", k=12))
                    xsS = _A(xs, xs.t[:, 192:320].rearrange("p (k b) -> p k b", k=8))
                    ctmp = _A(xs, xs.t[:, 320:512].rearrange("p (k b) -> p k b", k=12))
                    zsS = _A(zs, zs.t[:, 0:128].rearrange("p (k b) -> p k b", k=8))
                    yS = _A(yz, yz.t[:, 0:128].rearrange("p (k b) -> p k b", k=8))
                    yzS = _A(yz, yz.t[:, 128:256].rearrange("p (k b) -> p k b", k=8))
                    xdtS = _A(xdd, xdd.t[0:16, :])
                    diagE = _A(MT, wr.t[0:16, 0, 0:256])
                    cdS = _A(LT, T[4].t[:, 0:256].rearrange("p (b h) -> p b h", b=NS))
                    sqS = _A(sqm, sqm.t[:].rearrange("p (k b) -> p k b", k=8))
                    rsS = _A(rs2, T[4].t[:, 768:768 + NS])
                    SnAB = [_A(Sns, RB[1].t[:, 0:1024]), _A(xdt, RB[0].t[:, 0:1024])]
                    SnAB[1].d = P.dsem("R0")
                    dtrS, eeS, dtS, dAS, edA = [sm.t[:, i, 0:NS] for i in range(5)]
                    DMA(xe.d, cbS.t, cbuf0[l], [], [xe.k])
                    DMA(xe.d, xS.t, binF[BIN_XBC:BIN_XBC + 12, :, 2048:2048 + NS].rearrange("k p c -> p k c"), [k_bin[BIN_XBC + i] for i in range(12)], [xe.k])
                    DMA(zs.d, zsS.t, binF[BIN_Z:BIN_Z + 8, :, 2048:2048 + NS].rearrange("k p c -> p k c"), [k_bin[BIN_Z + i] for i in range(8)], [zs.k])
                    DMA(sm.d, dtrS, binF[BIN_DT, 0:16, 2048:2048 + NS], [k_bin[BIN_DT]], [sm.k])
                    wb = lambda j: cw.t[:, :, j:j + 1].to_broadcast([128, 12, NS])
                    TT(cv.t, xS.t, wb(3), ALU.mult, [xe.k, cw.k], [xs.k])
                    for j in (2, 1, 0):
                        TT(ctmp.t, cbS.t[:, :, :, j], wb(j), ALU.mult, [xe.k, cw.k, xs.k], [xs.k])
                        TT(cv.t, cv.t, ctmp.t, ALU.add, [xs.k], [xs.k])
                    TT(cv.t, cv.t, cb.t[:].unsqueeze(2).to_broadcast([128, 12, NS]), ALU.add, [xs.k, cb.k], [xs.k])
                    ACT(xsS.t, cv.t[:, 0:8, :], AF.Silu, [xs.k], [xs.k])
                    ACT(bc.t[:, :, 0:NS], cv.t[:, 8:12, :], AF.Silu, [xs.k], [bc.k])
                    ACT(b32.t[:, :, 0:NS], cv.t[:, 8:10, :], AF.Silu, [xs.k], [b32.k])
                    CP(ncb.t[:, :, :, 0:2], cbS.t[:, :, :, 1:3], [xe.k], [xe.k])
                    CP(ncb.t[:, :, :, 2], xS.t, [xe.k], [xe.k])
                    DMA(xe.d, o_ccs[l], ncb.t, [xe.k], [k_out])
                    ACT(eeS, dtrS, AF.Exp, [sm.k, p16.k], [sm.k], bias=p16.t[:, 0:1], scale=1.0)
                    TS(eeS, eeS, 1.0, None, ALU.add, None, [sm.k], [sm.k])
                    ACT(dtS, eeS, AF.Ln, [sm.k], [sm.k])
                    TS(dAS, dtS, p16.t[:, 2:3], None, ALU.mult, None, [sm.k, p16.k], [sm.k])
                    ACT(edA, dAS, AF.Exp, [sm.k], [sm.k])
                    TT(diagE.t.rearrange("p (b h) -> p b h", b=NS), ident.t[0:16, 0:16].unsqueeze(1).to_broadcast([16, NS, 16]),
                       edA.unsqueeze(2).to_broadcast([16, NS, 16]), ALU.mult, [ident.k, sm.k], [MT.k])
                    MM(psum[4][:, 0:256], ones.t[0:16, :], diagE.t, True, True, [ones.k, MT.k], [k_ps[4]] + q4k, True)
                    CP(cdS.t, psum[4][:, 0:256].rearrange("p (b h) -> p b h", b=NS), [k_ps[4]] + q4k, [LT.k, LTs[0].k, LTs[1].k])
                    TR(psum[0][0:16, 0:16], dtS, ident.t[0:16, 0:16], [sm.k, ident.k], [k_ps[0]])
                    CP(tk_.t[0:16, 0, :], psum[0][0:16, 0:16], [k_ps[0]], [tk_.k])
                    for m in range(8):
                        bank = 2 + m // 4
                        TR(psum[bank][0:16, 128 * (m % 4):128 * (m % 4) + 128], xsS.t[:, m, :], ident.t[:], [xs.k, ident.k], [k_ps[bank]])
                    for hb in range(2):
                        TT(xdtS.t[:, 512 * hb:512 * hb + 512].rearrange("p (h d) -> p h d", h=8), psum[2 + hb][0:16, :].rearrange("p (h d) -> p h d", h=8),
                           tk_.t[0:16, 0, 8 * hb:8 * hb + 8].unsqueeze(2).to_broadcast([16, 8, 64]), ALU.mult, [k_ps[2 + hb], tk_.k], [xdd.k])
                    for g in range(2):
                        TR(psum[1][0:16, 128 * g:128 * g + 128], b32.t[:, g, 0:NS], ident.t[:], [b32.k, ident.k], [k_ps[1]])
                    ACT(Btok.t[0:16, :, :], psum[1][0:16, 0:256].rearrange("p (g i) -> p g i", g=2), AF.Copy, [k_ps[1]], [Btok.k])
                    for b in range(NS):
                        Sn = SnAB[b % 2]
                        DMA(Sn.d, Sn.t, ssm0[l, b], [], [Sn.k])
                        TS(csm.t[:], Btok.t[0:16, :, :].bitcast(F32), ident.t[0:16, b:b + 1], None, ALU.mult, None, [Btok.k, ident.k], [csm.k])
                        for g in range(2):
                            MM(psum[2 + g][:, :], csm.t[:, g, :], xdtS.t[:, 512 * g:512 * g + 512], True, True, [csm.k, xdd.k], [k_ps[2 + g]], True)
                            sv = Sn.t[:, 512 * g:512 * g + 512]
                            TT(sv.rearrange("p (h d) -> p h d", h=8), sv.bitcast(F32).rearrange("p (h d) -> p h d", h=8),
                               cdS.t[:, b, 8 * g:8 * g + 8].unsqueeze(2).to_broadcast([128, 8, 64]), ALU.mult, [Sn.k, LT.k], [Sn.k])
                            TT(sv, sv.bitcast(F32), psum[2 + g][:, :], ALU.add, [Sn.k, k_ps[2 + g]], [Sn.k])
                        pb = 5 + (b % 2)
                        for m in range(8):
                            MM(psum[pb][:, 2 * m:2 * m + 2], Sn.t[:, 128 * m:128 * m + 128], bc.t[:, 2 + m // 4, b:b + 2], True, True, [Sn.k, bc.k], [k_ps[pb]], m == 7)
                        CP(yS.t[:, :, b], psum[pb][:, 0:16].rearrange("p (m c) -> p m c", m=8)[:, :, 0], [k_ps[pb]], [yz.k])
                        DMA(Sn.d, o_ssms[l, b], Sn.t.bitcast(F32), [Sn.k], [k_out])
                    TT(yzS.t, xsS.t, Dx.t[:].unsqueeze(2).to_broadcast([128, 8, NS]), ALU.mult, [xs.k, Dx.k], [yz.k])
                    TT(yzS.t, yzS.t, yS.t, ALU.add, [yz.k], [yz.k])
                    TT(yzS.t, yzS.t, zsS.t, ALU.mult, [yz.k, zs.k], [yz.k])
                    ACT(sqS.t, yzS.t, AF.Square, [yz.k], [sqm.k])
                    for m in range(8):
                        MM(psum[7][:, 0:NS], ones.t[:], sqS.t[:, m, :], m == 0, m == 7, [ones.k, sqm.k], [k_ps[7]], m == 7)
                    TS(rsS.t, psum[7][:, 0:NS], 1.0 / 1024, EPS, ALU.mult, ALU.add, [k_ps[7]], [rs2.k])
                    ACT(rsS.t, rsS.t, AF.Sqrt, [rs2.k], [rs2.k])
                    P.op("dve", lambda e: e.reciprocal(rsS.t, rsS.t), [rs2.k], [rs2.k])
                    for m in range(8):
                        STT(oc.t[:, m, NP:NP + NS], yzS.t[:, m, :], gn.t[:, m:m + 1], rsS.t, ALU.mult, ALU.mult, [yz.k, gn.k, rs2.k], [oc.k])

                for c in range(8):
                    unit(8 * t + c, None)
                if t == 0:
                    ssd_samples()

        def phase_merge(l, t, ncol, oa, ob, oc):
            w = Lw[l]
            blocks = []
            for fo in range(16):
                blocks += [(w["ing"][fo], 2048), (w["pa"][fo], 512), (w["ing"][16 + fo], 2048), (w["pb"][fo], 512),
                           (w["ing"][32 + fo], 2048), (w["pc"][fo], 1024)]
            ws = WS(blocks)
            bi = 0
            for fo in range(16):
                macc = T[fo % 2]
                for br, (ob_, kc) in enumerate(((oa, 4), (ob, 4), (oc, 8))):
                    s_ = ws.get(bi); bi += 1
                    wv = wr.t[:, s_, :].rearrange("p (k n) -> p k n", k=16)
                    res = proj_group(lambda k: (wv[:, k, :], [k_wr[s_]]), lambda k, c0, cn: (hbuf.t[:, k, c0:c0 + cn], [k_h[k]]), 16, ncol)
                    for (ps_, c0, cn, tk) in res:
                        ACT(T[2].t[:, c0:c0 + cn], ps_[:, 0:cn], AF.Sigmoid, [tk], [T[2].k])
                    s2 = ws.get(bi); bi += 1
                    wv2 = wr.t[:, s2, 0:kc * 128].rearrange("p (k n) -> p k n", k=kc)
                    res = proj_group(lambda k: (wv2[:, k, :], [k_wr[s2]]), lambda k, c0, cn: (ob_.t[:, k, c0:c0 + cn], [ob_.k]), kc, ncol)
                    for (ps_, c0, cn, tk) in res:
                        if br == 0:
                            TT(macc.t[:, c0:c0 + cn], ps_[:, 0:cn], T[2].t[:, c0:c0 + cn], ALU.mult, [tk, T[2].k], [macc.k])
                        else:
                            TT(T[3].t[:, c0:c0 + cn], ps_[:, 0:cn], T[2].t[:, c0:c0 + cn], ALU.mult, [tk, T[2].k], [T[3].k])
                            TT(macc.t[:, c0:c0 + cn], macc.t[:, c0:c0 + cn], T[3].t[:, c0:c0 + cn], ALU.add, [macc.k, T[3].k], [macc.k])
                DMA(macc.d, mrg[fo, :, 0:ncol], macc.t[:, 0:ncol].bitcast(FR), [macc.k], [k_mrg[fo]])

        def resid_update(l, t, ncol, fo, res, src, gtoff):
            xin = T[fo % 2]; xo = T[2 + fo % 2]
            DMA(xin.d, xin.t[:, 0:ncol], src[:, fo, 0:ncol], [k_xres[t][fo]], [xin.k])
            for (ps_, c0, cn, tk) in res:
                if c0 < NP:
                    STT(xo.t[:, c0:c0 + cn], ps_[:, 0:cn], modT.t[:, gtoff + fo, 0:1], xin.t[:, c0:c0 + cn], ALU.mult, ALU.add, [tk, modT.k, xin.k], [xo.k])
                else:
                    TT(xo.t[:, c0:c0 + cn], ps_[:, 0:cn], modT.t[:, gtoff + fo, 1:17], ALU.mult, [tk, modT.k], [xo.k])
                    TT(xo.t[:, c0:c0 + cn], xo.t[:, c0:c0 + cn], xin.t[:, c0:c0 + cn], ALU.add, [xo.k, xin.k], [xo.k])
            DMA(xo.d, xres[t][:, fo, 0:ncol], xo.t[:, 0:ncol], [xo.k], [k_xres[t][fo]])

        def phase_wout(l, t, ncol, src):
            w = Lw[l]
            for k in range(16):
                DMA(P.dsem("hld"), hbuf.t[:, k, 0:ncol], mrg[k, :, 0:ncol], [k_mrg[k]], [k_h[k]])
            ws = WS([(w["out"][fo], 2048) for fo in range(16)])
            for fo in range(16):
                s_ = ws.get(fo)
                wv = wr.t[:, s_, :].rearrange("p (k n) -> p k n", k=16)
                res = proj_group(lambda k: (wv[:, k, :], [k_wr[s_]]), lambda k, c0, cn: (hbuf.t[:, k, c0:c0 + cn], [k_h[k]]), 16, ncol)
                resid_update(l, t, ncol, fo, res, src, 32)

        def phase_ffn(l, t, ncol):
            w = Lw[l]
            with ExitStack() as s:
                acc = sbuf(s, "facc", [128, 16, NP + NSMP]); hid = sbuf(s, "fhid", [128, 2, NP + NSMP], FR)
                fcw = sbuf(s, "fcwS", [128, 88, 3]); fcb = sbuf(s, "fcbS", [128, 88])
                fb = sbuf(s, "fbS", [128, NSMP, 2]); nb = sbuf(s, "nbS", [128, NSMP, 2])
                DMA(fcw.d, fcw.t[:], w["fcw"], [], [fcw.k]); DMA(fcb.d, fcb.t[:], w["fcb"], [], [fcb.k])
                if t == 0:
                    MS(fhalo.t[:], 0.0, [fhalo.k])
                blocks = []
                import os
                _ngb = int(os.environ.get("FFN_NG", "22")) if (l == 1 and t == 0) else 22
                for grp in range(_ngb):
                    for ff in range(2):
                        blocks += [(w["up"][2 * grp + ff], 2048), (w["up"][44 + 2 * grp + ff], 2048)]
                    blocks += [(w["down"][2 * grp + q], 2048) for q in range(2)]
                ws = WS(blocks)
                bi = 0

                def conv_evac(q, res, ext, yo):
                    CP(ext.t[:, 0:2], fhalo.t[:, q, :], [fhalo.k], [ext.k])
                    for (ps_, c0, cn, tk) in res:
                        ACT(ext.t[:, 2 + c0:2 + c0 + cn], ps_[:, 0:cn], AF.Copy, [tk], [ext.k])
                    TS(yo.t[:, 0:NP], ext.t[:, 2:2 + NP], fcw.t[:, q, 2:3], fcb.t[:, q:q + 1], ALU.mult, ALU.add, [ext.k, fcw.k, fcb.k], [yo.k])
                    STT(yo.t[:, 0:NP], ext.t[:, 1:1 + NP], fcw.t[:, q, 1:2], yo.t[:, 0:NP], ALU.mult, ALU.add, [ext.k, fcw.k, yo.k], [yo.k])
                    STT(yo.t[:, 0:NP], ext.t[:, 0:NP], fcw.t[:, q, 0:1], yo.t[:, 0:NP], ALU.mult, ALU.add, [ext.k, fcw.k, yo.k], [yo.k])
                    CP(fhalo.t[:, q, :], ext.t[:, NP:NP + 2], [ext.k], [fhalo.k])
                    if t == 1:
                        pass
                    if ncol > NP:
                        xs_ = ext.t[:, 2 + NP:2 + ncol]
                        DMA(fb.d, fb.t[:], fbuf0[l][:, q, :, :], [], [fb.k])
                        TS(yo.t[:, NP:ncol], xs_, fcw.t[:, q, 2:3], fcb.t[:, q:q + 1], ALU.mult, ALU.add, [ext.k, fcw.k, fcb.k], [yo.k])
                        STT(yo.t[:, NP:ncol], fb.t[:, :, 1], fcw.t[:, q, 1:2], yo.t[:, NP:ncol], ALU.mult, ALU.add, [fb.k, fcw.k, yo.k], [yo.k])
                        STT(yo.t[:, NP:ncol], fb.t[:, :, 0], fcw.t[:, q, 0:1], yo.t[:, NP:ncol], ALU.mult, ALU.add, [fb.k, fcw.k, yo.k], [yo.k])
                        CP(nb.t[:, :, 0], fb.t[:, :, 1], [fb.k], [nb.k])
                        CP(nb.t[:, :, 1], xs_, [ext.k], [nb.k])
                        DMA(nb.d, o_cfs[l][:, q, :, :], nb.t[:], [nb.k], [k_out])

                import os
                _ng = int(os.environ.get("FFN_NG", "22")) if (l == 1 and t == 0) else 22
                for grp in range(_ng):
                    if os.environ.get("FFN_DBG") and l == 1 and t == 0 and grp >= 17:
                        print("FFNDBG grp", grp, dict(P.cnt), {k_: v_[1] for k_, v_ in P.dsems.items() if v_[1] > 2000},
                              {e_: len(v_) + sum(len(w_[0]) for w_ in v_) for e_, v_ in P.streams.items()})
                    for ff in range(2):
                        f = 2 * grp + ff
                        ys = []
                        for part in range(2):
                            q = f + 44 * part
                            s_ = ws.get(bi); bi += 1
                            wv = wr.t[:, s_, :].rearrange("p (k n) -> p k n", k=16)
                            res = proj_group(lambda k: (wv[:, k, :], [k_wr[s_]]), lambda k, c0, cn: (hbuf.t[:, k, c0:c0 + cn], [k_h[k]]), 16, ncol)
                            ext = T[0 + part]; yo = T[2 + part]
                            conv_evac(q, res, ext, yo)
                            ys.append(yo)
                        ACT(T[4].t[:, 0:ncol], ys[0].t[:, 0:ncol], AF.Silu, [ys[0].k], [T[4].k])
                        TT(hid.t[:, ff, 0:ncol], T[4].t[:, 0:ncol], ys[1].t[:, 0:ncol], ALU.mult, [T[4].k, ys[1].k], [hid.k], eng="pool")
                    for q4 in range(2):
                        s_ = ws.get(bi); bi += 1
                        wv = wr.t[:, s_, :].rearrange("p (k n) -> p k n", k=2)
                        for fl in range(8):
                            fo = 8 * q4 + fl
                            res = proj_group(lambda k: (wv[:, k, 128 * fl:128 * fl + 128], [k_wr[s_]]), lambda k, c0, cn: (hid.t[:, k, c0:c0 + cn], [hid.k]), 2, ncol)
                            for (ps_, c0, cn, tk) in res:
                                if grp == 0:
                                    ACT(acc.t[:, fo, c0:c0 + cn], ps_[:, 0:cn], AF.Copy, [tk], [acc.k])
                                else:
                                    TT(acc.t[:, fo, c0:c0 + cn], acc.t[:, fo, c0:c0 + cn], ps_[:, 0:cn], ALU.add, [tk, acc.k], [acc.k])
                if t == 1:
                    DMA(fhalo.d, o_cfp[l], fhalo.t[:], [fhalo.k], [k_out])
                for fo in range(16):
                    xin = T[fo % 2]; xo = T[2 + fo % 2]
                    DMA(xin.d, xin.t[:, 0:ncol], xres[t][:, fo, 0:ncol], [k_xres[t][fo]], [xin.k])
                    STT(xo.t[:, 0:NP], acc.t[:, fo, 0:NP], modT.t[:, 80 + fo, 0:1], xin.t[:, 0:NP], ALU.mult, ALU.add, [acc.k, modT.k, xin.k], [xo.k])
                    if ncol > NP:
                        TT(xo.t[:, NP:ncol], acc.t[:, fo, NP:ncol], modT.t[:, 80 + fo, 1:17], ALU.mult, [acc.k, modT.k], [xo.k])
                        TT(xo.t[:, NP:ncol], xo.t[:, NP:ncol], xin.t[:, NP:ncol], ALU.add, [xo.k, xin.k], [xo.k])
                    DMA(xo.d, xres[t][:, fo, 0:ncol], xo.t[:, 0:ncol], [xo.k], [k_xres[t][fo]])

        class _Stop(Exception):
            pass
        nph = [0]

        def _wrap(fn):
            def g(*a, **k):
                if stop_after is not None and nph[0] >= stop_after:
                    return None
                nph[0] += 1
                return fn(*a, **k)
            return g
        phase_mod, phase_norm, phase_inproj, phase_s5, phase_gmlp, phase_ssd, phase_merge, phase_wout, phase_ffn = [
            _wrap(f_) for f_ in (phase_mod, phase_norm, phase_inproj, phase_s5, phase_gmlp, phase_ssd, phase_merge, phase_wout, phase_ffn)]
        def _program():
            for l in range(DEPTH):
                cur_l[0] = l
                phase_mod(l)
                for t in range(2):
                    ncol = NP + NSMP if t == 0 else NP
                    src = xT[t] if l == 0 else xres[t]
                    phase_norm(src, k_xres[t], ncol, G1, 0)
                    phase_inproj(l, t, ncol)
                    P.barrier()
                    with ExitStack() as so:
                        oa = sbuf(so, "oa", [128, 4, NP + NSMP], FR)
                        phase_s5(l, t, ncol, oa)
                        P.barrier()
                        ob = sbuf(so, "ob", [128, 4, NP + NSMP], FR)
                        phase_gmlp(l, t, ncol, ob)
                        P.barrier()
                        oc = sbuf(so, "oc", [128, 8, NP + NSMP], FR)
                        phase_ssd(l, t, ncol, oc)
                        P.barrier()
                        phase_merge(l, t, ncol, oa, ob, oc)
                        P.barrier()
                    phase_wout(l, t, ncol, src)
                    phase_norm(xres[t], k_xres[t], ncol, G2, 48)
                    phase_ffn(l, t, ncol)
                    P.barrier()
                    if l == DEPTH - 1:
                        phase_norm(xres[t], k_xres[t], ncol, dst=o_y[t])

        try:
            _program()
        except _Stop:
            pass
        P.barrier()
        with nc.allow_non_contiguous_dma(reason="single-column gathers for padded sample chunks"), nc.Block() as block:
            P.emit(block)
        return nc

_NC = None


def _tile_w(W, kc, nw):
    K, N = W.shape
    assert K == kc * 128 and N % nw == 0
    return np.ascontiguousarray(W.reshape(kc, 128, N // nw, nw).transpose(2, 1, 0, 3).reshape(N // nw, 128, kc * nw))


def _fm(v):
    n = v.shape[0] // 128
    return np.ascontiguousarray(v.reshape((n, 128) + v.shape[1:]).swapaxes(0, 1))


def _prep_shared(inp):
    f = np.float32
    sh = {}
    sh["g_final"] = _fm(inp["g_final"])
    sh["ident"] = np.eye(128, dtype=f)
    sh["ones"] = np.ones((128, 128), f)
    jj, ii = np.meshgrid(np.arange(128), np.arange(128), indexing="ij")
    sh["negmaskT"] = np.where(ii >= jj, 0.0, -30000.0).astype(f)
    sh["trilT"] = (ii >= jj).astype(f)
    sel = np.zeros((16, 16, 128), f)
    for h in range(16):
        sel[h, h, :] = 1.0
    sh["sel16"] = sel
    sx = np.zeros((16, 8, 128), f)
    for m in range(8):
        sx[2 * m, m, 0:64] = 1.0
        sx[2 * m + 1, m, 64:128] = 1.0
    sh["selexp"] = sx
    sh["iota"] = np.tile(np.arange(128, dtype=f)[None, :], (128, 1))
    for l in range(DEPTH):
        sh[f"w_mod{l}"] = _tile_w(inp["w_mod"][l], 16, 128)
        sh[f"b_mod{l}"] = _fm(inp["b_mod"][l])
        sh[f"g_mix{l}"] = _fm(inp["g_mix"][l]); sh[f"g_ffn{l}"] = _fm(inp["g_ffn"][l])
        win = inp["w_in"][l]
        sh[f"w_inb{l}"] = _tile_w(win[:, 0:4096], 16, 128)
        sh[f"w_indt{l}"] = np.ascontiguousarray(win[:, 4096:4112].reshape(16, 128, 16).transpose(1, 0, 2).reshape(128, 256))
        sh[f"w_ing{l}"] = _tile_w(win[:, 4112:], 16, 128)
        sh[f"w_pa{l}"] = _tile_w(inp["w_pa"][l], 4, 128)
        sh[f"w_pb{l}"] = _tile_w(inp["w_pb"][l], 4, 128)
        sh[f"w_pc{l}"] = _tile_w(inp["w_pc"][l], 8, 128)
        sh[f"w_out{l}"] = _tile_w(inp["w_out"][l], 16, 128)
        sh[f"w_up{l}"] = _tile_w(inp["ffn_w_up"][l], 16, 128)
        wd = inp["ffn_w_down"][l]
        sh[f"w_down{l}"] = np.ascontiguousarray(
            wd.reshape(22, 2, 128, 2, 1024).transpose(0, 3, 2, 1, 4).reshape(44, 128, 2048))
        sh[f"w_glu{l}"] = np.ascontiguousarray(inp["s5_w_glu"][l].reshape(4, 128, 512).transpose(1, 0, 2).reshape(128, 2048))
        sh[f"lamr{l}"] = _fm(inp["s5_lam_re"][l].reshape(2048)); sh[f"lami{l}"] = _fm(inp["s5_lam_im"][l].reshape(2048))
        sh[f"logdt{l}"] = _fm(np.repeat(inp["s5_log_dt"][l], 64))
        Bre = np.zeros((128, 16, 128), f); Bim = np.zeros((128, 16, 128), f)
        Cre = np.zeros((128, 16, 128), f); Cim = np.zeros((128, 16, 128), f)
        for m in range(16):
            for gg in range(2):
                g = 2 * m + gg
                r0 = (g % 8) * 16
                Bre[r0:r0 + 16, m, gg * 64:(gg + 1) * 64] = inp["s5_b_re"][l, g].T
                Bim[r0:r0 + 16, m, gg * 64:(gg + 1) * 64] = inp["s5_b_im"][l, g].T
                Cre[gg * 64:(gg + 1) * 64, m, r0:r0 + 16] = inp["s5_c_re"][l, g].T
                Cim[gg * 64:(gg + 1) * 64, m, r0:r0 + 16] = inp["s5_c_im"][l, g].T
        sh[f"Bre{l}"] = Bre; sh[f"Bim{l}"] = Bim; sh[f"Cre{l}"] = Cre; sh[f"Cim{l}"] = Cim
        sh[f"s5d{l}"] = _fm(inp["s5_d"][l])
        sh[f"gv{l}"] = _fm(inp["gm_g_v"][l])
        sh[f"w00{l}"] = _fm(np.repeat(inp["gm_w_s"][l][:, 0, 0], 64))
        sh[f"wsT{l}"] = np.ascontiguousarray(inp["gm_w_s"][l].transpose(2, 0, 1))
        sh[f"bsx{l}"] = _fm(np.repeat(inp["gm_b_s"][l], 64, axis=0))
        sh[f"cw{l}"] = _fm(np.ascontiguousarray(inp["ssd_conv_w"][l].T))
        sh[f"cb{l}"] = _fm(inp["ssd_conv_b"][l])
        sh[f"dtb{l}"] = inp["ssd_dt_bias"][l].reshape(16, 1).copy(); sh[f"alog{l}"] = inp["ssd_a_log"][l].reshape(16, 1).copy()
        sh[f"ssdD{l}"] = _fm(np.repeat(inp["ssd_d"][l], 64)); sh[f"ssdg{l}"] = _fm(inp["ssd_g_norm"][l])
        sh[f"fcw{l}"] = _fm(np.ascontiguousarray(inp["ffn_conv_w"][l].T)); sh[f"fcb{l}"] = _fm(inp["ffn_conv_b"][l])
    return {k: np.ascontiguousarray(v, dtype=f) for k, v in sh.items()}


def _core_inputs(inp, sh, c):
    f = np.float32
    sq = c % 4
    rows = slice(16 * c, 16 * c + 16)
    m = dict(sh)
    xp = inp["x_prompt"][sq]
    xs = inp["x_sample"][rows, 0, :]
    x0 = np.concatenate([xp[0:NP], xs], axis=0)
    m["xT0"] = _fm(np.ascontiguousarray(x0.T)).astype(f)
    m["xT1"] = _fm(np.ascontiguousarray(xp[NP:2 * NP].T)).astype(f)
    cc = np.zeros((18, D), f)
    cc[0] = inp["c_prompt"][sq]; cc[1:17] = inp["c_sample"][rows]
    m["cT"] = _fm(np.ascontiguousarray(cc.T))
    m["s5r0"] = np.ascontiguousarray(inp["state_s5_re"][:, rows].reshape(DEPTH, 16, 16, 128).transpose(0, 3, 2, 1))
    m["s5i0"] = np.ascontiguousarray(inp["state_s5_im"][:, rows].reshape(DEPTH, 16, 16, 128).transpose(0, 3, 2, 1))
    m["ssm0"] = np.ascontiguousarray(inp["state_ssm"][:, rows].reshape(DEPTH, 16, 1024, 128).transpose(0, 1, 3, 2))
    m["cbuf0"] = np.ascontiguousarray(inp["state_ssd_conv"][:, rows].reshape(DEPTH, 16, 3, 12, 128).transpose(0, 4, 3, 1, 2))
    m["fbuf0"] = np.ascontiguousarray(inp["state_ffn_conv"][:, rows].reshape(DEPTH, 16, 2, 88, 128).transpose(0, 4, 3, 1, 2))
    return m


def _unfm(a):
    return a.swapaxes(0, 1).reshape((a.shape[0] * a.shape[1],) + a.shape[2:])


def _unpack_core(r):
    o = {}
    y0 = _unfm(r["o_y0"]).T
    o["y_s"] = y0[NP:]
    o["y_p"] = np.concatenate([y0[0:NP], _unfm(r["o_y1"]).T], axis=0)
    s5s = r["o_s5s"]
    o["s5r_s"] = s5s[:, 0].transpose(0, 3, 2, 1).reshape(DEPTH, 16, 32, 64)
    o["s5i_s"] = s5s[:, 1].transpose(0, 3, 2, 1).reshape(DEPTH, 16, 32, 64)
    o["ssm_s"] = r["o_ssms"].transpose(0, 1, 3, 2).reshape(DEPTH, 16, 16, 64, 128)
    o["cc_s"] = r["o_ccs"].transpose(0, 3, 4, 2, 1).reshape(DEPTH, 16, 3, 1536)
    o["cf_s"] = r["o_cfs"].transpose(0, 3, 4, 2, 1).reshape(DEPTH, 16, 2, 11264)
    o["gv_s"] = r["o_gv"].transpose(0, 3, 2, 1).reshape(DEPTH, 16, 512)
    s5p = r["o_s5p"]
    o["s5r_p"] = s5p[:, 0].transpose(0, 2, 1).reshape(DEPTH, 32, 64)
    o["s5i_p"] = s5p[:, 1].transpose(0, 2, 1).reshape(DEPTH, 32, 64)
    o["ssm_p"] = r["o_ssmp"].transpose(0, 2, 1).reshape(DEPTH, 16, 64, 128)
    o["cc_p"] = r["o_ccp"].transpose(0, 3, 2, 1).reshape(DEPTH, 3, 1536)
    o["cf_p"] = r["o_cfp"].transpose(0, 3, 2, 1).reshape(DEPTH, 2, 11264)
    return o


def kernel(**inp):
    global _NC
    inp = {k: np.asarray(v) for k, v in inp.items()}
    f = np.float32
    if _NC is None:
        _NC = build_program()
    nc = _NC
    sh = _prep_shared(inp)
    in_maps = [_core_inputs(inp, sh, c) for c in range(8)]
    res = run_bass_kernel_spmd(nc, in_maps, core_ids=list(range(8)))
    R = res.results
    y_p = np.zeros((4, 2048, D), f); y_s = np.zeros((128, 1, D), f)
    s5r_p = np.zeros((DEPTH, 4, 32, 64), f); s5i_p = np.zeros_like(s5r_p)
    ssm_p = np.zeros((DEPTH, 4, 16, 64, 128), f)
    cc_p = np.zeros((DEPTH, 4, 3, 1536), f); cf_p = np.zeros((DEPTH, 4, 2, 11264), f)
    s5r_s = np.zeros((DEPTH, 128, 32, 64), f); s5i_s = np.zeros_like(s5r_s)
    ssm_s = np.zeros((DEPTH, 128, 16, 64, 128), f)
    cc_s = np.zeros((DEPTH, 128, 3, 1536), f); cf_s = np.zeros((DEPTH, 128, 2, 11264), f)
    gv_s = np.zeros((DEPTH, 128, 1, 512), f)
    for c in range(8):
        o = _unpack_core(R[c])
        rows = slice(16 * c, 16 * c + 16)
        y_s[rows, 0, :] = o["y_s"]
        s5r_s[:, rows] = o["s5r_s"]; s5i_s[:, rows] = o["s5i_s"]; ssm_s[:, rows] = o["ssm_s"]
        cc_s[:, rows] = o["cc_s"]; cf_s[:, rows] = o["cf_s"]; gv_s[:, rows, 0, :] = o["gv_s"]
        if c < 4:
            y_p[c] = o["y_p"]
            s5r_p[:, c] = o["s5r_p"]; s5i_p[:, c] = o["s5i_p"]; ssm_p[:, c] = o["ssm_p"]
            cc_p[:, c] = o["cc_p"]; cf_p[:, c] = o["cf_p"]
    return (y_p, y_s, s5r_p, s5i_p, ssm_p, cc_p, cf_p, s5r_s, s5i_s, ssm_s, cc_s, cf_s, gv_s)
```
